# Optimizing a Trainium2 kernel written in Bass

```python
import jax, jax.numpy as jnp
from jax import lax
import numpy as np

D_MODEL = 2048
BATCH = 1
SEQ = 8192
DEPTH = 4

MEM_LEN = 256
CONV_DIM = D_MODEL // 2
CONV_KERNEL = 31
SCONV_DIM = D_MODEL // 2
SCONV_KERNEL = 3
XATTN_DIM = D_MODEL // 2
XATTN_HEADS = 4
XATTN_HEAD_DIM = XATTN_DIM // XATTN_HEADS
N_BRANCH = 3
D_FF = 4 * D_MODEL
EPS = 1e-6

IN_SIZES = (CONV_DIM, CONV_DIM, SCONV_DIM, SCONV_DIM, SCONV_DIM, XATTN_DIM, N_BRANCH * D_MODEL)
IN_DIM = int(sum(IN_SIZES))
IN_SPLITS = tuple(int(v) for v in np.cumsum(IN_SIZES)[:-1])

kernel_name = "hybrid_conformer_shortconv_memxattn_block"


def rms_norm(x, g):
    xf = x.astype(jnp.float32)
    y = xf * lax.rsqrt(jnp.mean(xf * xf, axis=-1, keepdims=True) + EPS)
    return (y * g.astype(jnp.float32)).astype(x.dtype)


def layer_norm(x, g, b):
    xf = x.astype(jnp.float32)
    mu = jnp.mean(xf, axis=-1, keepdims=True)
    var = jnp.mean(jnp.square(xf - mu), axis=-1, keepdims=True)
    y = (xf - mu) * lax.rsqrt(var + EPS)
    return (y * g.astype(jnp.float32) + b.astype(jnp.float32)).astype(x.dtype)


def causal_depthwise_conv(u, w):
    k, c = w.shape
    return lax.conv_general_dilated(
        u, w[:, None, :].astype(u.dtype),
        window_strides=(1,), padding=[(k - 1, 0)],
        dimension_numbers=("NWC", "WIO", "NWC"),
        feature_group_count=c)


def memory_cross_attention(q, mem_n, w_kv):
    b, s, _ = q.shape
    m = mem_n.shape[1]
    kv = mem_n @ w_kv
    k, v = jnp.split(kv, 2, axis=-1)
    qh = q.reshape(b, s, XATTN_HEADS, XATTN_HEAD_DIM)
    kh = k.reshape(b, m, XATTN_HEADS, XATTN_HEAD_DIM)
    vh = v.reshape(b, m, XATTN_HEADS, XATTN_HEAD_DIM)
    scale = XATTN_HEAD_DIM ** -0.5
    scores = jnp.einsum("bshd,bmhd->bhsm", qh, kh).astype(jnp.float32) * scale
    probs = jax.nn.softmax(scores, axis=-1).astype(v.dtype)
    o = jnp.einsum("bhsm,bmhd->bshd", probs, vh)
    return o.reshape(b, s, XATTN_DIM)


def hybrid_layer(x, mem, g_mix_pre, w_in, conv_a_w, conv_a_b, ln_a_g, ln_a_b, w_a_out,
                 conv_b_w, w_b_out, g_mem, w_kv, w_x_out, w_o, g_mix_post,
                 g_mlp_pre, w_up, w_down, g_mlp_post):
    b, s, d = x.shape
    h = rms_norm(x, g_mix_pre)
    proj = h @ w_in
    a_val, a_gate, sb, sc, sx, q, gates = jnp.split(proj, IN_SPLITS, axis=-1)

    a = a_val * jax.nn.sigmoid(a_gate)
    a = causal_depthwise_conv(a, conv_a_w) + conv_a_b
    a = jax.nn.silu(layer_norm(a, ln_a_g, ln_a_b))
    y_a = a @ w_a_out

    u = causal_depthwise_conv(sc * sx, conv_b_w)
    y_b = (sb * u) @ w_b_out

    mem_n = rms_norm(mem, g_mem)
    y_x = memory_cross_attention(q, mem_n, w_kv) @ w_x_out

    g = jax.nn.sigmoid(gates).reshape(b, s, N_BRANCH, d)
    merged = g[:, :, 0] * y_a + g[:, :, 1] * y_b + g[:, :, 2] * y_x
    x = x + rms_norm(merged @ w_o, g_mix_post)

    h = rms_norm(x, g_mlp_pre)
    f = jnp.square(jax.nn.relu(h @ w_up)) @ w_down
    x = x + rms_norm(f, g_mlp_post)
    return x


def setup_inputs(seed: int = 0) -> dict:
    key = jax.random.key(seed)
    ks = jax.random.split(key, 24)
    f32 = jnp.float32

    def nrm(k, shape, scale):
        return jax.random.normal(k, shape, f32) * scale

    def gain(k, shape):
        return 1.0 + 0.02 * jax.random.normal(k, shape, f32)

    L, D = DEPTH, D_MODEL
    return {
        "x": nrm(ks[0], (BATCH, SEQ, D), 1.0),
        "mem": nrm(ks[1], (BATCH, MEM_LEN, D), 1.0),
        "g_mix_pre": gain(ks[2], (L, D)),
        "w_in": nrm(ks[3], (L, D, IN_DIM), D ** -0.5),
        "conv_a_w": nrm(ks[4], (L, CONV_KERNEL, CONV_DIM), CONV_KERNEL ** -0.5),
        "conv_a_b": nrm(ks[5], (L, CONV_DIM), 0.02),
        "ln_a_g": gain(ks[6], (L, CONV_DIM)),
        "ln_a_b": nrm(ks[7], (L, CONV_DIM), 0.02),
        "w_a_out": nrm(ks[8], (L, CONV_DIM, D), CONV_DIM ** -0.5),
        "conv_b_w": nrm(ks[9], (L, SCONV_KERNEL, SCONV_DIM), SCONV_KERNEL ** -0.5),
        "w_b_out": nrm(ks[10], (L, SCONV_DIM, D), SCONV_DIM ** -0.5),
        "g_mem": gain(ks[11], (L, D)),
        "w_kv": nrm(ks[12], (L, D, 2 * XATTN_DIM), D ** -0.5),
        "w_x_out": nrm(ks[13], (L, XATTN_DIM, D), XATTN_DIM ** -0.5),
        "w_o": nrm(ks[14], (L, D, D), D ** -0.5),
        "g_mix_post": gain(ks[15], (L, D)),
        "g_mlp_pre": gain(ks[16], (L, D)),
        "w_up": nrm(ks[17], (L, D, D_FF), D ** -0.5),
        "w_down": nrm(ks[18], (L, D_FF, D), D_FF ** -0.5),
        "g_mlp_post": gain(ks[19], (L, D)),
    }


def reference(x, mem, g_mix_pre, w_in, conv_a_w, conv_a_b, ln_a_g, ln_a_b, w_a_out,
              conv_b_w, w_b_out, g_mem, w_kv, w_x_out, w_o, g_mix_post,
              g_mlp_pre, w_up, w_down, g_mlp_post):
    for l in range(DEPTH):
        x = hybrid_layer(x, mem, g_mix_pre[l], w_in[l], conv_a_w[l], conv_a_b[l],
                         ln_a_g[l], ln_a_b[l], w_a_out[l], conv_b_w[l], w_b_out[l],
                         g_mem[l], w_kv[l], w_x_out[l], w_o[l], g_mix_post[l],
                         g_mlp_pre[l], w_up[l], w_down[l], g_mlp_post[l])
    return x
```

```python
import numpy as np
from contextlib import ExitStack
import concourse.bass as bass
import concourse.mybir as mybir
from concourse.bass_utils import run_bass_kernel_spmd

F32 = mybir.dt.float32
BF16 = mybir.dt.bfloat16
AF = mybir.ActivationFunctionType
ALU = mybir.AluOpType

D = 2048
SEQ = 8192
DEPTH = 4
NCORE = 8
TOK = SEQ // NCORE
HALO = 128
NT = 576
NH = 288
NBLK = 2
MEM = 256
CK = 31
EPS = 1e-6
TPL = 140
TILE = 4096
NSLOT = 4
NTMP = 8
POOL_CHUNKS = 0
LN_AT = 6
TRIM = True
BG_RATE = 1
BALANCE = False
FAST_RECIP = False
LNEXP = True
KV_SPLIT = 3
Q_DRAIN = 4
G_DRAIN = 2

C_GPRE, C_GPOST, C_GMPRE, C_GMPOST, C_GMEM = 0, 16, 32, 48, 64
C_CAW = 80
C_CAB = C_CAW + 8 * CK
C_LNG = C_CAB + 8
C_LNB = C_LNG + 8
C_CBW = C_LNB + 8
C_PER = C_CBW + 24

ENGS = ("pe", "act", "dve", "pool", "sp")


class Tk:
    __slots__ = ("w", "r")

    def __init__(self):
        self.w = {}
        self.r = {}


class _Rec:
    def __init__(self):
        self.call = None

    def __getattr__(self, name):
        def f(*a, **k):
            assert self.call is None
            self.call = (name, a, k)
        return f


def _eager(fn):
    r = _Rec()
    fn(r)
    name, a, k = r.call
    return lambda e: getattr(e, name)(*a, **k)


class Emitter:
    def __init__(self, nc, strict_same=False):
        self.nc = nc
        self.streams = {e: [] for e in ENGS}
        self.cnt = {e: 0 for e in ENGS}
        self.waited = {e: {} for e in ENGS}
        self.dma_cnt = {}
        self.strict_same = strict_same
        self.sems = {}
        self.bg = {"dve": [], "pool": []}
        self.bg_rate = {"dve": 0, "pool": 0}
        self.in_bg = False

    def _deps(self, rd, wr, extra):
        deps = {}
        for t in rd:
            for k, v in t.w.items():
                if deps.get(k, 0) < v:
                    deps[k] = v
        for t in wr:
            for d in (t.w, t.r):
                for k, v in d.items():
                    if deps.get(k, 0) < v:
                        deps[k] = v
        for d in extra:
            if d is None:
                continue
            for k, v in d.items():
                if deps.get(k, 0) < v:
                    deps[k] = v
        return deps

    def _waits(self, eng, deps, strict):
        waits = []
        for k, v in deps.items():
            if k == eng and not strict:
                continue
            if self.waited[eng].get(k, 0) >= v:
                continue
            self.waited[eng][k] = v
            waits.append((k, v))
        return waits

    def op(self, eng, fn, rd=(), wr=(), sig=True, extra=(), strict=None):
        strict = self.strict_same if strict is None else strict
        deps = self._deps(rd, wr, extra)
        waits = self._waits(eng, deps, strict)
        if sig:
            self.cnt[eng] += 1
            c = self.cnt[eng]
        else:
            c = self.cnt[eng] + 1
        self.streams[eng].append((waits, _eager(fn), 1 if sig else 0, eng))
        for t in rd:
            if t.r.get(eng, 0) < c:
                t.r[eng] = c
        for t in wr:
            t.w = {eng: c}
            t.r = {}
        ev = {eng: c}
        if eng == "dve" and self.bg["dve"] and not self.in_bg:
            self.drain_bg("dve", self.bg_rate["dve"])
        return ev

    def drain_bg(self, eng, n=None):
        self.in_bg = True
        k = 0
        q = self.bg[eng]
        while q and (n is None or k < n):
            q.pop(0)()
            k += 1
        self.in_bg = False

    def dma(self, q, slot, out, in_, rd=(), wr=(), extra=()):
        key = "dma:" + slot
        deps = self._deps(rd, wr, extra)
        waits = self._waits(q, deps, False)
        self.dma_cnt[key] = self.dma_cnt.get(key, 0) + 16
        c = self.dma_cnt[key]
        self.streams[q].append((waits, lambda e, o=out, i=in_: e.dma_start(out=o, in_=i), 16, key))
        for t in rd:
            if t.r.get(key, 0) < c:
                t.r[key] = c
        for t in wr:
            t.w = {key: c}
            t.r = {}
        if q == "pool" and self.bg["pool"] and not self.in_bg:
            self.drain_bg("pool", self.bg_rate["pool"])
        return {key: c}

    def finalize(self, st, final_waits):
        nc = self.nc
        keys = list(ENGS) + sorted(self.dma_cnt.keys())
        for k in keys:
            self.sems[k] = st.enter_context(nc.semaphore("s_" + k.replace(":", "_")))
        block = st.enter_context(nc.Block())
        fin = {}
        for d in final_waits:
            for k, v in d.items():
                fin[k] = max(fin.get(k, 0), v)

        def replay(eng_name):
            def run(e):
                for waits, fn, inc, semkey in self.streams[eng_name]:
                    for k, v in waits:
                        e.wait_ge(self.sems[k], v)
                    ins = fn(e)
                    if inc:
                        ins.then_inc(self.sems[semkey], inc)
                if eng_name == "sp":
                    for k, v in fin.items():
                        e.wait_ge(self.sems[k], v)
            return run

        block.tensor(replay("pe"))
        block.scalar(replay("act"))
        block.vector(replay("dve"))
        block.gpsimd(replay("pool"))
        block.sync(replay("sp"))


def build_program(n_layers, nblk=NBLK, bg_conv=True):
    nc = bass.Bass("TRN2", target_bir_lowering=False)
    xT = nc.dram_tensor("xT", [128, 16, nblk * NT], F32, kind="ExternalInput").ap()
    wts = nc.dram_tensor("wts", [n_layers * TPL, 128, TILE], F32, kind="ExternalInput").ap()
    cstd = nc.dram_tensor("cst", [128, n_layers * C_PER], F32, kind="ExternalInput").ap()
    memTd = nc.dram_tensor("memT", [128, 16, MEM], F32, kind="ExternalInput").ap()
    hmaskd = nc.dram_tensor("hmask", [128, 1], F32, kind="ExternalInput").ap()
    outT = nc.dram_tensor("outT", [128, 16, nblk * NT - HALO], F32, kind="ExternalOutput").ap()

    st = ExitStack()
    with st:
        def sb(name, shape, dt):
            return st.enter_context(nc.sbuf_tensor(name, shape, dt))

        cst = sb("cst_sb", [128, n_layers * C_PER], F32)
        hmask = sb("hmask_sb", [128, 1], F32)
        xs = sb("xs", [128, 16, NT], F32)
        ZA = sb("ZA", [128, 16 * NT], F32)
        HA = sb("HA", [128, 64 * NT], BF16)
        ring = sb("ring", [128, NSLOT, TILE], BF16)
        ones = sb("ones", [128, 128], BF16)
        sq = sb("sq", [128, 8, NH], BF16)
        st_a = sb("st_a", [128, 2, NH], F32)
        st_b = sb("st_b", [128, 2, NH], F32)
        st_c = sb("st_c", [128, 2, NH], F32)
        tmpf = sb("tmpf", [128, NTMP, NH], F32)
        ptmp = sb("ptmp", [128, NH], F32) if POOL_CHUNKS > 0 else None
        ahist = sb("ahist", [128, n_layers, 8, 30], F32)
        uhist = sb("uhist", [128, n_layers, 8, 2], F32)
        ps = st.enter_context(nc.psum_tensor("ps", [128, 8, 512], F32))

        ZAb = ZA[:].bitcast(BF16)
        hb = ZAb[:, 0:16 * NT].rearrange("p (c n) -> p c n", c=16)
        merged = ZAb[:, 16 * NT:32 * NT].rearrange("p (c n) -> p c n", c=16)
        zf = ZA[:].rearrange("p (c n) -> p c n", c=16)
        HAf = HA[:].bitcast(F32)
        o_s1 = 0
        n_s1 = 8 * (NT + 32)
        o_s2 = n_s1
        n_s2 = 8 * NT
        s1 = HAf[:, o_s1:o_s1 + n_s1].rearrange("p (c n) -> p c n", c=8)
        s2 = HAf[:, o_s2:o_s2 + n_s2].rearrange("p (c n) -> p c n", c=8)
        zm = HAf[:, 0:16 * NT].rearrange("p (c n) -> p c n", c=16)
        ob16 = 2 * (n_s1 + n_s2)
        aact = HA[:, ob16:ob16 + 8 * NT].rearrange("p (c n) -> p c n", c=8)
        bact = HA[:, ob16 + 8 * NT:ob16 + 16 * NT].rearrange("p (c n) -> p c n", c=8)
        qo = HA[:, ob16 + 16 * NT:ob16 + 24 * NT].rearrange("p (c n) -> p c n", c=8)
        okv = ob16 + 24 * NT
        kT = HA[:, okv:okv + 8 * MEM].rearrange("p (c n) -> p c n", c=8)
        vv = HA[:, okv + 8 * MEM:okv + 16 * MEM].rearrange("p (c n) -> p c n", c=2)
        assert okv + 16 * MEM <= 64 * NT
        hid = HA[:].rearrange("p (c n) -> p c n", c=64)
        memf = HAf[:, 0:16 * MEM].rearrange("p (c n) -> p c n", c=16)
        memn = HA[:, 2 * o_s2:2 * o_s2 + 16 * MEM].rearrange("p (c n) -> p c n", c=16)

        E = Emitter(nc)
        t_cst = Tk(); txs = [[Tk(), Tk()] for _ in range(16)]; t_xs_all = [t for p in txs for t in p]; thb = [[Tk(), Tk()] for _ in range(16)]; t_hb_all = [t for p in thb for t in p]; t_mrg = Tk(); t_z = Tk()
        t_s1 = Tk(); t_s2 = Tk(); t_aact = Tk(); t_bact = Tk(); t_qo = [Tk() for _ in range(4)]
        t_kv = Tk(); t_hid = Tk(); t_mem = Tk(); t_memn = Tk()
        t_slot = [Tk() for _ in range(NSLOT)]
        t_bank = [Tk() for _ in range(8)]
        t_sq = [Tk() for _ in range(8)]
        t_sta = [Tk(), Tk()]; t_stb = [Tk(), Tk()]; t_stc = [Tk(), Tk()]
        t_tmp = [Tk() for _ in range(NTMP)]
        t_s2p = Tk(); t_ptmp = Tk()
        t_hist = Tk(); t_ones = Tk()
        rr = {"bank": 0, "sq": 0, "tmp": 0, "pt": 0, "tile": 0, "done": 0}

        cur = {"lo": 0}
        LO = [8, 38, 68, 98]

        def MID():
            lo = cur["lo"]
            if lo == 0 or not BALANCE:
                return NH
            return lo + ((NT - lo) // 4) * 2 + ((NT - lo) % 4 > 0) * 2 if False else max(NH, ((lo + NT) // 4) * 2)

        def H(h):
            return slice(cur["lo"], MID()) if h == 0 else slice(MID(), NT)

        def P(h):
            sl = H(h)
            return slice(0, sl.stop - sl.start)

        def cs(l, col):
            c0 = l * C_PER + col
            return cst[:, c0:c0 + 1]

        def nbank():
            b = rr["bank"]
            rr["bank"] = (b + 1) % 6
            return b

        def nsq():
            i = rr["sq"]; rr["sq"] = (i + 1) % 8
            return i

        def ntmp():
            i = rr["tmp"]; rr["tmp"] = (i + 1) % NTMP
            return i


        E.dma("sp", "cst", cst[:], cstd[:], wr=[t_cst])
        E.dma("sp", "hmask", hmask[:], hmaskd[:], wr=[t_cst])
        E.op("dve", lambda e: e.memset(ones[:], 1.0), wr=[t_ones])
        E.op("dve", lambda e: e.memset(HAf[:, 0:n_s1 + n_s2], 0.0), wr=[t_s1, t_s2])
        E.op("dve", lambda e: e.memset(ahist[:], 0.0), wr=[t_hist])
        E.op("dve", lambda e: e.memset(uhist[:], 0.0), wr=[t_hist])

        tile_state = {"next_load": 0, "order": []}

        def prefetch(upto):
            while tile_state["next_load"] < min(upto, len(tile_state["order"])):
                i = tile_state["next_load"]
                s = i % NSLOT
                E.dma("pool", "w%d" % s, ring[:, s, :], wts[tile_state["order"][i]], wr=[t_slot[s]])
                tile_state["next_load"] += 1

        def next_tile():
            i = rr["tile"]
            rr["tile"] += 1
            assert i - rr["done"] < NSLOT
            prefetch(i + 1)
            return i % NSLOT

        def tiles(n):
            for _ in range(n):
                s_ = next_tile()
                yield s_
                rel(1)

        def rel(n=1):
            rr["done"] += n
            prefetch(rr["done"] + NSLOT)

        def group(mms, h=None, nout=None, bank=None):
            b = nbank() if bank is None else bank
            n = len(mms)
            psl = P(h) if h is not None else slice(0, nout)
            for i, (l_ap, r_ap, rdt) in enumerate(mms):
                E.op("pe", lambda e, l_ap=l_ap, r_ap=r_ap, i=i, b=b: e.matmul(
                    ps[:, b, psl], lhsT=l_ap, rhs=r_ap, start=(i == 0), stop=(i == n - 1)),
                    rd=rdt, wr=[t_bank[b]] if i == 0 else [], sig=(i == n - 1))
            t_bank[b].w = {"pe": E.cnt["pe"]}
            return b

        def colsum(srcs, bank, h=None, nout=None):
            return group([(ones[:], s_ap, [t_ones] + tk) for s_ap, tk in srcs], h=h, nout=nout, bank=bank)

        def rstd_from_bank(bank, h, dst, t_dst, inv_n, nout=None):
            sl = P(h) if nout is None else slice(0, nout)
            if LNEXP:
                E.op("act", lambda e: e.activation(out=dst[:, h, sl], in_=ps[:, bank, sl], func=AF.Ln,
                                                   bias=cst_eps[:], scale=inv_n),
                     rd=[t_bank[bank], t_cst], wr=[t_dst])
                E.op("act", lambda e: e.activation(out=dst[:, h, sl], in_=dst[:, h, sl], func=AF.Exp, scale=-0.5),
                     rd=[t_dst], wr=[t_dst])
                return
            E.op("act", lambda e: e.activation(out=dst[:, h, sl], in_=ps[:, bank, sl], func=AF.Sqrt,
                                               bias=cst_eps[:], scale=inv_n),
                 rd=[t_bank[bank], t_cst], wr=[t_dst])
            E.op("dve", lambda e: e.reciprocal(out=dst[:, h, sl], in_=dst[:, h, sl]), rd=[t_dst], wr=[t_dst])

        st_mem = sb("st_mem", [128, 1, MEM], F32)
        t_stmem = Tk()
        cst_eps = sb("cst_eps", [128, 1], F32)
        E.op("dve", lambda e: e.memset(cst_eps[:], EPS), wr=[t_cst])

        def rms_pre(l, gcol):
            for h in range(2):
                b = 6 + h
                for c in range(16):
                    i = nsq()
                    E.op("act", lambda e, c=c, i=i, h=h: e.activation(out=sq[:, i, P(h)], in_=xs[:, c, H(h)], func=AF.Square),
                         rd=[txs[c][h]], wr=[t_sq[i]])
                    E.op("pe", lambda e, i=i, c=c, b=b: e.matmul(
                        ps[:, b, P(h)], lhsT=ones[:], rhs=sq[:, i, P(h)], start=(c == 0), stop=(c == 15)),
                        rd=[t_ones, t_sq[i]], wr=[t_bank[b]] if c == 0 else [], sig=True)
                t_bank[b].w = {"pe": E.cnt["pe"]}
                rstd_from_bank(b, h, st_a, t_sta[h], 1.0 / D)
                for c in range(16):
                    E.op("dve", lambda e, c=c, h=h: e.scalar_tensor_tensor(
                        out=hb[:, c, H(h)], in0=xs[:, c, H(h)], scalar=cs(l, gcol + c), in1=st_a[:, h, P(h)],
                        op0=ALU.mult, op1=ALU.mult), rd=[txs[c][h], t_sta[h], t_cst], wr=[thb[c][h]])

        def std_unit(slot, sc, nk, rhs_fn, rhs_tk, evac, ncols=256, koff=0):
            wv = ring[:, slot, :].rearrange("p (k c) -> p k c", c=ncols)
            for h in range(2):
                b = group([(wv[:, koff + k, sc * 128:(sc + 1) * 128], rhs_fn(k, h),
                            [t_slot[slot]] + (rhs_tk(k, h) if callable(rhs_tk) else rhs_tk))
                           for k in range(nk)], h=h)
                evac(h, b)

        def post_norm_rstd(eps_tile=False):
            for h in range(2):
                if eps_tile:
                    b = 6 + h
                    E.op("dve", lambda e, h=h, b=b: e.scalar_tensor_tensor(
                        out=st_a[:, h, P(h)], in0=ps[:, b, P(h)], scalar=1.0 / D, in1=st_b[:, h, P(h)],
                        op0=ALU.mult, op1=ALU.add), rd=[t_bank[b], t_stb[h]], wr=[t_sta[h]])
                    E.op("act", lambda e, h=h: e.activation(out=st_a[:, h, P(h)], in_=st_a[:, h, P(h)], func=AF.Ln),
                         rd=[t_sta[h]], wr=[t_sta[h]])
                    E.op("act", lambda e, h=h: e.activation(out=st_a[:, h, P(h)], in_=st_a[:, h, P(h)], func=AF.Exp, scale=-0.5),
                         rd=[t_sta[h]], wr=[t_sta[h]])
                else:
                    rstd_from_bank(6 + h, h, st_a, t_sta[h], 1.0 / D)

        def post_norm_xupdate(l, gcol, z, t_zz):
            for h in range(2):
                for c in range(16):
                    i = ntmp()
                    E.op("dve", lambda e, c=c, h=h, i=i: e.scalar_tensor_tensor(
                        out=tmpf[:, i, P(h)], in0=z[:, c, H(h)], scalar=cs(l, gcol + c), in1=st_a[:, h, P(h)],
                        op0=ALU.mult, op1=ALU.mult), rd=[t_zz, t_sta[h], t_cst], wr=[t_tmp[i]])
                    E.op("dve", lambda e, c=c, h=h, i=i: e.tensor_tensor(
                        out=xs[:, c, H(h)], in0=xs[:, c, H(h)], in1=tmpf[:, i, P(h)], op=ALU.add),
                        rd=[t_tmp[i]], wr=[txs[c][h]])

        def post_norm_update(l, gcol, z, t_zz, eps_tile=False):
            post_norm_rstd(eps_tile)
            post_norm_xupdate(l, gcol, z, t_zz)

        def ffn_pre(l):
            for h in range(2):
                for c in range(16):
                    E.op("act", lambda e, c=c, h=h: e.activation(out=hb[:, c, H(h)], in_=xs[:, c, H(h)], func=AF.Copy,
                                                               scale=cs(l, C_GMPRE + c)),
                         rd=[txs[c][h], t_cst], wr=[thb[c][h]])
            th = []
            for h in range(2):
                for c in range(16):
                    def f(c=c, h=h):
                        b = 6 + h
                        i = nsq()
                        E.op("act", lambda e: e.activation(out=sq[:, i, P(h)], in_=xs[:, c, H(h)], func=AF.Square),
                             rd=[txs[c][h]], wr=[t_sq[i]])
                        E.op("pe", lambda e: e.matmul(ps[:, b, P(h)], lhsT=ones[:], rhs=sq[:, i, P(h)], start=(c == 0), stop=(c == 15)),
                             rd=[t_ones, t_sq[i]], wr=[t_bank[b]] if c == 0 else [], sig=True)
                        t_bank[b].w = {"pe": E.cnt["pe"]}
                        if c == 15:
                            E.op("dve", lambda e: e.tensor_scalar(out=st_b[:, h, P(h)], in0=ps[:, b, P(h)], scalar1=1.0 / D,
                                                                  scalar2=EPS, op0=ALU.mult, op1=ALU.add),
                                 rd=[t_bank[b]], wr=[t_stb[h]])
                            E.op("dve", lambda e: e.tensor_tensor(out=st_b[:, h, P(h)], in0=st_b[:, h, P(h)], in1=st_b[:, h, P(h)], op=ALU.mult),
                                 rd=[t_stb[h]], wr=[t_stb[h]])
                            E.op("dve", lambda e: e.tensor_scalar(out=st_b[:, h, P(h)], in0=st_b[:, h, P(h)], scalar1=EPS,
                                                                  scalar2=None, op0=ALU.mult),
                                 rd=[t_stb[h]], wr=[t_stb[h]])
                    th.append(f)
            return th

        def zevac_with_stats(z, t_zz, j, h, b, first, last, pend):
            E.op("act", lambda e: e.activation(out=z[:, j, H(h)], in_=ps[:, b, P(h)], func=AF.Copy),
                 rd=[t_bank[b]], wr=[t_zz])
            i = nsq()
            E.op("act", lambda e: e.activation(out=sq[:, i, P(h)], in_=ps[:, b, P(h)], func=AF.Square),
                 rd=[t_bank[b]], wr=[t_sq[i]])
            pend.append((i, h, first, last))

        def flush_stats(pend, keep=0):
            while len(pend) > keep:
                i, h, first, last = pend.pop(0)
                b = 6 + h
                E.op("pe", lambda e, i=i, b=b, first=first, last=last: e.matmul(
                    ps[:, b, P(h)], lhsT=ones[:], rhs=sq[:, i, P(h)], start=first, stop=last),
                    rd=[t_ones, t_sq[i]], wr=[t_bank[b]] if first else [], sig=True)
                t_bank[b].w = {"pe": E.cnt["pe"]}

        kv_state = {"have_rstd": False}

        def kv_norm(l):
            E.dma("sp", "mem", memf[:], memTd[:], wr=[t_s1])
            if not kv_state["have_rstd"]:
                kv_state["have_rstd"] = True
                b = nbank()
                for c in range(16):
                    i = nsq()
                    E.op("act", lambda e, c=c, i=i: e.activation(out=sq[:, i, 0:MEM], in_=memf[:, c, :], func=AF.Square),
                         rd=[t_s1], wr=[t_sq[i]])
                    E.op("pe", lambda e, i=i, c=c, b=b: e.matmul(
                        ps[:, b, 0:MEM], lhsT=ones[:], rhs=sq[:, i, 0:MEM], start=(c == 0), stop=(c == 15)),
                        rd=[t_ones, t_sq[i]], wr=[t_bank[b]] if c == 0 else [], sig=True)
                t_bank[b].w = {"pe": E.cnt["pe"]}
                rstd_from_bank(b, 0, st_mem, t_stmem, 1.0 / D, nout=MEM)
            for c in range(16):
                E.op("dve", lambda e, c=c, g_ap=cs(l, C_GMEM + c): e.scalar_tensor_tensor(
                    out=memn[:, c, :], in0=memf[:, c, :], scalar=g_ap, in1=st_mem[:, 0, 0:MEM],
                    op0=ALU.mult, op1=ALU.mult), rd=[t_s1, t_stmem, t_cst], wr=[t_s2])

        def kv_tiles(t0, t1):
            for t8 in range(t0, t1):
                s = next_tile()
                wv = ring[:, s, :].rearrange("p (k c) -> p k c", c=256)
                if t8 < 4:
                    t = t8
                    for sc in range(2):
                        dch = t * 2 + sc
                        b = group([(wv[:, k, sc * 128:(sc + 1) * 128], memn[:, k, :], [t_slot[s], t_s2])
                                   for k in range(16)], nout=MEM)
                        E.op("act", lambda e, dch=dch, b=b: e.activation(out=kT[:, dch, :], in_=ps[:, b, 0:MEM], func=AF.Copy),
                             rd=[t_bank[b]], wr=[t_kv])
                else:
                    t = t8 - 4
                    for mc in range(2):
                        b = group([(memn[:, k, mc * 128:(mc + 1) * 128], wv[:, k, :], [t_slot[s], t_s2])
                                   for k in range(16)], nout=256)
                        E.op("act", lambda e, t=t, mc=mc, b=b: e.activation(out=vv[:, mc, t * 256:(t + 1) * 256],
                                                                         in_=ps[:, b, 0:256], func=AF.Copy),
                             rd=[t_bank[b]], wr=[t_kv])
                rel(1)

        final_evs = []
        for blk in range(nblk):
            base_i = len(tile_state["order"])
            tile_state["order"].extend(list(range(n_layers * TPL)))
            prefetch(rr["done"] + NSLOT)
            E.dma("sp", "xin", xs[:], xT[:, :, blk * NT:(blk + 1) * NT], wr=t_xs_all)
            for l in range(n_layers):
                cur["lo"] = LO[l + DEPTH - n_layers] if (blk == 0 and TRIM) else 0
                if l == 0:
                    kv_norm(0)
                kv_tiles(0, KV_SPLIT)
                rms_pre(l, C_GPRE)
                kv_tiles(KV_SPLIT, 8)
                hrhs = lambda k, h: hb[:, k, H(h)]
                hbtk = lambda k, h: [thb[k][h]]
                for t, s in enumerate(tiles(4)):
                    for sc in range(2):
                        c = t * 2 + sc
                        std_unit(s, sc, 16, hrhs, hbtk, lambda h, b, c=c: E.op(
                            "act", lambda e: e.activation(out=s2[:, c, H(h)], in_=ps[:, b, P(h)], func=AF.Copy),
                            rd=[t_bank[b]], wr=[t_s2]))
                E.op("dve", lambda e, l=l: e.tensor_copy(out=s1[:, :, 30:32], in_=uhist[:, l, :, :]),
                     rd=[t_hist], wr=[t_s1], strict=True)
                for t, s in enumerate(tiles(4)):
                    for sc in range(2):
                        c = t * 2 + sc
                        std_unit(s, sc, 16, hrhs, hbtk, lambda h, b, c=c: E.op(
                            "dve", lambda e: e.tensor_tensor(out=s1[:, c, 32 + H(h).start:32 + H(h).stop], in0=ps[:, b, P(h)],
                                                             in1=s2[:, c, H(h)], op=ALU.mult),
                            rd=[t_bank[b], t_s2], wr=[t_s1]))
                E.op("dve", lambda e, l=l: e.tensor_copy(out=uhist[:, l, :, :], in_=s1[:, :, 32 + NT - 2:32 + NT]),
                     rd=[t_s1], wr=[t_hist], strict=True)
                for c in range(8):
                    for h in range(2):
                        E.op("dve", lambda e, c=c, h=h, w_ap=cs(l, C_CBW + c * 3 + 0): e.tensor_scalar(
                            out=s2[:, c, H(h)], in0=s1[:, c, 30 + H(h).start:30 + H(h).stop],
                            scalar1=w_ap, scalar2=None, op0=ALU.mult),
                            rd=[t_s1, t_cst], wr=[t_s2])
                        for k in (1, 2):
                            E.op("dve", lambda e, c=c, h=h, k=k, w_ap=cs(l, C_CBW + c * 3 + k): e.scalar_tensor_tensor(
                                out=s2[:, c, H(h)], in0=s1[:, c, 30 + k + H(h).start:30 + k + H(h).stop],
                                scalar=w_ap, in1=s2[:, c, H(h)], op0=ALU.mult, op1=ALU.add),
                                rd=[t_s1, t_cst], wr=[t_s2])
                for t, s in enumerate(tiles(4)):
                    for sc in range(2):
                        c = t * 2 + sc
                        std_unit(s, sc, 16, hrhs, hbtk, lambda h, b, c=c: E.op(
                            "dve", lambda e: e.tensor_tensor(out=bact[:, c, H(h)], in0=ps[:, b, P(h)],
                                                             in1=s2[:, c, H(h)], op=ALU.mult),
                            rd=[t_bank[b], t_s2], wr=[t_bact]))
                for t, s in enumerate(tiles(4)):
                    for sc in range(2):
                        c = t * 2 + sc
                        std_unit(s, sc, 16, hrhs, hbtk, lambda h, b, c=c: E.op(
                            "act", lambda e: e.activation(out=s2[:, c, H(h)], in_=ps[:, b, P(h)], func=AF.Sigmoid),
                            rd=[t_bank[b]], wr=[t_s2]))
                E.op("dve", lambda e, l=l: e.tensor_copy(out=s1[:, :, 2:32], in_=ahist[:, l, :, :]),
                     rd=[t_hist], wr=[t_s1], strict=True)
                for t, s in enumerate(tiles(4)):
                    for sc in range(2):
                        c = t * 2 + sc
                        std_unit(s, sc, 16, hrhs, hbtk, lambda h, b, c=c: E.op(
                            "dve", lambda e: e.tensor_tensor(out=s1[:, c, 32 + H(h).start:32 + H(h).stop], in0=ps[:, b, P(h)],
                                                             in1=s2[:, c, H(h)], op=ALU.mult),
                            rd=[t_bank[b], t_s2], wr=[t_s1]))
                E.op("dve", lambda e, l=l: e.tensor_copy(out=ahist[:, l, :, :], in_=s1[:, :, 32 + NT - 30:32 + NT]),
                     rd=[t_s1], wr=[t_hist], strict=True)

                if cur["lo"] > 0:
                    cur["lo"] += 30
                NDC = 8 - POOL_CHUNKS
                t_s2p.w = dict(t_s2.w); t_s2p.r = dict(t_s2.r)

                def conv_ops(l=l):
                    ops = []
                    for c in range(NDC):
                        def first(c=c):
                            E.op("dve", lambda e: e.tensor_scalar(
                                out=s2[:, c, cur["lo"]:NT], in0=s1[:, c, 2 + cur["lo"]:2 + NT],
                                scalar1=cs(l, C_CAW + c * CK), scalar2=cs(l, C_CAB + c), op0=ALU.mult, op1=ALU.add),
                                rd=[t_s1, t_cst], wr=[t_s2])
                        ops.append(first)
                        for k in range(1, CK):
                            def tap(c=c, k=k):
                                E.op("dve", lambda e: e.scalar_tensor_tensor(
                                    out=s2[:, c, cur["lo"]:NT], in0=s1[:, c, 2 + k + cur["lo"]:2 + k + NT],
                                    scalar=cs(l, C_CAW + c * CK + k), in1=s2[:, c, cur["lo"]:NT], op0=ALU.mult, op1=ALU.add),
                                    rd=[t_s1, t_cst], wr=[t_s2])
                            ops.append(tap)
                    return ops

                def conv_ops_pool(l=l):
                    ops = []
                    for c in range(NDC, 8):
                        for h in range(2):
                            def first(c=c, h=h):
                                E.op("pool", lambda e: e.tensor_scalar(
                                    out=s2[:, c, H(h)], in0=s1[:, c, 2 + H(h).start:2 + H(h).stop],
                                    scalar1=cs(l, C_CAW + c * CK), scalar2=cs(l, C_CAB + c), op0=ALU.mult, op1=ALU.add),
                                    rd=[t_s1, t_cst], wr=[t_s2p])
                            ops.append(first)
                            for k in range(1, CK):
                                def tap(c=c, h=h, k=k):
                                    E.op("pool", lambda e: e.tensor_scalar(
                                        out=ptmp[:], in0=s1[:, c, 2 + k + H(h).start:2 + k + H(h).stop],
                                        scalar1=cs(l, C_CAW + c * CK + k), scalar2=None, op0=ALU.mult),
                                        rd=[t_s1, t_cst], wr=[t_ptmp])
                                    E.op("pool", lambda e: e.tensor_tensor(
                                        out=s2[:, c, H(h)], in0=s2[:, c, H(h)], in1=ptmp[:], op=ALU.add),
                                        rd=[t_ptmp], wr=[t_s2p])
                                ops.append(tap)
                    return ops

                def ln_silu(l=l):
                    for h in range(2):
                        s_sum, s_sq = [], []
                        for c in range(8):
                            i = nsq()
                            E.op("act", lambda e, c=c, i=i, h=h: e.activation(out=sq[:, i, P(h)], in_=s2[:, c, H(h)], func=AF.Copy),
                                 rd=[t_s2, t_s2p], wr=[t_sq[i]])
                            s_sum.append((sq[:, i, P(h)], [t_sq[i]]))
                        bsum = colsum(s_sum, 6 + h, h=h)
                        E.op("dve", lambda e, h=h, bsum=bsum: e.tensor_scalar(
                            out=st_b[:, h, P(h)], in0=ps[:, bsum, P(h)], scalar1=1.0 / 1024, scalar2=None, op0=ALU.mult),
                            rd=[t_bank[bsum]], wr=[t_stb[h]])
                        for c in range(8):
                            i = nsq()
                            E.op("act", lambda e, c=c, i=i, h=h: e.activation(out=sq[:, i, P(h)], in_=s2[:, c, H(h)], func=AF.Square),
                                 rd=[t_s2, t_s2p], wr=[t_sq[i]])
                            s_sq.append((sq[:, i, P(h)], [t_sq[i]]))
                        bsq = colsum(s_sq, 6 + h, h=h)
                        E.op("dve", lambda e, h=h: e.tensor_tensor(out=st_a[:, h, P(h)], in0=st_b[:, h, P(h)], in1=st_b[:, h, P(h)], op=ALU.mult),
                             rd=[t_stb[h]], wr=[t_sta[h]])
                        E.op("dve", lambda e, h=h, bsq=bsq: e.scalar_tensor_tensor(
                            out=st_c[:, h, P(h)], in0=ps[:, bsq, P(h)], scalar=1.0 / 1024, in1=st_a[:, h, P(h)],
                            op0=ALU.mult, op1=ALU.subtract), rd=[t_bank[bsq], t_sta[h]], wr=[t_stc[h]])
                        E.op("act", lambda e, h=h: e.activation(out=st_c[:, h, P(h)], in_=st_c[:, h, P(h)], func=AF.Ln,
                                                             bias=cst_eps[:], scale=1.0), rd=[t_stc[h], t_cst], wr=[t_stc[h]])
                        E.op("act", lambda e, h=h: e.activation(out=st_c[:, h, P(h)], in_=st_c[:, h, P(h)], func=AF.Exp, scale=-0.5),
                             rd=[t_stc[h]], wr=[t_stc[h]])
                        for c in range(8):
                            E.op("dve", lambda e, c=c, h=h: e.tensor_tensor(out=s2[:, c, H(h)], in0=s2[:, c, H(h)],
                                                                          in1=st_b[:, h, P(h)], op=ALU.subtract),
                                 rd=[t_stb[h], t_s2p], wr=[t_s2])
                            E.op("dve", lambda e, c=c, h=h: e.tensor_tensor(out=s2[:, c, H(h)], in0=s2[:, c, H(h)],
                                                                          in1=st_c[:, h, P(h)], op=ALU.mult),
                                 rd=[t_stc[h]], wr=[t_s2])
                            E.op("act", lambda e, c=c, h=h: e.activation(out=aact[:, c, H(h)], in_=s2[:, c, H(h)], func=AF.Silu,
                                                                       bias=cs(l, C_LNB + c), scale=cs(l, C_LNG + c)),
                                 rd=[t_s2, t_cst], wr=[t_aact])
                    for k_, v_ in list(t_s2p.r.items()) + list(t_s2p.w.items()):
                        if t_s2.r.get(k_, 0) < v_:
                            t_s2.r[k_] = v_

                cops = conv_ops()
                pops = conv_ops_pool()
                if bg_conv:
                    E.bg["dve"] = cops
                    E.bg_rate["dve"] = BG_RATE
                    E.bg["pool"] = pops
                    E.bg_rate["pool"] = 14
                else:
                    for f in pops:
                        f()
                    for f in cops:
                        f()
                    ln_silu()

                for t, s in enumerate(tiles(4)):
                    for sc in range(2):
                        c = t * 2 + sc
                        std_unit(s, sc, 16, hrhs, hbtk, lambda h, b, c=c: E.op(
                            "act", lambda e: e.activation(out=qo[:, c, H(h)], in_=ps[:, b, P(h)], func=AF.Copy),
                            rd=[t_bank[b]], wr=[t_qo[c // 2]]))
                        if bg_conv:
                            E.drain_bg("dve", Q_DRAIN)
                units = [(hd, h) for hd in range(4) for h in range(2)]
                upts = {}

                def att_a(u):
                    hd, h = units[u]
                    pts = []
                    for mc in range(2):
                        b = group([(kT[:, hd * 2 + dc, mc * 128:(mc + 1) * 128], qo[:, hd * 2 + dc, H(h)], [t_kv, t_qo[hd]])
                                   for dc in range(2)], h=h)
                        i = nsq()
                        E.op("act", lambda e, i=i, b=b: e.activation(out=sq[:, i, P(h)], in_=ps[:, b, P(h)], func=AF.Exp,
                                                                   scale=1.0 / 16.0),
                             rd=[t_bank[b]], wr=[t_sq[i]])
                        pts.append(i)
                    upts[u] = pts

                def att_b(u):
                    hd, h = units[u]
                    pts = upts[u]
                    bden = colsum([(sq[:, i, P(h)], [t_sq[i]]) for i in pts], 6 + h, h=h)
                    E.op("act", lambda e, h=h, bden=bden: e.activation(out=st_b[:, h, P(h)], in_=ps[:, bden, P(h)], func=AF.Ln),
                         rd=[t_bank[bden]], wr=[t_stb[h]])
                    E.op("act", lambda e, h=h: e.activation(out=st_b[:, h, P(h)], in_=st_b[:, h, P(h)], func=AF.Exp, scale=-1.0),
                         rd=[t_stb[h]], wr=[t_stb[h]])
                    for dc in range(2):
                        b = group([(vv[:, mc, (hd * 2 + dc) * 128:(hd * 2 + dc + 1) * 128], sq[:, pts[mc], P(h)], [t_kv, t_sq[pts[mc]]])
                                   for mc in range(2)], h=h)
                        E.op("dve", lambda e, hd=hd, dc=dc, h=h, b=b: e.tensor_tensor(
                            out=qo[:, hd * 2 + dc, H(h)], in0=ps[:, b, P(h)], in1=st_b[:, h, P(h)], op=ALU.mult),
                            rd=[t_bank[b], t_stb[h]], wr=[t_qo[hd]])

                att_a(0)
                for u in range(8):
                    if u + 1 < 8:
                        att_a(u + 1)
                    att_b(u)
                for jp in range(8):
                    T = {}
                    for br in (1, 2):
                        s = next_tile()
                        wg = ring[:, s, :].rearrange("p (k c) -> p k c", c=256)
                        for sc in range(2):
                            cs_ = slice(sc * 128, (sc + 1) * 128)
                            for h in range(2):
                                bg_ = group([(wg[:, k, cs_], hb[:, k, H(h)], [t_slot[s], thb[k][h]]) for k in range(16)], h=h)
                                i1 = ntmp()
                                E.op("act", lambda e, i1=i1, bg_=bg_: e.activation(out=tmpf[:, i1, P(h)], in_=ps[:, bg_, P(h)], func=AF.Sigmoid),
                                     rd=[t_bank[bg_]], wr=[t_tmp[i1]])
                                T[br, sc, h] = i1
                                if bg_conv:
                                    E.drain_bg("dve", G_DRAIN)
                        rel(1)
                    s = next_tile()
                    wbx = ring[:, s, :].rearrange("p (b k c) -> p b k c", b=2, c=256)
                    for sc in range(2):
                        j = jp * 2 + sc
                        cs_ = slice(sc * 128, (sc + 1) * 128)
                        for h in range(2):
                            i1 = T[1, sc, h]; i2 = T[2, sc, h]
                            byb = group([(wbx[:, 0, k, cs_], bact[:, k, H(h)], [t_slot[s], t_bact]) for k in range(8)], h=h)
                            E.op("dve", lambda e, i1=i1, byb=byb: e.tensor_tensor(out=tmpf[:, i1, P(h)], in0=ps[:, byb, P(h)],
                                                                                in1=tmpf[:, i1, P(h)], op=ALU.mult),
                                 rd=[t_bank[byb], t_tmp[i1]], wr=[t_tmp[i1]])
                            byx = group([(wbx[:, 1, k, cs_], qo[:, k, H(h)], [t_slot[s]] + t_qo) for k in range(8)], h=h)
                            E.op("dve", lambda e, i2=i2, byx=byx: e.tensor_tensor(out=tmpf[:, i2, P(h)], in0=ps[:, byx, P(h)],
                                                                                in1=tmpf[:, i2, P(h)], op=ALU.mult),
                                 rd=[t_bank[byx], t_tmp[i2]], wr=[t_tmp[i2]])
                            E.op("dve", lambda e, i1=i1, i2=i2, j=j, h=h: e.tensor_tensor(
                                out=merged[:, j, H(h)], in0=tmpf[:, i1, P(h)], in1=tmpf[:, i2, P(h)], op=ALU.add),
                                rd=[t_tmp[i1], t_tmp[i2]], wr=[t_mrg])
                    rel(1)
                    if bg_conv and jp == LN_AT:
                        E.drain_bg("pool")
                        E.drain_bg("dve")
                        ln_silu()
                for jq in range(4):
                    T = {}
                    for half4 in range(2):
                        s = next_tile()
                        wg = ring[:, s, :].rearrange("p (k c) -> p k c", c=256)
                        for sc in range(2):
                            q4 = half4 * 2 + sc
                            cg = slice(sc * 128, (sc + 1) * 128)
                            for h in range(2):
                                bg0 = group([(wg[:, k, cg], hb[:, k, H(h)], [t_slot[s], thb[k][h]]) for k in range(16)], h=h)
                                i1 = ntmp()
                                E.op("act", lambda e, i1=i1, bg0=bg0: e.activation(out=tmpf[:, i1, P(h)], in_=ps[:, bg0, P(h)], func=AF.Sigmoid),
                                     rd=[t_bank[bg0]], wr=[t_tmp[i1]])
                                T[q4, h] = i1
                        rel(1)
                    if bg_conv and jq == 0 and LN_AT < 0:
                        E.drain_bg("pool")
                        E.drain_bg("dve")
                        ln_silu()
                    s = next_tile()
                    wa = ring[:, s, :].rearrange("p (k c) -> p k c", c=512)
                    for q4 in range(4):
                        j = jq * 4 + q4
                        ca = slice(q4 * 128, (q4 + 1) * 128)
                        for h in range(2):
                            i1 = T[q4, h]
                            bya = group([(wa[:, k, ca], aact[:, k, H(h)], [t_slot[s], t_aact]) for k in range(8)], h=h)
                            E.op("dve", lambda e, i1=i1, bya=bya: e.tensor_tensor(out=tmpf[:, i1, P(h)], in0=ps[:, bya, P(h)],
                                                                                in1=tmpf[:, i1, P(h)], op=ALU.mult),
                                 rd=[t_bank[bya], t_tmp[i1]], wr=[t_tmp[i1]])
                            E.op("dve", lambda e, i1=i1, j=j, h=h: e.tensor_tensor(
                                out=merged[:, j, H(h)], in0=merged[:, j, H(h)], in1=tmpf[:, i1, P(h)], op=ALU.add),
                                rd=[t_tmp[i1]], wr=[t_mrg])
                    rel(1)
                for k, v in list(t_s2.w.items()) + list(t_s2.r.items()):
                    if t_s1.r.get(k, 0) < v:
                        t_s1.r[k] = v
                pend = []
                for t, s in enumerate(tiles(8)):
                    for sc in range(2):
                        j = t * 2 + sc
                        std_unit(s, sc, 16, lambda k, h: merged[:, k, H(h)], [t_mrg],
                                 lambda h, b, j=j: zevac_with_stats(zm, t_s1, j, h, b, j == 0, j == 15, pend))
                        flush_stats(pend, keep=2)
                flush_stats(pend)
                post_norm_update(l, C_GPOST, zm, t_s1)
                t_s2.w = dict(t_s1.w); t_s2.r = dict(t_s1.r)
                ffn_stats = ffn_pre(l)
                for tk in (t_s1, t_s2, t_aact, t_bact, t_kv) + tuple(t_qo):
                    for k, v in list(tk.w.items()) + list(tk.r.items()):
                        if t_hid.r.get(k, 0) < v:
                            t_hid.r[k] = v
                for t, s in enumerate(tiles(32)):
                    for sc in range(2):
                        c = t * 2 + sc

                        def ev_up(h, b, c=c):
                            i = ntmp()
                            E.op("act", lambda e: e.activation(out=tmpf[:, i, P(h)], in_=ps[:, b, P(h)], func=AF.Relu),
                                 rd=[t_bank[b]], wr=[t_tmp[i]])
                            E.op("dve", lambda e: e.tensor_tensor(out=hid[:, c, H(h)], in0=tmpf[:, i, P(h)], in1=tmpf[:, i, P(h)], op=ALU.mult),
                                 rd=[t_tmp[i]], wr=[t_hid])
                        std_unit(s, sc, 16, hrhs, hbtk, ev_up)
                    if t >= 2:
                        for _ in range(2):
                            if ffn_stats:
                                ffn_stats.pop(0)()
                for tk in t_hb_all + [t_mrg]:
                    for k, v in list(tk.w.items()) + list(tk.r.items()):
                        if t_z.r.get(k, 0) < v:
                            t_z.r[k] = v
                pend = []
                for jc in range(16):
                    sl = [next_tile(), next_tile()]
                    for h in range(2):
                        mms = []
                        for kg in range(2):
                            wv = ring[:, sl[kg], :].rearrange("p (k c) -> p k c", c=128)
                            mms += [(wv[:, k, :], hid[:, kg * 32 + k, H(h)], [t_slot[sl[kg]], t_hid]) for k in range(32)]
                        b = group(mms, h=h)
                        zevac_with_stats(zf, t_z, jc, h, b, jc == 0, jc == 15, pend)
                    flush_stats(pend, keep=2)
                    rel(2)
                flush_stats(pend)
                for tk in (t_s1, t_s2, t_aact, t_bact, t_kv) + tuple(t_qo):
                    tk.w = dict(t_hid.w); tk.r = dict(t_hid.r)
                post_norm_rstd(eps_tile=True)
                if l + 1 < n_layers:
                    kv_norm(l + 1)
                post_norm_xupdate(l, C_GMPOST, zf, t_z)
                for tk in t_hb_all:
                    tk.w = dict(t_z.w); tk.r = dict(t_z.r)
                t_mrg.w = dict(t_z.w); t_mrg.r = dict(t_z.r)
                if blk == 0:
                    for c in range(16):
                        E.op("dve", lambda e, c=c: e.tensor_scalar(out=xs[:, c, 0:HALO], in0=xs[:, c, 0:HALO],
                                                                  scalar1=hmask[:, 0:1], scalar2=None, op0=ALU.mult),
                             rd=[t_cst], wr=txs[c])
            if blk == 0:
                ev = E.dma("sp", "out", outT[:, :, 0:NT - HALO], xs[:, :, HALO:NT], rd=t_xs_all)
            else:
                ev = E.dma("sp", "out", outT[:, :, blk * NT - HALO:(blk + 1) * NT - HALO], xs[:], rd=t_xs_all)
            final_evs.append(ev)
        E.finalize(st, final_evs)
    return nc


def _std(W, c0, ncols=256, k0=0, nk=16):
    blk = W[k0 * 128:(k0 + nk) * 128, c0:c0 + ncols]
    return blk.reshape(nk, 128, ncols).transpose(1, 0, 2).reshape(128, nk * ncols)


def pack_layer_tiles(out, w_in, w_a_out, w_b_out, w_kv, w_x_out, w_o, w_up, w_down):
    i = 0

    def put(a):
        nonlocal i
        out[i] = a
        i += 1
    for t in range(8):
        put(_std(w_kv, t * 256))
    for base in (3072, 4096, 2048, 1024, 0, 5120):
        for t in range(4):
            put(_std(w_in, base + t * 256))
    for jp in range(8):
        put(_std(w_in, 8192 + jp * 256))
        put(_std(w_in, 10240 + jp * 256))
        put(np.concatenate([_std(w_b_out, jp * 256, nk=8), _std(w_x_out, jp * 256, nk=8)], axis=1))
    for jq in range(4):
        put(_std(w_in, 6144 + (2 * jq) * 256))
        put(_std(w_in, 6144 + (2 * jq + 1) * 256))
        put(_std(w_a_out, jq * 512, ncols=512, nk=8))
    for t in range(8):
        put(_std(w_o, t * 256))
    for t in range(32):
        put(_std(w_up, t * 256))
    for jc in range(16):
        for kg in range(2):
            put(_std(w_down, jc * 128, ncols=128, k0=kg * 32, nk=32))
    assert i == TPL


def pack_consts(layers, g_mix_pre, g_mix_post, g_mlp_pre, g_mlp_post, g_mem, conv_a_w, conv_a_b, ln_a_g, ln_a_b, conv_b_w):
    cst = np.zeros((128, len(layers) * C_PER), np.float32)
    for li, l in enumerate(layers):
        o = li * C_PER
        for col, g in ((C_GPRE, g_mix_pre), (C_GPOST, g_mix_post), (C_GMPRE, g_mlp_pre), (C_GMPOST, g_mlp_post), (C_GMEM, g_mem)):
            cst[:, o + col:o + col + 16] = g[l].reshape(16, 128).T
        cst[:, o + C_CAW:o + C_CAW + 8 * CK] = conv_a_w[l].reshape(CK, 8, 128).transpose(2, 1, 0).reshape(128, 8 * CK)
        cst[:, o + C_CAB:o + C_CAB + 8] = conv_a_b[l].reshape(8, 128).T
        cst[:, o + C_LNG:o + C_LNG + 8] = ln_a_g[l].reshape(8, 128).T
        cst[:, o + C_LNB:o + C_LNB + 8] = ln_a_b[l].reshape(8, 128).T
        cst[:, o + C_CBW:o + C_CBW + 24] = conv_b_w[l].reshape(3, 8, 128).transpose(2, 1, 0).reshape(128, 24)
    return cst


def shard_x(x2d, core):
    lo = core * TOK - HALO
    if lo < 0:
        blk = np.concatenate([np.zeros((HALO, D), np.float32), x2d[0:TOK]], axis=0)
    else:
        blk = x2d[lo:lo + TOK + HALO]
    return np.ascontiguousarray(blk.T.reshape(16, 128, TOK + HALO).transpose(1, 0, 2))


_PROG_CACHE = {}


def _get_prog(n_layers):
    if n_layers not in _PROG_CACHE:
        _PROG_CACHE[n_layers] = build_program(n_layers)
    return _PROG_CACHE[n_layers]


FUSED = True


def kernel(x, mem, g_mix_pre, w_in, conv_a_w, conv_a_b, ln_a_g, ln_a_b, w_a_out, conv_b_w, w_b_out,
           g_mem, w_kv, w_x_out, w_o, g_mix_post, g_mlp_pre, w_up, w_down, g_mlp_post):
    f = lambda a: np.asarray(a, dtype=np.float32)
    x2d = f(x)[0]
    memT = np.ascontiguousarray(f(mem)[0].T.reshape(16, 128, MEM).transpose(1, 0, 2))
    groups = [list(range(DEPTH))] if FUSED else [[l] for l in range(DEPTH)]
    for layers in groups:
        nl = len(layers)
        wts = np.empty((nl * TPL, 128, TILE), np.float32)
        for li, l in enumerate(layers):
            pack_layer_tiles(wts[li * TPL:(li + 1) * TPL], f(w_in[l]), f(w_a_out[l]), f(w_b_out[l]), f(w_kv[l]),
                             f(w_x_out[l]), f(w_o[l]), f(w_up[l]), f(w_down[l]))
        cst = pack_consts(layers, f(g_mix_pre), f(g_mix_post), f(g_mlp_pre), f(g_mlp_post), f(g_mem),
                          f(conv_a_w), f(conv_a_b), f(ln_a_g), f(ln_a_b), f(conv_b_w))
        nc = _get_prog(nl)
        in_maps = []
        for c in range(NCORE):
            in_maps.append({"xT": shard_x(x2d, c), "wts": wts, "cst": cst, "memT": memT,
                            "hmask": np.full((128, 1), 0.0 if c == 0 else 1.0, np.float32)})
        res = run_bass_kernel_spmd(nc, in_maps, core_ids=list(range(NCORE)))
        outs = []
        for c in range(NCORE):
            o = res.results[c]["outT"]
            outs.append(o.transpose(2, 1, 0).reshape(TOK, D))
        x2d = np.concatenate(outs, axis=0)
    return np.ascontiguousarray(x2d[None]).astype(np.float32)
```

```python
import numpy as np
from contextlib import ExitStack
import concourse.bass as bass
import concourse.mybir as mybir
from concourse.bass_utils import run_bass_kernel_spmd

F32 = mybir.dt.float32
BF16 = mybir.dt.bfloat16
AF = mybir.ActivationFunctionType
ALU = mybir.AluOpType

D = 2048
SEQ = 8192
DEPTH = 4
NCORE = 8
TOK = SEQ // NCORE
HALO = 128
NT = 576
NH = 288
NBLK = 2
MEM = 256
CK = 31
EPS = 1e-6
TPL = 140
TILE = 4096
NSLOT = 4
NTMP = 8
POOL_CHUNKS = 0
LN_AT = 6
TRIM = True
BG_RATE = 1
BALANCE = False
FAST_RECIP = False
LNEXP = True
KV_SPLIT = 3
Q_DRAIN = 4
G_DRAIN = 2

C_GPRE, C_GPOST, C_GMPRE, C_GMPOST, C_GMEM = 0, 16, 32, 48, 64
C_CAW = 80
C_CAB = C_CAW + 8 * CK
C_LNG = C_CAB + 8
C_LNB = C_LNG + 8
C_CBW = C_LNB + 8
C_PER = C_CBW + 24

ENGS = ("pe", "act", "dve", "pool", "sp")


class Tk:
    __slots__ = ("w", "r")

    def __init__(self):
        self.w = {}
        self.r = {}


class _Rec:
    def __init__(self):
        self.call = None

    def __getattr__(self, name):
        def f(*a, **k):
            assert self.call is None
            self.call = (name, a, k)
        return f


def _eager(fn):
    r = _Rec()
    fn(r)
    name, a, k = r.call
    return lambda e: getattr(e, name)(*a, **k)


class Emitter:
    def __init__(self, nc, strict_same=False):
        self.nc = nc
        self.streams = {e: [] for e in ENGS}
        self.cnt = {e: 0 for e in ENGS}
        self.waited = {e: {} for e in ENGS}
        self.dma_cnt = {}
        self.strict_same = strict_same
        self.sems = {}
        self.bg = {"dve": [], "pool": []}
        self.bg_rate = {"dve": 0, "pool": 0}
        self.in_bg = False

    def _deps(self, rd, wr, extra):
        deps = {}
        for t in rd:
            for k, v in t.w.items():
                if deps.get(k, 0) < v:
                    deps[k] = v
        for t in wr:
            for d in (t.w, t.r):
                for k, v in d.items():
                    if deps.get(k, 0) < v:
                        deps[k] = v
        for d in extra:
            if d is None:
                continue
            for k, v in d.items():
                if deps.get(k, 0) < v:
                    deps[k] = v
        return deps

    def _waits(self, eng, deps, strict):
        waits = []
        for k, v in deps.items():
            if k == eng and not strict:
                continue
            if self.waited[eng].get(k, 0) >= v:
                continue
            self.waited[eng][k] = v
            waits.append((k, v))
        return waits

    def op(self, eng, fn, rd=(), wr=(), sig=True, extra=(), strict=None):
        strict = self.strict_same if strict is None else strict
        deps = self._deps(rd, wr, extra)
        waits = self._waits(eng, deps, strict)
        if sig:
            self.cnt[eng] += 1
            c = self.cnt[eng]
        else:
            c = self.cnt[eng] + 1
        self.streams[eng].append((waits, _eager(fn), 1 if sig else 0, eng))
        for t in rd:
            if t.r.get(eng, 0) < c:
                t.r[eng] = c
        for t in wr:
            t.w = {eng: c}
            t.r = {}
        ev = {eng: c}
        if eng == "dve" and self.bg["dve"] and not self.in_bg:
            self.drain_bg("dve", self.bg_rate["dve"])
        return ev

    def drain_bg(self, eng, n=None):
        self.in_bg = True
        k = 0
        q = self.bg[eng]
        while q and (n is None or k < n):
            q.pop(0)()
            k += 1
        self.in_bg = False

    def dma(self, q, slot, out, in_, rd=(), wr=(), extra=()):
        key = "dma:" + slot
        deps = self._deps(rd, wr, extra)
        waits = self._waits(q, deps, False)
        self.dma_cnt[key] = self.dma_cnt.get(key, 0) + 16
        c = self.dma_cnt[key]
        self.streams[q].append((waits, lambda e, o=out, i=in_: e.dma_start(out=o, in_=i), 16, key))
        for t in rd:
            if t.r.get(key, 0) < c:
                t.r[key] = c
        for t in wr:
            t.w = {key: c}
            t.r = {}
        if q == "pool" and self.bg["pool"] and not self.in_bg:
            self.drain_bg("pool", self.bg_rate["pool"])
        return {key: c}

    def finalize(self, st, final_waits):
        nc = self.nc
        keys = list(ENGS) + sorted(self.dma_cnt.keys())
        for k in keys:
            self.sems[k] = st.enter_context(nc.semaphore("s_" + k.replace(":", "_")))
        block = st.enter_context(nc.Block())
        fin = {}
        for d in final_waits:
            for k, v in d.items():
                fin[k] = max(fin.get(k, 0), v)

        def replay(eng_name):
            def run(e):
                for waits, fn, inc, semkey in self.streams[eng_name]:
                    for k, v in waits:
                        e.wait_ge(self.sems[k], v)
                    ins = fn(e)
                    if inc:
                        ins.then_inc(self.sems[semkey], inc)
                if eng_name == "sp":
                    for k, v in fin.items():
                        e.wait_ge(self.sems[k], v)
            return run

        block.tensor(replay("pe"))
        block.scalar(replay("act"))
        block.vector(replay("dve"))
        block.gpsimd(replay("pool"))
        block.sync(replay("sp"))


def build_program(n_layers, nblk=NBLK, bg_conv=True):
    nc = bass.Bass("TRN2", target_bir_lowering=False)
    xT = nc.dram_tensor("xT", [128, 16, nblk * NT], F32, kind="ExternalInput").ap()
    wts = nc.dram_tensor("wts", [n_layers * TPL, 128, TILE], F32, kind="ExternalInput").ap()
    cstd = nc.dram_tensor("cst", [128, n_layers * C_PER], F32, kind="ExternalInput").ap()
    memTd = nc.dram_tensor("memT", [128, 16, MEM], F32, kind="ExternalInput").ap()
    hmaskd = nc.dram_tensor("hmask", [128, 1], F32, kind="ExternalInput").ap()
    outT = nc.dram_tensor("outT", [128, 16, nblk * NT - HALO], F32, kind="ExternalOutput").ap()

    st = ExitStack()
    with st:
        def sb(name, shape, dt):
            return st.enter_context(nc.sbuf_tensor(name, shape, dt))

        cst = sb("cst_sb", [128, n_layers * C_PER], F32)
        hmask = sb("hmask_sb", [128, 1], F32)
        xs = sb("xs", [128, 16, NT], F32)
        ZA = sb("ZA", [128, 16 * NT], F32)
        HA = sb("HA", [128, 64 * NT], BF16)
        ring = sb("ring", [128, NSLOT, TILE], BF16)
        ones = sb("ones", [128, 128], BF16)
        sq = sb("sq", [128, 8, NH], BF16)
        st_a = sb("st_a", [128, 2, NH], F32)
        st_b = sb("st_b", [128, 2, NH], F32)
        st_c = sb("st_c", [128, 2, NH], F32)
        tmpf = sb("tmpf", [128, NTMP, NH], F32)
        ptmp = sb("ptmp", [128, NH], F32) if POOL_CHUNKS > 0 else None
        ahist = sb("ahist", [128, n_layers, 8, 30], F32)
        uhist = sb("uhist", [128, n_layers, 8, 2], F32)
        ps = st.enter_context(nc.psum_tensor("ps", [128, 8, 512], F32))

        ZAb = ZA[:].bitcast(BF16)
        hb = ZAb[:, 0:16 * NT].rearrange("p (c n) -> p c n", c=16)
        merged = ZAb[:, 16 * NT:32 * NT].rearrange("p (c n) -> p c n", c=16)
        zf = ZA[:].rearrange("p (c n) -> p c n", c=16)
        HAf = HA[:].bitcast(F32)
        o_s1 = 0
        n_s1 = 8 * (NT + 32)
        o_s2 = n_s1
        n_s2 = 8 * NT
        s1 = HAf[:, o_s1:o_s1 + n_s1].rearrange("p (c n) -> p c n", c=8)
        s2 = HAf[:, o_s2:o_s2 + n_s2].rearrange("p (c n) -> p c n", c=8)
        zm = HAf[:, 0:16 * NT].rearrange("p (c n) -> p c n", c=16)
        ob16 = 2 * (n_s1 + n_s2)
        aact = HA[:, ob16:ob16 + 8 * NT].rearrange("p (c n) -> p c n", c=8)
        bact = HA[:, ob16 + 8 * NT:ob16 + 16 * NT].rearrange("p (c n) -> p c n", c=8)
        qo = HA[:, ob16 + 16 * NT:ob16 + 24 * NT].rearrange("p (c n) -> p c n", c=8)
        okv = ob16 + 24 * NT
        kT = HA[:, okv:okv + 8 * MEM].rearrange("p (c n) -> p c n", c=8)
        vv = HA[:, okv + 8 * MEM:okv + 16 * MEM].rearrange("p (c n) -> p c n", c=2)
        assert okv + 16 * MEM <= 64 * NT
        hid = HA[:].rearrange("p (c n) -> p c n", c=64)
        memf = HAf[:, 0:16 * MEM].rearrange("p (c n) -> p c n", c=16)
        memn = HA[:, 2 * o_s2:2 * o_s2 + 16 * MEM].rearrange("p (c n) -> p c n", c=16)

        E = Emitter(nc)
        t_cst = Tk(); txs = [[Tk(), Tk()] for _ in range(16)]; t_xs_all = [t for p in txs for t in p]; thb = [[Tk(), Tk()] for _ in range(16)]; t_hb_all = [t for p in thb for t in p]; t_mrg = Tk(); t_z = Tk()
        t_s1 = Tk(); t_s2 = Tk(); t_aact = Tk(); t_bact = Tk(); t_qo = [Tk() for _ in range(4)]
        t_kv = Tk(); t_hid = Tk(); t_mem = Tk(); t_memn = Tk()
        t_slot = [Tk() for _ in range(NSLOT)]
        t_bank = [Tk() for _ in range(8)]
        t_sq = [Tk() for _ in range(8)]
        t_sta = [Tk(), Tk()]; t_stb = [Tk(), Tk()]; t_stc = [Tk(), Tk()]
        t_tmp = [Tk() for _ in range(NTMP)]
        t_s2p = Tk(); t_ptmp = Tk()
        t_hist = Tk(); t_ones = Tk()
        rr = {"bank": 0, "sq": 0, "tmp": 0, "pt": 0, "tile": 0, "done": 0}

        cur = {"lo": 0}
        LO = [8, 38, 68, 98]

        def MID():
            lo = cur["lo"]
            if lo == 0 or not BALANCE:
                return NH
            return lo + ((NT - lo) // 4) * 2 + ((NT - lo) % 4 > 0) * 2 if False else max(NH, ((lo + NT) // 4) * 2)

        def H(h):
            return slice(cur["lo"], MID()) if h == 0 else slice(MID(), NT)

        def P(h):
            sl = H(h)
            return slice(0, sl.stop - sl.start)

        def cs(l, col):
            c0 = l * C_PER + col
            return cst[:, c0:c0 + 1]

        def nbank():
            b = rr["bank"]
            rr["bank"] = (b + 1) % 6
            return b

        def nsq():
            i = rr["sq"]; rr["sq"] = (i + 1) % 8
            return i

        def ntmp():
            i = rr["tmp"]; rr["tmp"] = (i + 1) % NTMP
            return i


        E.dma("sp", "cst", cst[:], cstd[:], wr=[t_cst])
        E.dma("sp", "hmask", hmask[:], hmaskd[:], wr=[t_cst])
        E.op("dve", lambda e: e.memset(ones[:], 1.0), wr=[t_ones])
        E.op("dve", lambda e: e.memset(HAf[:, 0:n_s1 + n_s2], 0.0), wr=[t_s1, t_s2])
        E.op("dve", lambda e: e.memset(ahist[:], 0.0), wr=[t_hist])
        E.op("dve", lambda e: e.memset(uhist[:], 0.0), wr=[t_hist])

        tile_state = {"next_load": 0, "order": []}

        def prefetch(upto):
            while tile_state["next_load"] < min(upto, len(tile_state["order"])):
                i = tile_state["next_load"]
                s = i % NSLOT
                E.dma("pool", "w%d" % s, ring[:, s, :], wts[tile_state["order"][i]], wr=[t_slot[s]])
                tile_state["next_load"] += 1

        def next_tile():
            i = rr["tile"]
            rr["tile"] += 1
            assert i - rr["done"] < NSLOT
            prefetch(i + 1)
            return i % NSLOT

        def tiles(n):
            for _ in range(n):
                s_ = next_tile()
                yield s_
                rel(1)

        def rel(n=1):
            rr["done"] += n
            prefetch(rr["done"] + NSLOT)

        def group(mms, h=None, nout=None, bank=None):
            b = nbank() if bank is None else bank
            n = len(mms)
            psl = P(h) if h is not None else slice(0, nout)
            for i, (l_ap, r_ap, rdt) in enumerate(mms):
                E.op("pe", lambda e, l_ap=l_ap, r_ap=r_ap, i=i, b=b: e.matmul(
                    ps[:, b, psl], lhsT=l_ap, rhs=r_ap, start=(i == 0), stop=(i == n - 1)),
                    rd=rdt, wr=[t_bank[b]] if i == 0 else [], sig=(i == n - 1))
            t_bank[b].w = {"pe": E.cnt["pe"]}
            return b

        def colsum(srcs, bank, h=None, nout=None):
            return group([(ones[:], s_ap, [t_ones] + tk) for s_ap, tk in srcs], h=h, nout=nout, bank=bank)

        def rstd_from_bank(bank, h, dst, t_dst, inv_n, nout=None):
            sl = P(h) if nout is None else slice(0, nout)
            if LNEXP:
                E.op("act", lambda e: e.activation(out=dst[:, h, sl], in_=ps[:, bank, sl], func=AF.Ln,
                                                   bias=cst_eps[:], scale=inv_n),
                     rd=[t_bank[bank], t_cst], wr=[t_dst])
                E.op("act", lambda e: e.activation(out=dst[:, h, sl], in_=dst[:, h, sl], func=AF.Exp, scale=-0.5),
                     rd=[t_dst], wr=[t_dst])
                return
            E.op("act", lambda e: e.activation(out=dst[:, h, sl], in_=ps[:, bank, sl], func=AF.Sqrt,
                                               bias=cst_eps[:], scale=inv_n),
                 rd=[t_bank[bank], t_cst], wr=[t_dst])
            E.op("dve", lambda e: e.reciprocal(out=dst[:, h, sl], in_=dst[:, h, sl]), rd=[t_dst], wr=[t_dst])

        st_mem = sb("st_mem", [128, 1, MEM], F32)
        t_stmem = Tk()
        cst_eps = sb("cst_eps", [128, 1], F32)
        E.op("dve", lambda e: e.memset(cst_eps[:], EPS), wr=[t_cst])

        def rms_pre(l, gcol):
            for h in range(2):
                b = 6 + h
                for c in range(16):
                    i = nsq()
                    E.op("act", lambda e, c=c, i=i, h=h: e.activation(out=sq[:, i, P(h)], in_=xs[:, c, H(h)], func=AF.Square),
                         rd=[txs[c][h]], wr=[t_sq[i]])
                    E.op("pe", lambda e, i=i, c=c, b=b: e.matmul(
                        ps[:, b, P(h)], lhsT=ones[:], rhs=sq[:, i, P(h)], start=(c == 0), stop=(c == 15)),
                        rd=[t_ones, t_sq[i]], wr=[t_bank[b]] if c == 0 else [], sig=True)
                t_bank[b].w = {"pe": E.cnt["pe"]}
                rstd_from_bank(b, h, st_a, t_sta[h], 1.0 / D)
                for c in range(16):
                    E.op("dve", lambda e, c=c, h=h: e.scalar_tensor_tensor(
                        out=hb[:, c, H(h)], in0=xs[:, c, H(h)], scalar=cs(l, gcol + c), in1=st_a[:, h, P(h)],
                        op0=ALU.mult, op1=ALU.mult), rd=[txs[c][h], t_sta[h], t_cst], wr=[thb[c][h]])

        def std_unit(slot, sc, nk, rhs_fn, rhs_tk, evac, ncols=256, koff=0):
            wv = ring[:, slot, :].rearrange("p (k c) -> p k c", c=ncols)
            for h in range(2):
                b = group([(wv[:, koff + k, sc * 128:(sc + 1) * 128], rhs_fn(k, h),
                            [t_slot[slot]] + (rhs_tk(k, h) if callable(rhs_tk) else rhs_tk))
                           for k in range(nk)], h=h)
                evac(h, b)

        def post_norm_rstd(eps_tile=False):
            for h in range(2):
                if eps_tile:
                    b = 6 + h
                    E.op("dve", lambda e, h=h, b=b: e.scalar_tensor_tensor(
                        out=st_a[:, h, P(h)], in0=ps[:, b, P(h)], scalar=1.0 / D, in1=st_b[:, h, P(h)],
                        op0=ALU.mult, op1=ALU.add), rd=[t_bank[b], t_stb[h]], wr=[t_sta[h]])
                    E.op("act", lambda e, h=h: e.activation(out=st_a[:, h, P(h)], in_=st_a[:, h, P(h)], func=AF.Ln),
                         rd=[t_sta[h]], wr=[t_sta[h]])
                    E.op("act", lambda e, h=h: e.activation(out=st_a[:, h, P(h)], in_=st_a[:, h, P(h)], func=AF.Exp, scale=-0.5),
                         rd=[t_sta[h]], wr=[t_sta[h]])
                else:
                    rstd_from_bank(6 + h, h, st_a, t_sta[h], 1.0 / D)

        def post_norm_xupdate(l, gcol, z, t_zz):
            for h in range(2):
                for c in range(16):
                    i = ntmp()
                    E.op("dve", lambda e, c=c, h=h, i=i: e.scalar_tensor_tensor(
                        out=tmpf[:, i, P(h)], in0=z[:, c, H(h)], scalar=cs(l, gcol + c), in1=st_a[:, h, P(h)],
                        op0=ALU.mult, op1=ALU.mult), rd=[t_zz, t_sta[h], t_cst], wr=[t_tmp[i]])
                    E.op("dve", lambda e, c=c, h=h, i=i: e.tensor_tensor(
                        out=xs[:, c, H(h)], in0=xs[:, c, H(h)], in1=tmpf[:, i, P(h)], op=ALU.add),
                        rd=[t_tmp[i]], wr=[txs[c][h]])

        def post_norm_update(l, gcol, z, t_zz, eps_tile=False):
            post_norm_rstd(eps_tile)
            post_norm_xupdate(l, gcol, z, t_zz)

        def ffn_pre(l):
            for h in range(2):
                for c in range(16):
                    E.op("act", lambda e, c=c, h=h: e.activation(out=hb[:, c, H(h)], in_=xs[:, c, H(h)], func=AF.Copy,
                                                               scale=cs(l, C_GMPRE + c)),
                         rd=[txs[c][h], t_cst], wr=[thb[c][h]])
            th = []
            lag = []

            def pe_one():
                (i, c, h) = lag.pop(0)
                b = 6 + h
                E.op("pe", lambda e: e.matmul(ps[:, b, P(h)], lhsT=ones[:], rhs=sq[:, i, P(h)], start=(c == 0), stop=(c == 15)),
                     rd=[t_ones, t_sq[i]], wr=[t_bank[b]] if c == 0 else [], sig=True)
                t_bank[b].w = {"pe": E.cnt["pe"]}
                if c == 15:
                    E.op("dve", lambda e: e.tensor_scalar(out=st_b[:, h, P(h)], in0=ps[:, b, P(h)], scalar1=1.0 / D,
                                                          scalar2=EPS, op0=ALU.mult, op1=ALU.add),
                         rd=[t_bank[b]], wr=[t_stb[h]])
                    E.op("dve", lambda e: e.tensor_tensor(out=st_b[:, h, P(h)], in0=st_b[:, h, P(h)], in1=st_b[:, h, P(h)], op=ALU.mult),
                         rd=[t_stb[h]], wr=[t_stb[h]])
                    E.op("dve", lambda e: e.tensor_scalar(out=st_b[:, h, P(h)], in0=st_b[:, h, P(h)], scalar1=EPS,
                                                          scalar2=None, op0=ALU.mult),
                         rd=[t_stb[h]], wr=[t_stb[h]])
            for h in range(2):
                for c in range(16):
                    def f(c=c, h=h):
                        i = nsq()
                        E.op("act", lambda e: e.activation(out=sq[:, i, P(h)], in_=xs[:, c, H(h)], func=AF.Square),
                             rd=[txs[c][h]], wr=[t_sq[i]])
                        lag.append((i, c, h))
                        if len(lag) > 3:
                            pe_one()
                    th.append(f)

            def fin():
                while lag:
                    pe_one()
            th.append(fin)
            return th

        def zevac_with_stats(z, t_zz, j, h, b, first, last, pend):
            E.op("act", lambda e: e.activation(out=z[:, j, H(h)], in_=ps[:, b, P(h)], func=AF.Copy),
                 rd=[t_bank[b]], wr=[t_zz])
            i = nsq()
            E.op("act", lambda e: e.activation(out=sq[:, i, P(h)], in_=ps[:, b, P(h)], func=AF.Square),
                 rd=[t_bank[b]], wr=[t_sq[i]])
            pend.append((i, h, first, last))

        def flush_stats(pend, keep=0):
            while len(pend) > keep:
                i, h, first, last = pend.pop(0)
                b = 6 + h
                E.op("pe", lambda e, i=i, b=b, first=first, last=last: e.matmul(
                    ps[:, b, P(h)], lhsT=ones[:], rhs=sq[:, i, P(h)], start=first, stop=last),
                    rd=[t_ones, t_sq[i]], wr=[t_bank[b]] if first else [], sig=True)
                t_bank[b].w = {"pe": E.cnt["pe"]}

        kv_state = {"have_rstd": False}

        def kv_norm(l):
            E.dma("sp", "mem", memf[:], memTd[:], wr=[t_s1])
            if not kv_state["have_rstd"]:
                kv_state["have_rstd"] = True
                b = nbank()
                for c in range(16):
                    i = nsq()
                    E.op("act", lambda e, c=c, i=i: e.activation(out=sq[:, i, 0:MEM], in_=memf[:, c, :], func=AF.Square),
                         rd=[t_s1], wr=[t_sq[i]])
                    E.op("pe", lambda e, i=i, c=c, b=b: e.matmul(
                        ps[:, b, 0:MEM], lhsT=ones[:], rhs=sq[:, i, 0:MEM], start=(c == 0), stop=(c == 15)),
                        rd=[t_ones, t_sq[i]], wr=[t_bank[b]] if c == 0 else [], sig=True)
                t_bank[b].w = {"pe": E.cnt["pe"]}
                rstd_from_bank(b, 0, st_mem, t_stmem, 1.0 / D, nout=MEM)
            for c in range(16):
                E.op("dve", lambda e, c=c, g_ap=cs(l, C_GMEM + c): e.scalar_tensor_tensor(
                    out=memn[:, c, :], in0=memf[:, c, :], scalar=g_ap, in1=st_mem[:, 0, 0:MEM],
                    op0=ALU.mult, op1=ALU.mult), rd=[t_s1, t_stmem, t_cst], wr=[t_s2])

        def kv_tiles(t0, t1):
            for t8 in range(t0, t1):
                s = next_tile()
                wv = ring[:, s, :].rearrange("p (k c) -> p k c", c=256)
                if t8 < 4:
                    t = t8
                    for sc in range(2):
                        dch = t * 2 + sc
                        b = group([(wv[:, k, sc * 128:(sc + 1) * 128], memn[:, k, :], [t_slot[s], t_s2])
                                   for k in range(16)], nout=MEM)
                        E.op("act", lambda e, dch=dch, b=b: e.activation(out=kT[:, dch, :], in_=ps[:, b, 0:MEM], func=AF.Copy),
                             rd=[t_bank[b]], wr=[t_kv])
                else:
                    t = t8 - 4
                    for mc in range(2):
                        b = group([(memn[:, k, mc * 128:(mc + 1) * 128], wv[:, k, :], [t_slot[s], t_s2])
                                   for k in range(16)], nout=256)
                        E.op("act", lambda e, t=t, mc=mc, b=b: e.activation(out=vv[:, mc, t * 256:(t + 1) * 256],
                                                                         in_=ps[:, b, 0:256], func=AF.Copy),
                             rd=[t_bank[b]], wr=[t_kv])
                rel(1)

        final_evs = []
        for blk in range(nblk):
            base_i = len(tile_state["order"])
            tile_state["order"].extend(list(range(n_layers * TPL)))
            prefetch(rr["done"] + NSLOT)
            E.dma("sp", "xin", xs[:], xT[:, :, blk * NT:(blk + 1) * NT], wr=t_xs_all)
            for l in range(n_layers):
                cur["lo"] = LO[l + DEPTH - n_layers] if (blk == 0 and TRIM) else 0
                if l == 0:
                    kv_norm(0)
                kv_tiles(0, KV_SPLIT)
                rms_pre(l, C_GPRE)
                kv_tiles(KV_SPLIT, 8)
                hrhs = lambda k, h: hb[:, k, H(h)]
                hbtk = lambda k, h: [thb[k][h]]
                for t, s in enumerate(tiles(4)):
                    for sc in range(2):
                        c = t * 2 + sc
                        std_unit(s, sc, 16, hrhs, hbtk, lambda h, b, c=c: E.op(
                            "act", lambda e: e.activation(out=s2[:, c, H(h)], in_=ps[:, b, P(h)], func=AF.Copy),
                            rd=[t_bank[b]], wr=[t_s2]))
                E.op("dve", lambda e, l=l: e.tensor_copy(out=s1[:, :, 30:32], in_=uhist[:, l, :, :]),
                     rd=[t_hist], wr=[t_s1], strict=True)
                for t, s in enumerate(tiles(4)):
                    for sc in range(2):
                        c = t * 2 + sc
                        std_unit(s, sc, 16, hrhs, hbtk, lambda h, b, c=c: E.op(
                            "dve", lambda e: e.tensor_tensor(out=s1[:, c, 32 + H(h).start:32 + H(h).stop], in0=ps[:, b, P(h)],
                                                             in1=s2[:, c, H(h)], op=ALU.mult),
                            rd=[t_bank[b], t_s2], wr=[t_s1]))
                E.op("dve", lambda e, l=l: e.tensor_copy(out=uhist[:, l, :, :], in_=s1[:, :, 32 + NT - 2:32 + NT]),
                     rd=[t_s1], wr=[t_hist], strict=True)
                for c in range(8):
                    for h in range(2):
                        E.op("dve", lambda e, c=c, h=h, w_ap=cs(l, C_CBW + c * 3 + 0): e.tensor_scalar(
                            out=s2[:, c, H(h)], in0=s1[:, c, 30 + H(h).start:30 + H(h).stop],
                            scalar1=w_ap, scalar2=None, op0=ALU.mult),
                            rd=[t_s1, t_cst], wr=[t_s2])
                        for k in (1, 2):
                            E.op("dve", lambda e, c=c, h=h, k=k, w_ap=cs(l, C_CBW + c * 3 + k): e.scalar_tensor_tensor(
                                out=s2[:, c, H(h)], in0=s1[:, c, 30 + k + H(h).start:30 + k + H(h).stop],
                                scalar=w_ap, in1=s2[:, c, H(h)], op0=ALU.mult, op1=ALU.add),
                                rd=[t_s1, t_cst], wr=[t_s2])
                for t, s in enumerate(tiles(4)):
                    for sc in range(2):
                        c = t * 2 + sc
                        std_unit(s, sc, 16, hrhs, hbtk, lambda h, b, c=c: E.op(
                            "dve", lambda e: e.tensor_tensor(out=bact[:, c, H(h)], in0=ps[:, b, P(h)],
                                                             in1=s2[:, c, H(h)], op=ALU.mult),
                            rd=[t_bank[b], t_s2], wr=[t_bact]))
                for t, s in enumerate(tiles(4)):
                    for sc in range(2):
                        c = t * 2 + sc
                        std_unit(s, sc, 16, hrhs, hbtk, lambda h, b, c=c: E.op(
                            "act", lambda e: e.activation(out=s2[:, c, H(h)], in_=ps[:, b, P(h)], func=AF.Sigmoid),
                            rd=[t_bank[b]], wr=[t_s2]))
                E.op("dve", lambda e, l=l: e.tensor_copy(out=s1[:, :, 2:32], in_=ahist[:, l, :, :]),
                     rd=[t_hist], wr=[t_s1], strict=True)
                for t, s in enumerate(tiles(4)):
                    for sc in range(2):
                        c = t * 2 + sc
                        std_unit(s, sc, 16, hrhs, hbtk, lambda h, b, c=c: E.op(
                            "dve", lambda e: e.tensor_tensor(out=s1[:, c, 32 + H(h).start:32 + H(h).stop], in0=ps[:, b, P(h)],
                                                             in1=s2[:, c, H(h)], op=ALU.mult),
                            rd=[t_bank[b], t_s2], wr=[t_s1]))
                E.op("dve", lambda e, l=l: e.tensor_copy(out=ahist[:, l, :, :], in_=s1[:, :, 32 + NT - 30:32 + NT]),
                     rd=[t_s1], wr=[t_hist], strict=True)

                if cur["lo"] > 0:
                    cur["lo"] += 30
                NDC = 8 - POOL_CHUNKS
                t_s2p.w = dict(t_s2.w); t_s2p.r = dict(t_s2.r)

                def conv_ops(l=l):
                    ops = []
                    for c in range(NDC):
                        def first(c=c):
                            E.op("dve", lambda e: e.tensor_scalar(
                                out=s2[:, c, cur["lo"]:NT], in0=s1[:, c, 2 + cur["lo"]:2 + NT],
                                scalar1=cs(l, C_CAW + c * CK), scalar2=cs(l, C_CAB + c), op0=ALU.mult, op1=ALU.add),
                                rd=[t_s1, t_cst], wr=[t_s2])
                        ops.append(first)
                        for k in range(1, CK):
                            def tap(c=c, k=k):
                                E.op("dve", lambda e: e.scalar_tensor_tensor(
                                    out=s2[:, c, cur["lo"]:NT], in0=s1[:, c, 2 + k + cur["lo"]:2 + k + NT],
                                    scalar=cs(l, C_CAW + c * CK + k), in1=s2[:, c, cur["lo"]:NT], op0=ALU.mult, op1=ALU.add),
                                    rd=[t_s1, t_cst], wr=[t_s2])
                            ops.append(tap)
                    return ops

                def conv_ops_pool(l=l):
                    ops = []
                    for c in range(NDC, 8):
                        for h in range(2):
                            def first(c=c, h=h):
                                E.op("pool", lambda e: e.tensor_scalar(
                                    out=s2[:, c, H(h)], in0=s1[:, c, 2 + H(h).start:2 + H(h).stop],
                                    scalar1=cs(l, C_CAW + c * CK), scalar2=cs(l, C_CAB + c), op0=ALU.mult, op1=ALU.add),
                                    rd=[t_s1, t_cst], wr=[t_s2p])
                            ops.append(first)
                            for k in range(1, CK):
                                def tap(c=c, h=h, k=k):
                                    E.op("pool", lambda e: e.tensor_scalar(
                                        out=ptmp[:], in0=s1[:, c, 2 + k + H(h).start:2 + k + H(h).stop],
                                        scalar1=cs(l, C_CAW + c * CK + k), scalar2=None, op0=ALU.mult),
                                        rd=[t_s1, t_cst], wr=[t_ptmp])
                                    E.op("pool", lambda e: e.tensor_tensor(
                                        out=s2[:, c, H(h)], in0=s2[:, c, H(h)], in1=ptmp[:], op=ALU.add),
                                        rd=[t_ptmp], wr=[t_s2p])
                                ops.append(tap)
                    return ops

                def ln_silu(l=l):
                    bsum = [6, 7]
                    bsq = [nbank(), nbank()]
                    lag = []

                    def pe_flush(keep):
                        while len(lag) > keep:
                            (i, bnk, h, first, last) = lag.pop(0)
                            E.op("pe", lambda e, i=i, bnk=bnk, h=h, first=first, last=last: e.matmul(
                                ps[:, bnk, P(h)], lhsT=ones[:], rhs=sq[:, i, P(h)], start=first, stop=last),
                                rd=[t_ones, t_sq[i]], wr=[t_bank[bnk]] if first else [], sig=True)
                            t_bank[bnk].w = {"pe": E.cnt["pe"]}
                    for h in range(2):
                        for c in range(8):
                            i = nsq()
                            E.op("act", lambda e, c=c, i=i, h=h: e.activation(out=sq[:, i, P(h)], in_=s2[:, c, H(h)], func=AF.Copy),
                                 rd=[t_s2, t_s2p], wr=[t_sq[i]])
                            lag.append((i, bsum[h], h, c == 0, c == 7))
                            j = nsq()
                            E.op("act", lambda e, c=c, j=j, h=h: e.activation(out=sq[:, j, P(h)], in_=s2[:, c, H(h)], func=AF.Square),
                                 rd=[t_s2, t_s2p], wr=[t_sq[j]])
                            lag.append((j, bsq[h], h, c == 0, c == 7))
                            pe_flush(4)
                    pe_flush(0)
                    for h in range(2):
                        E.op("dve", lambda e, h=h: e.tensor_scalar(
                            out=st_b[:, h, P(h)], in0=ps[:, bsum[h], P(h)], scalar1=1.0 / 1024, scalar2=None, op0=ALU.mult),
                            rd=[t_bank[bsum[h]]], wr=[t_stb[h]])
                        E.op("dve", lambda e, h=h: e.tensor_tensor(out=st_a[:, h, P(h)], in0=st_b[:, h, P(h)], in1=st_b[:, h, P(h)], op=ALU.mult),
                             rd=[t_stb[h]], wr=[t_sta[h]])
                        E.op("dve", lambda e, h=h: e.scalar_tensor_tensor(
                            out=st_c[:, h, P(h)], in0=ps[:, bsq[h], P(h)], scalar=1.0 / 1024, in1=st_a[:, h, P(h)],
                            op0=ALU.mult, op1=ALU.subtract), rd=[t_bank[bsq[h]], t_sta[h]], wr=[t_stc[h]])
                        E.op("act", lambda e, h=h: e.activation(out=st_c[:, h, P(h)], in_=st_c[:, h, P(h)], func=AF.Ln,
                                                             bias=cst_eps[:], scale=1.0), rd=[t_stc[h], t_cst], wr=[t_stc[h]])
                        E.op("act", lambda e, h=h: e.activation(out=st_c[:, h, P(h)], in_=st_c[:, h, P(h)], func=AF.Exp, scale=-0.5),
                             rd=[t_stc[h]], wr=[t_stc[h]])
                    for h in range(2):
                        for c in range(8):
                            E.op("dve", lambda e, c=c, h=h: e.tensor_tensor(out=s2[:, c, H(h)], in0=s2[:, c, H(h)],
                                                                          in1=st_b[:, h, P(h)], op=ALU.subtract),
                                 rd=[t_stb[h], t_s2p], wr=[t_s2])
                            E.op("dve", lambda e, c=c, h=h: e.tensor_tensor(out=s2[:, c, H(h)], in0=s2[:, c, H(h)],
                                                                          in1=st_c[:, h, P(h)], op=ALU.mult),
                                 rd=[t_stc[h]], wr=[t_s2])
                            E.op("act", lambda e, c=c, h=h: e.activation(out=aact[:, c, H(h)], in_=s2[:, c, H(h)], func=AF.Silu,
                                                                       bias=cs(l, C_LNB + c), scale=cs(l, C_LNG + c)),
                                 rd=[t_s2, t_cst], wr=[t_aact])
                    for k_, v_ in list(t_s2p.r.items()) + list(t_s2p.w.items()):
                        if t_s2.r.get(k_, 0) < v_:
                            t_s2.r[k_] = v_

                cops = conv_ops()
                pops = conv_ops_pool()
                if bg_conv:
                    E.bg["dve"] = cops
                    E.bg_rate["dve"] = BG_RATE
                    E.bg["pool"] = pops
                    E.bg_rate["pool"] = 14
                else:
                    for f in pops:
                        f()
                    for f in cops:
                        f()
                    ln_silu()

                for t, s in enumerate(tiles(4)):
                    for sc in range(2):
                        c = t * 2 + sc
                        std_unit(s, sc, 16, hrhs, hbtk, lambda h, b, c=c: E.op(
                            "act", lambda e: e.activation(out=qo[:, c, H(h)], in_=ps[:, b, P(h)], func=AF.Copy),
                            rd=[t_bank[b]], wr=[t_qo[c // 2]]))
                        if bg_conv:
                            E.drain_bg("dve", Q_DRAIN)
                units = [(hd, h) for hd in range(4) for h in range(2)]
                upts = {}

                def att_a(u):
                    hd, h = units[u]
                    pts = []
                    for mc in range(2):
                        b = group([(kT[:, hd * 2 + dc, mc * 128:(mc + 1) * 128], qo[:, hd * 2 + dc, H(h)], [t_kv, t_qo[hd]])
                                   for dc in range(2)], h=h)
                        i = nsq()
                        E.op("act", lambda e, i=i, b=b: e.activation(out=sq[:, i, P(h)], in_=ps[:, b, P(h)], func=AF.Exp,
                                                                   scale=1.0 / 16.0),
                             rd=[t_bank[b]], wr=[t_sq[i]])
                        pts.append(i)
                    upts[u] = pts

                def att_b(u):
                    hd, h = units[u]
                    pts = upts[u]
                    bden = colsum([(sq[:, i, P(h)], [t_sq[i]]) for i in pts], 6 + h, h=h)
                    E.op("act", lambda e, h=h, bden=bden: e.activation(out=st_b[:, h, P(h)], in_=ps[:, bden, P(h)], func=AF.Ln),
                         rd=[t_bank[bden]], wr=[t_stb[h]])
                    E.op("act", lambda e, h=h: e.activation(out=st_b[:, h, P(h)], in_=st_b[:, h, P(h)], func=AF.Exp, scale=-1.0),
                         rd=[t_stb[h]], wr=[t_stb[h]])
                    for dc in range(2):
                        b = group([(vv[:, mc, (hd * 2 + dc) * 128:(hd * 2 + dc + 1) * 128], sq[:, pts[mc], P(h)], [t_kv, t_sq[pts[mc]]])
                                   for mc in range(2)], h=h)
                        E.op("dve", lambda e, hd=hd, dc=dc, h=h, b=b: e.tensor_tensor(
                            out=qo[:, hd * 2 + dc, H(h)], in0=ps[:, b, P(h)], in1=st_b[:, h, P(h)], op=ALU.mult),
                            rd=[t_bank[b], t_stb[h]], wr=[t_qo[hd]])

                att_a(0)
                for u in range(8):
                    if u + 1 < 8:
                        att_a(u + 1)
                    att_b(u)
                for jp in range(8):
                    T = {}
                    for br in (1, 2):
                        s = next_tile()
                        wg = ring[:, s, :].rearrange("p (k c) -> p k c", c=256)
                        for sc in range(2):
                            cs_ = slice(sc * 128, (sc + 1) * 128)
                            for h in range(2):
                                bg_ = group([(wg[:, k, cs_], hb[:, k, H(h)], [t_slot[s], thb[k][h]]) for k in range(16)], h=h)
                                i1 = ntmp()
                                E.op("act", lambda e, i1=i1, bg_=bg_: e.activation(out=tmpf[:, i1, P(h)], in_=ps[:, bg_, P(h)], func=AF.Sigmoid),
                                     rd=[t_bank[bg_]], wr=[t_tmp[i1]])
                                T[br, sc, h] = i1
                                if bg_conv:
                                    E.drain_bg("dve", G_DRAIN)
                        rel(1)
                    s = next_tile()
                    wbx = ring[:, s, :].rearrange("p (b k c) -> p b k c", b=2, c=256)
                    for sc in range(2):
                        j = jp * 2 + sc
                        cs_ = slice(sc * 128, (sc + 1) * 128)
                        for h in range(2):
                            i1 = T[1, sc, h]; i2 = T[2, sc, h]
                            byb = group([(wbx[:, 0, k, cs_], bact[:, k, H(h)], [t_slot[s], t_bact]) for k in range(8)], h=h)
                            E.op("dve", lambda e, i1=i1, byb=byb: e.tensor_tensor(out=tmpf[:, i1, P(h)], in0=ps[:, byb, P(h)],
                                                                                in1=tmpf[:, i1, P(h)], op=ALU.mult),
                                 rd=[t_bank[byb], t_tmp[i1]], wr=[t_tmp[i1]])
                            byx = group([(wbx[:, 1, k, cs_], qo[:, k, H(h)], [t_slot[s]] + t_qo) for k in range(8)], h=h)
                            E.op("dve", lambda e, i2=i2, byx=byx: e.tensor_tensor(out=tmpf[:, i2, P(h)], in0=ps[:, byx, P(h)],
                                                                                in1=tmpf[:, i2, P(h)], op=ALU.mult),
                                 rd=[t_bank[byx], t_tmp[i2]], wr=[t_tmp[i2]])
                            E.op("dve", lambda e, i1=i1, i2=i2, j=j, h=h: e.tensor_tensor(
                                out=merged[:, j, H(h)], in0=tmpf[:, i1, P(h)], in1=tmpf[:, i2, P(h)], op=ALU.add),
                                rd=[t_tmp[i1], t_tmp[i2]], wr=[t_mrg])
                    rel(1)
                    if bg_conv and jp == LN_AT:
                        E.drain_bg("pool")
                        E.drain_bg("dve")
                        ln_silu()
                for jq in range(4):
                    T = {}
                    for half4 in range(2):
                        s = next_tile()
                        wg = ring[:, s, :].rearrange("p (k c) -> p k c", c=256)
                        for sc in range(2):
                            q4 = half4 * 2 + sc
                            cg = slice(sc * 128, (sc + 1) * 128)
                            for h in range(2):
                                bg0 = group([(wg[:, k, cg], hb[:, k, H(h)], [t_slot[s], thb[k][h]]) for k in range(16)], h=h)
                                i1 = ntmp()
                                E.op("act", lambda e, i1=i1, bg0=bg0: e.activation(out=tmpf[:, i1, P(h)], in_=ps[:, bg0, P(h)], func=AF.Sigmoid),
                                     rd=[t_bank[bg0]], wr=[t_tmp[i1]])
                                T[q4, h] = i1
                        rel(1)
                    if bg_conv and jq == 0 and LN_AT < 0:
                        E.drain_bg("pool")
                        E.drain_bg("dve")
                        ln_silu()
                    s = next_tile()
                    wa = ring[:, s, :].rearrange("p (k c) -> p k c", c=512)
                    for q4 in range(4):
                        j = jq * 4 + q4
                        ca = slice(q4 * 128, (q4 + 1) * 128)
                        for h in range(2):
                            i1 = T[q4, h]
                            bya = group([(wa[:, k, ca], aact[:, k, H(h)], [t_slot[s], t_aact]) for k in range(8)], h=h)
                            E.op("dve", lambda e, i1=i1, bya=bya: e.tensor_tensor(out=tmpf[:, i1, P(h)], in0=ps[:, bya, P(h)],
                                                                                in1=tmpf[:, i1, P(h)], op=ALU.mult),
                                 rd=[t_bank[bya], t_tmp[i1]], wr=[t_tmp[i1]])
                            E.op("dve", lambda e, i1=i1, j=j, h=h: e.tensor_tensor(
                                out=merged[:, j, H(h)], in0=merged[:, j, H(h)], in1=tmpf[:, i1, P(h)], op=ALU.add),
                                rd=[t_tmp[i1]], wr=[t_mrg])
                    rel(1)
                for k, v in list(t_s2.w.items()) + list(t_s2.r.items()):
                    if t_s1.r.get(k, 0) < v:
                        t_s1.r[k] = v
                pend = []
                for t, s in enumerate(tiles(8)):
                    for sc in range(2):
                        j = t * 2 + sc
                        std_unit(s, sc, 16, lambda k, h: merged[:, k, H(h)], [t_mrg],
                                 lambda h, b, j=j: zevac_with_stats(zm, t_s1, j, h, b, j == 0, j == 15, pend))
                        flush_stats(pend, keep=2)
                flush_stats(pend)
                post_norm_update(l, C_GPOST, zm, t_s1)
                t_s2.w = dict(t_s1.w); t_s2.r = dict(t_s1.r)
                ffn_stats = ffn_pre(l)
                for tk in (t_s1, t_s2, t_aact, t_bact, t_kv) + tuple(t_qo):
                    for k, v in list(tk.w.items()) + list(tk.r.items()):
                        if t_hid.r.get(k, 0) < v:
                            t_hid.r[k] = v
                for t, s in enumerate(tiles(32)):
                    for sc in range(2):
                        c = t * 2 + sc

                        def ev_up(h, b, c=c):
                            i = ntmp()
                            E.op("act", lambda e: e.activation(out=tmpf[:, i, P(h)], in_=ps[:, b, P(h)], func=AF.Relu),
                                 rd=[t_bank[b]], wr=[t_tmp[i]])
                            E.op("dve", lambda e: e.tensor_tensor(out=hid[:, c, H(h)], in0=tmpf[:, i, P(h)], in1=tmpf[:, i, P(h)], op=ALU.mult),
                                 rd=[t_tmp[i]], wr=[t_hid])
                        std_unit(s, sc, 16, hrhs, hbtk, ev_up)
                    if t >= 2:
                        for _ in range(2):
                            if ffn_stats:
                                ffn_stats.pop(0)()
                for tk in t_hb_all + [t_mrg]:
                    for k, v in list(tk.w.items()) + list(tk.r.items()):
                        if t_z.r.get(k, 0) < v:
                            t_z.r[k] = v
                pend = []
                for jc in range(16):
                    sl = [next_tile(), next_tile()]
                    for h in range(2):
                        mms = []
                        for kg in range(2):
                            wv = ring[:, sl[kg], :].rearrange("p (k c) -> p k c", c=128)
                            mms += [(wv[:, k, :], hid[:, kg * 32 + k, H(h)], [t_slot[sl[kg]], t_hid]) for k in range(32)]
                        b = group(mms, h=h)
                        zevac_with_stats(zf, t_z, jc, h, b, jc == 0, jc == 15, pend)
                    flush_stats(pend, keep=2)
                    rel(2)
                flush_stats(pend)
                for tk in (t_s1, t_s2, t_aact, t_bact, t_kv) + tuple(t_qo):
                    tk.w = dict(t_hid.w); tk.r = dict(t_hid.r)
                post_norm_rstd(eps_tile=True)
                if l + 1 < n_layers:
                    kv_norm(l + 1)
                post_norm_xupdate(l, C_GMPOST, zf, t_z)
                for tk in t_hb_all:
                    tk.w = dict(t_z.w); tk.r = dict(t_z.r)
                t_mrg.w = dict(t_z.w); t_mrg.r = dict(t_z.r)
                if blk == 0:
                    for c in range(16):
                        E.op("dve", lambda e, c=c: e.tensor_scalar(out=xs[:, c, 0:HALO], in0=xs[:, c, 0:HALO],
                                                                  scalar1=hmask[:, 0:1], scalar2=None, op0=ALU.mult),
                             rd=[t_cst], wr=txs[c])
            if blk == 0:
                ev = E.dma("sp", "out", outT[:, :, 0:NT - HALO], xs[:, :, HALO:NT], rd=t_xs_all)
            else:
                ev = E.dma("sp", "out", outT[:, :, blk * NT - HALO:(blk + 1) * NT - HALO], xs[:], rd=t_xs_all)
            final_evs.append(ev)
        E.finalize(st, final_evs)
    return nc


def _std(W, c0, ncols=256, k0=0, nk=16):
    blk = W[k0 * 128:(k0 + nk) * 128, c0:c0 + ncols]
    return blk.reshape(nk, 128, ncols).transpose(1, 0, 2).reshape(128, nk * ncols)


def pack_layer_tiles(out, w_in, w_a_out, w_b_out, w_kv, w_x_out, w_o, w_up, w_down):
    i = 0

    def put(a):
        nonlocal i
        out[i] = a
        i += 1
    for t in range(8):
        put(_std(w_kv, t * 256))
    for base in (3072, 4096, 2048, 1024, 0, 5120):
        for t in range(4):
            put(_std(w_in, base + t * 256))
    for jp in range(8):
        put(_std(w_in, 8192 + jp * 256))
        put(_std(w_in, 10240 + jp * 256))
        put(np.concatenate([_std(w_b_out, jp * 256, nk=8), _std(w_x_out, jp * 256, nk=8)], axis=1))
    for jq in range(4):
        put(_std(w_in, 6144 + (2 * jq) * 256))
        put(_std(w_in, 6144 + (2 * jq + 1) * 256))
        put(_std(w_a_out, jq * 512, ncols=512, nk=8))
    for t in range(8):
        put(_std(w_o, t * 256))
    for t in range(32):
        put(_std(w_up, t * 256))
    for jc in range(16):
        for kg in range(2):
            put(_std(w_down, jc * 128, ncols=128, k0=kg * 32, nk=32))
    assert i == TPL


def pack_consts(layers, g_mix_pre, g_mix_post, g_mlp_pre, g_mlp_post, g_mem, conv_a_w, conv_a_b, ln_a_g, ln_a_b, conv_b_w):
    cst = np.zeros((128, len(layers) * C_PER), np.float32)
    for li, l in enumerate(layers):
        o = li * C_PER
        for col, g in ((C_GPRE, g_mix_pre), (C_GPOST, g_mix_post), (C_GMPRE, g_mlp_pre), (C_GMPOST, g_mlp_post), (C_GMEM, g_mem)):
            cst[:, o + col:o + col + 16] = g[l].reshape(16, 128).T
        cst[:, o + C_CAW:o + C_CAW + 8 * CK] = conv_a_w[l].reshape(CK, 8, 128).transpose(2, 1, 0).reshape(128, 8 * CK)
        cst[:, o + C_CAB:o + C_CAB + 8] = conv_a_b[l].reshape(8, 128).T
        cst[:, o + C_LNG:o + C_LNG + 8] = ln_a_g[l].reshape(8, 128).T
        cst[:, o + C_LNB:o + C_LNB + 8] = ln_a_b[l].reshape(8, 128).T
        cst[:, o + C_CBW:o + C_CBW + 24] = conv_b_w[l].reshape(3, 8, 128).transpose(2, 1, 0).reshape(128, 24)
    return cst


def shard_x(x2d, core):
    lo = core * TOK - HALO
    if lo < 0:
        blk = np.concatenate([np.zeros((HALO, D), np.float32), x2d[0:TOK]], axis=0)
    else:
        blk = x2d[lo:lo + TOK + HALO]
    return np.ascontiguousarray(blk.T.reshape(16, 128, TOK + HALO).transpose(1, 0, 2))


_PROG_CACHE = {}


def _get_prog(n_layers):
    if n_layers not in _PROG_CACHE:
        _PROG_CACHE[n_layers] = build_program(n_layers)
    return _PROG_CACHE[n_layers]


FUSED = True


def kernel(x, mem, g_mix_pre, w_in, conv_a_w, conv_a_b, ln_a_g, ln_a_b, w_a_out, conv_b_w, w_b_out,
           g_mem, w_kv, w_x_out, w_o, g_mix_post, g_mlp_pre, w_up, w_down, g_mlp_post):
    f = lambda a: np.asarray(a, dtype=np.float32)
    x2d = f(x)[0]
    memT = np.ascontiguousarray(f(mem)[0].T.reshape(16, 128, MEM).transpose(1, 0, 2))
    groups = [list(range(DEPTH))] if FUSED else [[l] for l in range(DEPTH)]
    for layers in groups:
        nl = len(layers)
        wts = np.empty((nl * TPL, 128, TILE), np.float32)
        for li, l in enumerate(layers):
            pack_layer_tiles(wts[li * TPL:(li + 1) * TPL], f(w_in[l]), f(w_a_out[l]), f(w_b_out[l]), f(w_kv[l]),
                             f(w_x_out[l]), f(w_o[l]), f(w_up[l]), f(w_down[l]))
        cst = pack_consts(layers, f(g_mix_pre), f(g_mix_post), f(g_mlp_pre), f(g_mlp_post), f(g_mem),
                          f(conv_a_w), f(conv_a_b), f(ln_a_g), f(ln_a_b), f(conv_b_w))
        nc = _get_prog(nl)
        in_maps = []
        for c in range(NCORE):
            in_maps.append({"xT": shard_x(x2d, c), "wts": wts, "cst": cst, "memT": memT,
                            "hmask": np.full((128, 1), 0.0 if c == 0 else 1.0, np.float32)})
        res = run_bass_kernel_spmd(nc, in_maps, core_ids=list(range(NCORE)))
        outs = []
        for c in range(NCORE):
            o = res.results[c]["outT"]
            outs.append(o.transpose(2, 1, 0).reshape(TOK, D))
        x2d = np.concatenate(outs, axis=0)
    return np.ascontiguousarray(x2d[None]).astype(np.float32)
```

```python
import numpy as np
from contextlib import ExitStack
import concourse.bass as bass
import concourse.mybir as mybir
from concourse.bass_utils import run_bass_kernel_spmd

F32 = mybir.dt.float32
BF16 = mybir.dt.bfloat16
AF = mybir.ActivationFunctionType
ALU = mybir.AluOpType

D = 2048
SEQ = 8192
DEPTH = 4
NCORE = 8
TOK = SEQ // NCORE
HALO = 128
NT = 576
NH = 288
NBLK = 2
MEM = 256
CK = 31
EPS = 1e-6
TPL = 140
TILE = 4096
NSLOT = 4
NTMP = 8
POOL_CHUNKS = 0
LN_AT = 6
TRIM = True
BG_RATE = 1
BALANCE = False
FAST_RECIP = False
LNEXP = True
KV_SPLIT = 3
Q_DRAIN = 4
G_DRAIN = 2
EARLY_CONV = True
A_DRAIN = 4

C_GPRE, C_GPOST, C_GMPRE, C_GMPOST, C_GMEM = 0, 16, 32, 48, 64
C_CAW = 80
C_CAB = C_CAW + 8 * CK
C_LNG = C_CAB + 8
C_LNB = C_LNG + 8
C_CBW = C_LNB + 8
C_PER = C_CBW + 24

ENGS = ("pe", "act", "dve", "pool", "sp")


class Tk:
    __slots__ = ("w", "r")

    def __init__(self):
        self.w = {}
        self.r = {}


class _Rec:
    def __init__(self):
        self.call = None

    def __getattr__(self, name):
        def f(*a, **k):
            assert self.call is None
            self.call = (name, a, k)
        return f


def _eager(fn):
    r = _Rec()
    fn(r)
    name, a, k = r.call
    return lambda e: getattr(e, name)(*a, **k)


class Emitter:
    def __init__(self, nc, strict_same=False):
        self.nc = nc
        self.streams = {e: [] for e in ENGS}
        self.cnt = {e: 0 for e in ENGS}
        self.waited = {e: {} for e in ENGS}
        self.dma_cnt = {}
        self.strict_same = strict_same
        self.sems = {}
        self.bg = {"dve": [], "pool": []}
        self.bg_rate = {"dve": 0, "pool": 0}
        self.in_bg = False

    def _deps(self, rd, wr, extra):
        deps = {}
        for t in rd:
            for k, v in t.w.items():
                if deps.get(k, 0) < v:
                    deps[k] = v
        for t in wr:
            for d in (t.w, t.r):
                for k, v in d.items():
                    if deps.get(k, 0) < v:
                        deps[k] = v
        for d in extra:
            if d is None:
                continue
            for k, v in d.items():
                if deps.get(k, 0) < v:
                    deps[k] = v
        return deps

    def _waits(self, eng, deps, strict):
        waits = []
        for k, v in deps.items():
            if k == eng and not strict:
                continue
            if self.waited[eng].get(k, 0) >= v:
                continue
            self.waited[eng][k] = v
            waits.append((k, v))
        return waits

    def op(self, eng, fn, rd=(), wr=(), sig=True, extra=(), strict=None):
        strict = self.strict_same if strict is None else strict
        deps = self._deps(rd, wr, extra)
        waits = self._waits(eng, deps, strict)
        if sig:
            self.cnt[eng] += 1
            c = self.cnt[eng]
        else:
            c = self.cnt[eng] + 1
        self.streams[eng].append((waits, _eager(fn), 1 if sig else 0, eng))
        for t in rd:
            if t.r.get(eng, 0) < c:
                t.r[eng] = c
        for t in wr:
            t.w = {eng: c}
            t.r = {}
        ev = {eng: c}
        if eng == "dve" and self.bg["dve"] and not self.in_bg:
            self.drain_bg("dve", self.bg_rate["dve"])
        return ev

    def drain_bg(self, eng, n=None):
        self.in_bg = True
        k = 0
        q = self.bg[eng]
        while q and (n is None or k < n):
            q.pop(0)()
            k += 1
        self.in_bg = False

    def dma(self, q, slot, out, in_, rd=(), wr=(), extra=()):
        key = "dma:" + slot
        deps = self._deps(rd, wr, extra)
        waits = self._waits(q, deps, False)
        self.dma_cnt[key] = self.dma_cnt.get(key, 0) + 16
        c = self.dma_cnt[key]
        self.streams[q].append((waits, lambda e, o=out, i=in_: e.dma_start(out=o, in_=i), 16, key))
        for t in rd:
            if t.r.get(key, 0) < c:
                t.r[key] = c
        for t in wr:
            t.w = {key: c}
            t.r = {}
        if q == "pool" and self.bg["pool"] and not self.in_bg:
            self.drain_bg("pool", self.bg_rate["pool"])
        return {key: c}

    def finalize(self, st, final_waits):
        nc = self.nc
        keys = list(ENGS) + sorted(self.dma_cnt.keys())
        for k in keys:
            self.sems[k] = st.enter_context(nc.semaphore("s_" + k.replace(":", "_")))
        block = st.enter_context(nc.Block())
        fin = {}
        for d in final_waits:
            for k, v in d.items():
                fin[k] = max(fin.get(k, 0), v)

        def replay(eng_name):
            def run(e):
                for waits, fn, inc, semkey in self.streams[eng_name]:
                    for k, v in waits:
                        e.wait_ge(self.sems[k], v)
                    ins = fn(e)
                    if inc:
                        ins.then_inc(self.sems[semkey], inc)
                if eng_name == "sp":
                    for k, v in fin.items():
                        e.wait_ge(self.sems[k], v)
            return run

        block.tensor(replay("pe"))
        block.scalar(replay("act"))
        block.vector(replay("dve"))
        block.gpsimd(replay("pool"))
        block.sync(replay("sp"))


def build_program(n_layers, nblk=NBLK, bg_conv=True):
    nc = bass.Bass("TRN2", target_bir_lowering=False)
    xT = nc.dram_tensor("xT", [128, 16, nblk * NT], F32, kind="ExternalInput").ap()
    wts = nc.dram_tensor("wts", [n_layers * TPL, 128, TILE], F32, kind="ExternalInput").ap()
    cstd = nc.dram_tensor("cst", [128, n_layers * C_PER], F32, kind="ExternalInput").ap()
    memTd = nc.dram_tensor("memT", [128, 16, MEM], F32, kind="ExternalInput").ap()
    hmaskd = nc.dram_tensor("hmask", [128, 1], F32, kind="ExternalInput").ap()
    outT = nc.dram_tensor("outT", [128, 16, nblk * NT - HALO], F32, kind="ExternalOutput").ap()

    st = ExitStack()
    with st:
        def sb(name, shape, dt):
            return st.enter_context(nc.sbuf_tensor(name, shape, dt))

        cst = sb("cst_sb", [128, n_layers * C_PER], F32)
        hmask = sb("hmask_sb", [128, 1], F32)
        xs = sb("xs", [128, 16, NT], F32)
        ZA = sb("ZA", [128, 16 * NT], F32)
        HA = sb("HA", [128, 64 * NT], BF16)
        ring = sb("ring", [128, NSLOT, TILE], BF16)
        ones = sb("ones", [128, 128], BF16)
        sq = sb("sq", [128, 8, NH], BF16)
        st_a = sb("st_a", [128, 2, NH], F32)
        st_b = sb("st_b", [128, 2, NH], F32)
        st_c = sb("st_c", [128, 2, NH], F32)
        tmpf = sb("tmpf", [128, NTMP, NH], F32)
        ptmp = sb("ptmp", [128, NH], F32) if POOL_CHUNKS > 0 else None
        ahist = sb("ahist", [128, n_layers, 8, 30], F32)
        uhist = sb("uhist", [128, n_layers, 8, 2], F32)
        ps = st.enter_context(nc.psum_tensor("ps", [128, 8, 512], F32))

        ZAb = ZA[:].bitcast(BF16)
        hb = ZAb[:, 0:16 * NT].rearrange("p (c n) -> p c n", c=16)
        merged = ZAb[:, 16 * NT:32 * NT].rearrange("p (c n) -> p c n", c=16)
        zf = ZA[:].rearrange("p (c n) -> p c n", c=16)
        HAf = HA[:].bitcast(F32)
        o_s1 = 0
        n_s1 = 8 * (NT + 32)
        o_s2 = n_s1
        n_s2 = 8 * NT
        s1 = HAf[:, o_s1:o_s1 + n_s1].rearrange("p (c n) -> p c n", c=8)
        s2 = HAf[:, o_s2:o_s2 + n_s2].rearrange("p (c n) -> p c n", c=8)
        zm = HAf[:, 0:16 * NT].rearrange("p (c n) -> p c n", c=16)
        ob16 = 2 * (n_s1 + n_s2)
        aact = HA[:, ob16:ob16 + 8 * NT].rearrange("p (c n) -> p c n", c=8)
        bact = HA[:, ob16 + 8 * NT:ob16 + 16 * NT].rearrange("p (c n) -> p c n", c=8)
        qo = HA[:, ob16 + 16 * NT:ob16 + 24 * NT].rearrange("p (c n) -> p c n", c=8)
        okv = ob16 + 24 * NT
        kT = HA[:, okv:okv + 8 * MEM].rearrange("p (c n) -> p c n", c=8)
        vv = HA[:, okv + 8 * MEM:okv + 16 * MEM].rearrange("p (c n) -> p c n", c=2)
        assert okv + 16 * MEM <= 64 * NT
        hid = HA[:].rearrange("p (c n) -> p c n", c=64)
        memf = HAf[:, 0:16 * MEM].rearrange("p (c n) -> p c n", c=16)
        memn = HA[:, 2 * o_s2:2 * o_s2 + 16 * MEM].rearrange("p (c n) -> p c n", c=16)

        E = Emitter(nc)
        t_cst = Tk(); txs = [[Tk(), Tk()] for _ in range(16)]; t_xs_all = [t for p in txs for t in p]; thb = [[Tk(), Tk()] for _ in range(16)]; t_hb_all = [t for p in thb for t in p]; t_mrg = Tk(); t_z = Tk()
        t_s1 = Tk(); t_s2 = Tk(); t_aact = Tk(); t_bact = Tk(); t_qo = [Tk() for _ in range(4)]
        t_kv = Tk(); t_hid = Tk(); t_mem = Tk(); t_memn = Tk()
        t_slot = [Tk() for _ in range(NSLOT)]
        t_bank = [Tk() for _ in range(8)]
        t_sq = [Tk() for _ in range(8)]
        t_sta = [Tk(), Tk()]; t_stb = [Tk(), Tk()]; t_stc = [Tk(), Tk()]
        t_tmp = [Tk() for _ in range(NTMP)]
        t_s2p = Tk(); t_ptmp = Tk()
        t_hist = Tk(); t_ones = Tk()
        rr = {"bank": 0, "sq": 0, "tmp": 0, "pt": 0, "tile": 0, "done": 0}

        cur = {"lo": 0}
        LO = [8, 38, 68, 98]

        def MID():
            lo = cur["lo"]
            if lo == 0 or not BALANCE:
                return NH
            return lo + ((NT - lo) // 4) * 2 + ((NT - lo) % 4 > 0) * 2 if False else max(NH, ((lo + NT) // 4) * 2)

        def H(h):
            return slice(cur["lo"], MID()) if h == 0 else slice(MID(), NT)

        def P(h):
            sl = H(h)
            return slice(0, sl.stop - sl.start)

        def cs(l, col):
            c0 = l * C_PER + col
            return cst[:, c0:c0 + 1]

        def nbank():
            b = rr["bank"]
            rr["bank"] = (b + 1) % 6
            return b

        def nsq():
            i = rr["sq"]; rr["sq"] = (i + 1) % 8
            return i

        def ntmp():
            i = rr["tmp"]; rr["tmp"] = (i + 1) % NTMP
            return i


        E.dma("sp", "cst", cst[:], cstd[:], wr=[t_cst])
        E.dma("sp", "hmask", hmask[:], hmaskd[:], wr=[t_cst])
        E.op("dve", lambda e: e.memset(ones[:], 1.0), wr=[t_ones])
        E.op("dve", lambda e: e.memset(HAf[:, 0:n_s1 + n_s2], 0.0), wr=[t_s1, t_s2])
        E.op("dve", lambda e: e.memset(ahist[:], 0.0), wr=[t_hist])
        E.op("dve", lambda e: e.memset(uhist[:], 0.0), wr=[t_hist])

        tile_state = {"next_load": 0, "order": []}

        def prefetch(upto):
            while tile_state["next_load"] < min(upto, len(tile_state["order"])):
                i = tile_state["next_load"]
                s = i % NSLOT
                E.dma("pool", "w%d" % s, ring[:, s, :], wts[tile_state["order"][i]], wr=[t_slot[s]])
                tile_state["next_load"] += 1

        def next_tile():
            i = rr["tile"]
            rr["tile"] += 1
            assert i - rr["done"] < NSLOT
            prefetch(i + 1)
            return i % NSLOT

        def tiles(n):
            for _ in range(n):
                s_ = next_tile()
                yield s_
                rel(1)

        def rel(n=1):
            rr["done"] += n
            prefetch(rr["done"] + NSLOT)

        def group(mms, h=None, nout=None, bank=None):
            b = nbank() if bank is None else bank
            n = len(mms)
            psl = P(h) if h is not None else slice(0, nout)
            for i, (l_ap, r_ap, rdt) in enumerate(mms):
                E.op("pe", lambda e, l_ap=l_ap, r_ap=r_ap, i=i, b=b: e.matmul(
                    ps[:, b, psl], lhsT=l_ap, rhs=r_ap, start=(i == 0), stop=(i == n - 1)),
                    rd=rdt, wr=[t_bank[b]] if i == 0 else [], sig=(i == n - 1))
            t_bank[b].w = {"pe": E.cnt["pe"]}
            return b

        def colsum(srcs, bank, h=None, nout=None):
            return group([(ones[:], s_ap, [t_ones] + tk) for s_ap, tk in srcs], h=h, nout=nout, bank=bank)

        def rstd_from_bank(bank, h, dst, t_dst, inv_n, nout=None):
            sl = P(h) if nout is None else slice(0, nout)
            if LNEXP:
                E.op("act", lambda e: e.activation(out=dst[:, h, sl], in_=ps[:, bank, sl], func=AF.Ln,
                                                   bias=cst_eps[:], scale=inv_n),
                     rd=[t_bank[bank], t_cst], wr=[t_dst])
                E.op("act", lambda e: e.activation(out=dst[:, h, sl], in_=dst[:, h, sl], func=AF.Exp, scale=-0.5),
                     rd=[t_dst], wr=[t_dst])
                return
            E.op("act", lambda e: e.activation(out=dst[:, h, sl], in_=ps[:, bank, sl], func=AF.Sqrt,
                                               bias=cst_eps[:], scale=inv_n),
                 rd=[t_bank[bank], t_cst], wr=[t_dst])
            E.op("dve", lambda e: e.reciprocal(out=dst[:, h, sl], in_=dst[:, h, sl]), rd=[t_dst], wr=[t_dst])

        st_mem = sb("st_mem", [128, 1, MEM], F32)
        t_stmem = Tk()
        cst_eps = sb("cst_eps", [128, 1], F32)
        E.op("dve", lambda e: e.memset(cst_eps[:], EPS), wr=[t_cst])

        def rms_pre(l, gcol):
            for h in range(2):
                b = 6 + h
                for c in range(16):
                    i = nsq()
                    E.op("act", lambda e, c=c, i=i, h=h: e.activation(out=sq[:, i, P(h)], in_=xs[:, c, H(h)], func=AF.Square),
                         rd=[txs[c][h]], wr=[t_sq[i]])
                    E.op("pe", lambda e, i=i, c=c, b=b: e.matmul(
                        ps[:, b, P(h)], lhsT=ones[:], rhs=sq[:, i, P(h)], start=(c == 0), stop=(c == 15)),
                        rd=[t_ones, t_sq[i]], wr=[t_bank[b]] if c == 0 else [], sig=True)
                t_bank[b].w = {"pe": E.cnt["pe"]}
                rstd_from_bank(b, h, st_a, t_sta[h], 1.0 / D)
                for c in range(16):
                    E.op("dve", lambda e, c=c, h=h: e.scalar_tensor_tensor(
                        out=hb[:, c, H(h)], in0=xs[:, c, H(h)], scalar=cs(l, gcol + c), in1=st_a[:, h, P(h)],
                        op0=ALU.mult, op1=ALU.mult), rd=[txs[c][h], t_sta[h], t_cst], wr=[thb[c][h]])

        def std_unit(slot, sc, nk, rhs_fn, rhs_tk, evac, ncols=256, koff=0):
            wv = ring[:, slot, :].rearrange("p (k c) -> p k c", c=ncols)
            for h in range(2):
                b = group([(wv[:, koff + k, sc * 128:(sc + 1) * 128], rhs_fn(k, h),
                            [t_slot[slot]] + (rhs_tk(k, h) if callable(rhs_tk) else rhs_tk))
                           for k in range(nk)], h=h)
                evac(h, b)

        def post_norm_rstd(eps_tile=False):
            for h in range(2):
                if eps_tile:
                    b = 6 + h
                    E.op("dve", lambda e, h=h, b=b: e.scalar_tensor_tensor(
                        out=st_a[:, h, P(h)], in0=ps[:, b, P(h)], scalar=1.0 / D, in1=st_b[:, h, P(h)],
                        op0=ALU.mult, op1=ALU.add), rd=[t_bank[b], t_stb[h]], wr=[t_sta[h]])
                    E.op("act", lambda e, h=h: e.activation(out=st_a[:, h, P(h)], in_=st_a[:, h, P(h)], func=AF.Ln),
                         rd=[t_sta[h]], wr=[t_sta[h]])
                    E.op("act", lambda e, h=h: e.activation(out=st_a[:, h, P(h)], in_=st_a[:, h, P(h)], func=AF.Exp, scale=-0.5),
                         rd=[t_sta[h]], wr=[t_sta[h]])
                else:
                    rstd_from_bank(6 + h, h, st_a, t_sta[h], 1.0 / D)

        def post_norm_xupdate(l, gcol, z, t_zz):
            for h in range(2):
                for c in range(16):
                    i = ntmp()
                    E.op("dve", lambda e, c=c, h=h, i=i: e.scalar_tensor_tensor(
                        out=tmpf[:, i, P(h)], in0=z[:, c, H(h)], scalar=cs(l, gcol + c), in1=st_a[:, h, P(h)],
                        op0=ALU.mult, op1=ALU.mult), rd=[t_zz, t_sta[h], t_cst], wr=[t_tmp[i]])
                    E.op("dve", lambda e, c=c, h=h, i=i: e.tensor_tensor(
                        out=xs[:, c, H(h)], in0=xs[:, c, H(h)], in1=tmpf[:, i, P(h)], op=ALU.add),
                        rd=[t_tmp[i]], wr=[txs[c][h]])

        def post_norm_update(l, gcol, z, t_zz, eps_tile=False):
            post_norm_rstd(eps_tile)
            post_norm_xupdate(l, gcol, z, t_zz)

        def ffn_pre(l):
            for h in range(2):
                for c in range(16):
                    E.op("act", lambda e, c=c, h=h: e.activation(out=hb[:, c, H(h)], in_=xs[:, c, H(h)], func=AF.Copy,
                                                               scale=cs(l, C_GMPRE + c)),
                         rd=[txs[c][h], t_cst], wr=[thb[c][h]])
            th = []
            lag = []

            def pe_one():
                (i, c, h) = lag.pop(0)
                b = 6 + h
                E.op("pe", lambda e: e.matmul(ps[:, b, P(h)], lhsT=ones[:], rhs=sq[:, i, P(h)], start=(c == 0), stop=(c == 15)),
                     rd=[t_ones, t_sq[i]], wr=[t_bank[b]] if c == 0 else [], sig=True)
                t_bank[b].w = {"pe": E.cnt["pe"]}
                if c == 15:
                    E.op("dve", lambda e: e.tensor_scalar(out=st_b[:, h, P(h)], in0=ps[:, b, P(h)], scalar1=1.0 / D,
                                                          scalar2=EPS, op0=ALU.mult, op1=ALU.add),
                         rd=[t_bank[b]], wr=[t_stb[h]])
                    E.op("dve", lambda e: e.tensor_tensor(out=st_b[:, h, P(h)], in0=st_b[:, h, P(h)], in1=st_b[:, h, P(h)], op=ALU.mult),
                         rd=[t_stb[h]], wr=[t_stb[h]])
                    E.op("dve", lambda e: e.tensor_scalar(out=st_b[:, h, P(h)], in0=st_b[:, h, P(h)], scalar1=EPS,
                                                          scalar2=None, op0=ALU.mult),
                         rd=[t_stb[h]], wr=[t_stb[h]])
            for h in range(2):
                for c in range(16):
                    def f(c=c, h=h):
                        i = nsq()
                        E.op("act", lambda e: e.activation(out=sq[:, i, P(h)], in_=xs[:, c, H(h)], func=AF.Square),
                             rd=[txs[c][h]], wr=[t_sq[i]])
                        lag.append((i, c, h))
                        if len(lag) > 3:
                            pe_one()
                    th.append(f)

            def fin():
                while lag:
                    pe_one()
            th.append(fin)
            return th

        def zevac_with_stats(z, t_zz, j, h, b, first, last, pend):
            E.op("act", lambda e: e.activation(out=z[:, j, H(h)], in_=ps[:, b, P(h)], func=AF.Copy),
                 rd=[t_bank[b]], wr=[t_zz])
            i = nsq()
            E.op("act", lambda e: e.activation(out=sq[:, i, P(h)], in_=ps[:, b, P(h)], func=AF.Square),
                 rd=[t_bank[b]], wr=[t_sq[i]])
            pend.append((i, h, first, last))

        def flush_stats(pend, keep=0):
            while len(pend) > keep:
                i, h, first, last = pend.pop(0)
                b = 6 + h
                E.op("pe", lambda e, i=i, b=b, first=first, last=last: e.matmul(
                    ps[:, b, P(h)], lhsT=ones[:], rhs=sq[:, i, P(h)], start=first, stop=last),
                    rd=[t_ones, t_sq[i]], wr=[t_bank[b]] if first else [], sig=True)
                t_bank[b].w = {"pe": E.cnt["pe"]}

        kv_state = {"have_rstd": False}

        def kv_norm(l):
            E.dma("sp", "mem", memf[:], memTd[:], wr=[t_s1])
            if not kv_state["have_rstd"]:
                kv_state["have_rstd"] = True
                b = nbank()
                for c in range(16):
                    i = nsq()
                    E.op("act", lambda e, c=c, i=i: e.activation(out=sq[:, i, 0:MEM], in_=memf[:, c, :], func=AF.Square),
                         rd=[t_s1], wr=[t_sq[i]])
                    E.op("pe", lambda e, i=i, c=c, b=b: e.matmul(
                        ps[:, b, 0:MEM], lhsT=ones[:], rhs=sq[:, i, 0:MEM], start=(c == 0), stop=(c == 15)),
                        rd=[t_ones, t_sq[i]], wr=[t_bank[b]] if c == 0 else [], sig=True)
                t_bank[b].w = {"pe": E.cnt["pe"]}
                rstd_from_bank(b, 0, st_mem, t_stmem, 1.0 / D, nout=MEM)
            for c in range(16):
                E.op("dve", lambda e, c=c, g_ap=cs(l, C_GMEM + c): e.scalar_tensor_tensor(
                    out=memn[:, c, :], in0=memf[:, c, :], scalar=g_ap, in1=st_mem[:, 0, 0:MEM],
                    op0=ALU.mult, op1=ALU.mult), rd=[t_s1, t_stmem, t_cst], wr=[t_s2])

        def kv_tiles(t0, t1):
            for t8 in range(t0, t1):
                s = next_tile()
                wv = ring[:, s, :].rearrange("p (k c) -> p k c", c=256)
                if t8 < 4:
                    t = t8
                    for sc in range(2):
                        dch = t * 2 + sc
                        b = group([(wv[:, k, sc * 128:(sc + 1) * 128], memn[:, k, :], [t_slot[s], t_s2])
                                   for k in range(16)], nout=MEM)
                        E.op("act", lambda e, dch=dch, b=b: e.activation(out=kT[:, dch, :], in_=ps[:, b, 0:MEM], func=AF.Copy),
                             rd=[t_bank[b]], wr=[t_kv])
                else:
                    t = t8 - 4
                    for mc in range(2):
                        b = group([(memn[:, k, mc * 128:(mc + 1) * 128], wv[:, k, :], [t_slot[s], t_s2])
                                   for k in range(16)], nout=256)
                        E.op("act", lambda e, t=t, mc=mc, b=b: e.activation(out=vv[:, mc, t * 256:(t + 1) * 256],
                                                                         in_=ps[:, b, 0:256], func=AF.Copy),
                             rd=[t_bank[b]], wr=[t_kv])
                rel(1)

        final_evs = []
        for blk in range(nblk):
            base_i = len(tile_state["order"])
            tile_state["order"].extend(list(range(n_layers * TPL)))
            prefetch(rr["done"] + NSLOT)
            E.dma("sp", "xin", xs[:], xT[:, :, blk * NT:(blk + 1) * NT], wr=t_xs_all)
            for l in range(n_layers):
                cur["lo"] = LO[l + DEPTH - n_layers] if (blk == 0 and TRIM) else 0
                if l == 0:
                    kv_norm(0)
                kv_tiles(0, KV_SPLIT)
                rms_pre(l, C_GPRE)
                kv_tiles(KV_SPLIT, 8)
                hrhs = lambda k, h: hb[:, k, H(h)]
                hbtk = lambda k, h: [thb[k][h]]
                for t, s in enumerate(tiles(4)):
                    for sc in range(2):
                        c = t * 2 + sc
                        std_unit(s, sc, 16, hrhs, hbtk, lambda h, b, c=c: E.op(
                            "act", lambda e: e.activation(out=s2[:, c, H(h)], in_=ps[:, b, P(h)], func=AF.Copy),
                            rd=[t_bank[b]], wr=[t_s2]))
                E.op("dve", lambda e, l=l: e.tensor_copy(out=s1[:, :, 30:32], in_=uhist[:, l, :, :]),
                     rd=[t_hist], wr=[t_s1], strict=True)
                for t, s in enumerate(tiles(4)):
                    for sc in range(2):
                        c = t * 2 + sc
                        std_unit(s, sc, 16, hrhs, hbtk, lambda h, b, c=c: E.op(
                            "dve", lambda e: e.tensor_tensor(out=s1[:, c, 32 + H(h).start:32 + H(h).stop], in0=ps[:, b, P(h)],
                                                             in1=s2[:, c, H(h)], op=ALU.mult),
                            rd=[t_bank[b], t_s2], wr=[t_s1]))
                E.op("dve", lambda e, l=l: e.tensor_copy(out=uhist[:, l, :, :], in_=s1[:, :, 32 + NT - 2:32 + NT]),
                     rd=[t_s1], wr=[t_hist], strict=True)
                for c in range(8):
                    for h in range(2):
                        E.op("dve", lambda e, c=c, h=h, w_ap=cs(l, C_CBW + c * 3 + 0): e.tensor_scalar(
                            out=s2[:, c, H(h)], in0=s1[:, c, 30 + H(h).start:30 + H(h).stop],
                            scalar1=w_ap, scalar2=None, op0=ALU.mult),
                            rd=[t_s1, t_cst], wr=[t_s2])
                        for k in (1, 2):
                            E.op("dve", lambda e, c=c, h=h, k=k, w_ap=cs(l, C_CBW + c * 3 + k): e.scalar_tensor_tensor(
                                out=s2[:, c, H(h)], in0=s1[:, c, 30 + k + H(h).start:30 + k + H(h).stop],
                                scalar=w_ap, in1=s2[:, c, H(h)], op0=ALU.mult, op1=ALU.add),
                                rd=[t_s1, t_cst], wr=[t_s2])
                for t, s in enumerate(tiles(4)):
                    for sc in range(2):
                        c = t * 2 + sc
                        std_unit(s, sc, 16, hrhs, hbtk, lambda h, b, c=c: E.op(
                            "dve", lambda e: e.tensor_tensor(out=bact[:, c, H(h)], in0=ps[:, b, P(h)],
                                                             in1=s2[:, c, H(h)], op=ALU.mult),
                            rd=[t_bank[b], t_s2], wr=[t_bact]))
                for t, s in enumerate(tiles(4)):
                    for sc in range(2):
                        c = t * 2 + sc
                        std_unit(s, sc, 16, hrhs, hbtk, lambda h, b, c=c: E.op(
                            "act", lambda e: e.activation(out=s2[:, c, H(h)], in_=ps[:, b, P(h)], func=AF.Sigmoid),
                            rd=[t_bank[b]], wr=[t_s2]))
                E.op("dve", lambda e, l=l: e.tensor_copy(out=s1[:, :, 2:32], in_=ahist[:, l, :, :]),
                     rd=[t_hist], wr=[t_s1], strict=True)
                def conv_chunk_ops(c, l=l):
                    ops = []

                    def first():
                        E.op("dve", lambda e: e.tensor_scalar(
                            out=s2[:, c, cur["lo"]:NT], in0=s1[:, c, 2 + cur["lo"]:2 + NT],
                            scalar1=cs(l, C_CAW + c * CK), scalar2=cs(l, C_CAB + c), op0=ALU.mult, op1=ALU.add),
                            rd=[t_s1, t_cst], wr=[t_s2])
                    ops.append(first)
                    for k in range(1, CK):
                        def tap(k=k):
                            E.op("dve", lambda e: e.scalar_tensor_tensor(
                                out=s2[:, c, cur["lo"]:NT], in0=s1[:, c, 2 + k + cur["lo"]:2 + k + NT],
                                scalar=cs(l, C_CAW + c * CK + k), in1=s2[:, c, cur["lo"]:NT], op0=ALU.mult, op1=ALU.add),
                                rd=[t_s1, t_cst], wr=[t_s2])
                        ops.append(tap)
                    return ops

                early = bg_conv and EARLY_CONV and POOL_CHUNKS == 0
                if early:
                    E.bg_rate["dve"] = BG_RATE
                for t, s in enumerate(tiles(4)):
                    for sc in range(2):
                        c = t * 2 + sc
                        std_unit(s, sc, 16, hrhs, hbtk, lambda h, b, c=c: E.op(
                            "dve", lambda e: e.tensor_tensor(out=s1[:, c, 32 + H(h).start:32 + H(h).stop], in0=ps[:, b, P(h)],
                                                             in1=s2[:, c, H(h)], op=ALU.mult),
                            rd=[t_bank[b], t_s2], wr=[t_s1]))
                        if early:
                            E.drain_bg("dve", A_DRAIN)
                    if early:
                        E.bg["dve"].extend(conv_chunk_ops(2 * t) + conv_chunk_ops(2 * t + 1))
                E.op("dve", lambda e, l=l: e.tensor_copy(out=ahist[:, l, :, :], in_=s1[:, :, 32 + NT - 30:32 + NT]),
                     rd=[t_s1], wr=[t_hist], strict=True)

                if cur["lo"] > 0:
                    cur["lo"] += 30
                NDC = 8 - POOL_CHUNKS
                t_s2p.w = dict(t_s2.w); t_s2p.r = dict(t_s2.r)

                def conv_ops(l=l):
                    ops = []
                    for c in range(NDC):
                        ops += conv_chunk_ops(c)
                    return ops

                def conv_ops_pool(l=l):
                    ops = []
                    for c in range(NDC, 8):
                        for h in range(2):
                            def first(c=c, h=h):
                                E.op("pool", lambda e: e.tensor_scalar(
                                    out=s2[:, c, H(h)], in0=s1[:, c, 2 + H(h).start:2 + H(h).stop],
                                    scalar1=cs(l, C_CAW + c * CK), scalar2=cs(l, C_CAB + c), op0=ALU.mult, op1=ALU.add),
                                    rd=[t_s1, t_cst], wr=[t_s2p])
                            ops.append(first)
                            for k in range(1, CK):
                                def tap(c=c, h=h, k=k):
                                    E.op("pool", lambda e: e.tensor_scalar(
                                        out=ptmp[:], in0=s1[:, c, 2 + k + H(h).start:2 + k + H(h).stop],
                                        scalar1=cs(l, C_CAW + c * CK + k), scalar2=None, op0=ALU.mult),
                                        rd=[t_s1, t_cst], wr=[t_ptmp])
                                    E.op("pool", lambda e: e.tensor_tensor(
                                        out=s2[:, c, H(h)], in0=s2[:, c, H(h)], in1=ptmp[:], op=ALU.add),
                                        rd=[t_ptmp], wr=[t_s2p])
                                ops.append(tap)
                    return ops

                def ln_silu(l=l):
                    bsum = [6, 7]
                    bsq = [nbank(), nbank()]
                    lag = []

                    def pe_flush(keep):
                        while len(lag) > keep:
                            (i, bnk, h, first, last) = lag.pop(0)
                            E.op("pe", lambda e, i=i, bnk=bnk, h=h, first=first, last=last: e.matmul(
                                ps[:, bnk, P(h)], lhsT=ones[:], rhs=sq[:, i, P(h)], start=first, stop=last),
                                rd=[t_ones, t_sq[i]], wr=[t_bank[bnk]] if first else [], sig=True)
                            t_bank[bnk].w = {"pe": E.cnt["pe"]}
                    for h in range(2):
                        for c in range(8):
                            i = nsq()
                            E.op("act", lambda e, c=c, i=i, h=h: e.activation(out=sq[:, i, P(h)], in_=s2[:, c, H(h)], func=AF.Copy),
                                 rd=[t_s2, t_s2p], wr=[t_sq[i]])
                            lag.append((i, bsum[h], h, c == 0, c == 7))
                            j = nsq()
                            E.op("act", lambda e, c=c, j=j, h=h: e.activation(out=sq[:, j, P(h)], in_=s2[:, c, H(h)], func=AF.Square),
                                 rd=[t_s2, t_s2p], wr=[t_sq[j]])
                            lag.append((j, bsq[h], h, c == 0, c == 7))
                            pe_flush(4)
                    pe_flush(0)
                    for h in range(2):
                        E.op("dve", lambda e, h=h: e.tensor_scalar(
                            out=st_b[:, h, P(h)], in0=ps[:, bsum[h], P(h)], scalar1=1.0 / 1024, scalar2=None, op0=ALU.mult),
                            rd=[t_bank[bsum[h]]], wr=[t_stb[h]])
                        E.op("dve", lambda e, h=h: e.tensor_tensor(out=st_a[:, h, P(h)], in0=st_b[:, h, P(h)], in1=st_b[:, h, P(h)], op=ALU.mult),
                             rd=[t_stb[h]], wr=[t_sta[h]])
                        E.op("dve", lambda e, h=h: e.scalar_tensor_tensor(
                            out=st_c[:, h, P(h)], in0=ps[:, bsq[h], P(h)], scalar=1.0 / 1024, in1=st_a[:, h, P(h)],
                            op0=ALU.mult, op1=ALU.subtract), rd=[t_bank[bsq[h]], t_sta[h]], wr=[t_stc[h]])
                        E.op("act", lambda e, h=h: e.activation(out=st_c[:, h, P(h)], in_=st_c[:, h, P(h)], func=AF.Ln,
                                                             bias=cst_eps[:], scale=1.0), rd=[t_stc[h], t_cst], wr=[t_stc[h]])
                        E.op("act", lambda e, h=h: e.activation(out=st_c[:, h, P(h)], in_=st_c[:, h, P(h)], func=AF.Exp, scale=-0.5),
                             rd=[t_stc[h]], wr=[t_stc[h]])
                    for h in range(2):
                        for c in range(8):
                            E.op("dve", lambda e, c=c, h=h: e.tensor_tensor(out=s2[:, c, H(h)], in0=s2[:, c, H(h)],
                                                                          in1=st_b[:, h, P(h)], op=ALU.subtract),
                                 rd=[t_stb[h], t_s2p], wr=[t_s2])
                            E.op("dve", lambda e, c=c, h=h: e.tensor_tensor(out=s2[:, c, H(h)], in0=s2[:, c, H(h)],
                                                                          in1=st_c[:, h, P(h)], op=ALU.mult),
                                 rd=[t_stc[h]], wr=[t_s2])
                            E.op("act", lambda e, c=c, h=h: e.activation(out=aact[:, c, H(h)], in_=s2[:, c, H(h)], func=AF.Silu,
                                                                       bias=cs(l, C_LNB + c), scale=cs(l, C_LNG + c)),
                                 rd=[t_s2, t_cst], wr=[t_aact])
                    for k_, v_ in list(t_s2p.r.items()) + list(t_s2p.w.items()):
                        if t_s2.r.get(k_, 0) < v_:
                            t_s2.r[k_] = v_

                cops = [] if early else conv_ops()
                pops = conv_ops_pool()
                if bg_conv:
                    if not early:
                        E.bg["dve"] = cops
                    E.bg_rate["dve"] = BG_RATE
                    E.bg["pool"] = pops
                    E.bg_rate["pool"] = 14
                else:
                    for f in pops:
                        f()
                    for f in cops:
                        f()
                    ln_silu()

                for t, s in enumerate(tiles(4)):
                    for sc in range(2):
                        c = t * 2 + sc
                        std_unit(s, sc, 16, hrhs, hbtk, lambda h, b, c=c: E.op(
                            "act", lambda e: e.activation(out=qo[:, c, H(h)], in_=ps[:, b, P(h)], func=AF.Copy),
                            rd=[t_bank[b]], wr=[t_qo[c // 2]]))
                        if bg_conv:
                            E.drain_bg("dve", Q_DRAIN)
                units = [(hd, h) for hd in range(4) for h in range(2)]
                upts = {}

                def att_a(u):
                    hd, h = units[u]
                    pts = []
                    for mc in range(2):
                        b = group([(kT[:, hd * 2 + dc, mc * 128:(mc + 1) * 128], qo[:, hd * 2 + dc, H(h)], [t_kv, t_qo[hd]])
                                   for dc in range(2)], h=h)
                        i = nsq()
                        E.op("act", lambda e, i=i, b=b: e.activation(out=sq[:, i, P(h)], in_=ps[:, b, P(h)], func=AF.Exp,
                                                                   scale=1.0 / 16.0),
                             rd=[t_bank[b]], wr=[t_sq[i]])
                        pts.append(i)
                    upts[u] = pts

                def att_b(u):
                    hd, h = units[u]
                    pts = upts[u]
                    bden = colsum([(sq[:, i, P(h)], [t_sq[i]]) for i in pts], 6 + h, h=h)
                    E.op("act", lambda e, h=h, bden=bden: e.activation(out=st_b[:, h, P(h)], in_=ps[:, bden, P(h)], func=AF.Ln),
                         rd=[t_bank[bden]], wr=[t_stb[h]])
                    E.op("act", lambda e, h=h: e.activation(out=st_b[:, h, P(h)], in_=st_b[:, h, P(h)], func=AF.Exp, scale=-1.0),
                         rd=[t_stb[h]], wr=[t_stb[h]])
                    for dc in range(2):
                        b = group([(vv[:, mc, (hd * 2 + dc) * 128:(hd * 2 + dc + 1) * 128], sq[:, pts[mc], P(h)], [t_kv, t_sq[pts[mc]]])
                                   for mc in range(2)], h=h)
                        E.op("dve", lambda e, hd=hd, dc=dc, h=h, b=b: e.tensor_tensor(
                            out=qo[:, hd * 2 + dc, H(h)], in0=ps[:, b, P(h)], in1=st_b[:, h, P(h)], op=ALU.mult),
                            rd=[t_bank[b], t_stb[h]], wr=[t_qo[hd]])

                att_a(0)
                for u in range(8):
                    if u + 1 < 8:
                        att_a(u + 1)
                    att_b(u)
                for jp in range(8):
                    T = {}
                    for br in (1, 2):
                        s = next_tile()
                        wg = ring[:, s, :].rearrange("p (k c) -> p k c", c=256)
                        for sc in range(2):
                            cs_ = slice(sc * 128, (sc + 1) * 128)
                            for h in range(2):
                                bg_ = group([(wg[:, k, cs_], hb[:, k, H(h)], [t_slot[s], thb[k][h]]) for k in range(16)], h=h)
                                i1 = ntmp()
                                E.op("act", lambda e, i1=i1, bg_=bg_: e.activation(out=tmpf[:, i1, P(h)], in_=ps[:, bg_, P(h)], func=AF.Sigmoid),
                                     rd=[t_bank[bg_]], wr=[t_tmp[i1]])
                                T[br, sc, h] = i1
                                if bg_conv:
                                    E.drain_bg("dve", G_DRAIN)
                        rel(1)
                    s = next_tile()
                    wbx = ring[:, s, :].rearrange("p (b k c) -> p b k c", b=2, c=256)
                    for sc in range(2):
                        j = jp * 2 + sc
                        cs_ = slice(sc * 128, (sc + 1) * 128)
                        for h in range(2):
                            i1 = T[1, sc, h]; i2 = T[2, sc, h]
                            byb = group([(wbx[:, 0, k, cs_], bact[:, k, H(h)], [t_slot[s], t_bact]) for k in range(8)], h=h)
                            E.op("dve", lambda e, i1=i1, byb=byb: e.tensor_tensor(out=tmpf[:, i1, P(h)], in0=ps[:, byb, P(h)],
                                                                                in1=tmpf[:, i1, P(h)], op=ALU.mult),
                                 rd=[t_bank[byb], t_tmp[i1]], wr=[t_tmp[i1]])
                            byx = group([(wbx[:, 1, k, cs_], qo[:, k, H(h)], [t_slot[s]] + t_qo) for k in range(8)], h=h)
                            E.op("dve", lambda e, i2=i2, byx=byx: e.tensor_tensor(out=tmpf[:, i2, P(h)], in0=ps[:, byx, P(h)],
                                                                                in1=tmpf[:, i2, P(h)], op=ALU.mult),
                                 rd=[t_bank[byx], t_tmp[i2]], wr=[t_tmp[i2]])
                            E.op("dve", lambda e, i1=i1, i2=i2, j=j, h=h: e.tensor_tensor(
                                out=merged[:, j, H(h)], in0=tmpf[:, i1, P(h)], in1=tmpf[:, i2, P(h)], op=ALU.add),
                                rd=[t_tmp[i1], t_tmp[i2]], wr=[t_mrg])
                    rel(1)
                    if bg_conv and jp == LN_AT:
                        E.drain_bg("pool")
                        E.drain_bg("dve")
                        ln_silu()
                for jq in range(4):
                    T = {}
                    for half4 in range(2):
                        s = next_tile()
                        wg = ring[:, s, :].rearrange("p (k c) -> p k c", c=256)
                        for sc in range(2):
                            q4 = half4 * 2 + sc
                            cg = slice(sc * 128, (sc + 1) * 128)
                            for h in range(2):
                                bg0 = group([(wg[:, k, cg], hb[:, k, H(h)], [t_slot[s], thb[k][h]]) for k in range(16)], h=h)
                                i1 = ntmp()
                                E.op("act", lambda e, i1=i1, bg0=bg0: e.activation(out=tmpf[:, i1, P(h)], in_=ps[:, bg0, P(h)], func=AF.Sigmoid),
                                     rd=[t_bank[bg0]], wr=[t_tmp[i1]])
                                T[q4, h] = i1
                        rel(1)
                    if bg_conv and jq == 0 and LN_AT < 0:
                        E.drain_bg("pool")
                        E.drain_bg("dve")
                        ln_silu()
                    s = next_tile()
                    wa = ring[:, s, :].rearrange("p (k c) -> p k c", c=512)
                    for q4 in range(4):
                        j = jq * 4 + q4
                        ca = slice(q4 * 128, (q4 + 1) * 128)
                        for h in range(2):
                            i1 = T[q4, h]
                            bya = group([(wa[:, k, ca], aact[:, k, H(h)], [t_slot[s], t_aact]) for k in range(8)], h=h)
                            E.op("dve", lambda e, i1=i1, bya=bya: e.tensor_tensor(out=tmpf[:, i1, P(h)], in0=ps[:, bya, P(h)],
                                                                                in1=tmpf[:, i1, P(h)], op=ALU.mult),
                                 rd=[t_bank[bya], t_tmp[i1]], wr=[t_tmp[i1]])
                            E.op("dve", lambda e, i1=i1, j=j, h=h: e.tensor_tensor(
                                out=merged[:, j, H(h)], in0=merged[:, j, H(h)], in1=tmpf[:, i1, P(h)], op=ALU.add),
                                rd=[t_tmp[i1]], wr=[t_mrg])
                    rel(1)
                for k, v in list(t_s2.w.items()) + list(t_s2.r.items()):
                    if t_s1.r.get(k, 0) < v:
                        t_s1.r[k] = v
                pend = []
                for t, s in enumerate(tiles(8)):
                    for sc in range(2):
                        j = t * 2 + sc
                        std_unit(s, sc, 16, lambda k, h: merged[:, k, H(h)], [t_mrg],
                                 lambda h, b, j=j: zevac_with_stats(zm, t_s1, j, h, b, j == 0, j == 15, pend))
                        flush_stats(pend, keep=2)
                flush_stats(pend)
                post_norm_update(l, C_GPOST, zm, t_s1)
                t_s2.w = dict(t_s1.w); t_s2.r = dict(t_s1.r)
                ffn_stats = ffn_pre(l)
                for tk in (t_s1, t_s2, t_aact, t_bact, t_kv) + tuple(t_qo):
                    for k, v in list(tk.w.items()) + list(tk.r.items()):
                        if t_hid.r.get(k, 0) < v:
                            t_hid.r[k] = v
                for t, s in enumerate(tiles(32)):
                    for sc in range(2):
                        c = t * 2 + sc

                        def ev_up(h, b, c=c):
                            i = ntmp()
                            E.op("act", lambda e: e.activation(out=tmpf[:, i, P(h)], in_=ps[:, b, P(h)], func=AF.Relu),
                                 rd=[t_bank[b]], wr=[t_tmp[i]])
                            E.op("dve", lambda e: e.tensor_tensor(out=hid[:, c, H(h)], in0=tmpf[:, i, P(h)], in1=tmpf[:, i, P(h)], op=ALU.mult),
                                 rd=[t_tmp[i]], wr=[t_hid])
                        std_unit(s, sc, 16, hrhs, hbtk, ev_up)
                    if t >= 2:
                        for _ in range(2):
                            if ffn_stats:
                                ffn_stats.pop(0)()
                for tk in t_hb_all + [t_mrg]:
                    for k, v in list(tk.w.items()) + list(tk.r.items()):
                        if t_z.r.get(k, 0) < v:
                            t_z.r[k] = v
                pend = []
                for jc in range(16):
                    sl = [next_tile(), next_tile()]
                    for h in range(2):
                        mms = []
                        for kg in range(2):
                            wv = ring[:, sl[kg], :].rearrange("p (k c) -> p k c", c=128)
                            mms += [(wv[:, k, :], hid[:, kg * 32 + k, H(h)], [t_slot[sl[kg]], t_hid]) for k in range(32)]
                        b = group(mms, h=h)
                        zevac_with_stats(zf, t_z, jc, h, b, jc == 0, jc == 15, pend)
                    flush_stats(pend, keep=2)
                    rel(2)
                flush_stats(pend)
                for tk in (t_s1, t_s2, t_aact, t_bact, t_kv) + tuple(t_qo):
                    tk.w = dict(t_hid.w); tk.r = dict(t_hid.r)
                post_norm_rstd(eps_tile=True)
                if l + 1 < n_layers:
                    kv_norm(l + 1)
                post_norm_xupdate(l, C_GMPOST, zf, t_z)
                for tk in t_hb_all:
                    tk.w = dict(t_z.w); tk.r = dict(t_z.r)
                t_mrg.w = dict(t_z.w); t_mrg.r = dict(t_z.r)
                if blk == 0:
                    for c in range(16):
                        E.op("dve", lambda e, c=c: e.tensor_scalar(out=xs[:, c, 0:HALO], in0=xs[:, c, 0:HALO],
                                                                  scalar1=hmask[:, 0:1], scalar2=None, op0=ALU.mult),
                             rd=[t_cst], wr=txs[c])
            if blk == 0:
                ev = E.dma("sp", "out", outT[:, :, 0:NT - HALO], xs[:, :, HALO:NT], rd=t_xs_all)
            else:
                ev = E.dma("sp", "out", outT[:, :, blk * NT - HALO:(blk + 1) * NT - HALO], xs[:], rd=t_xs_all)
            final_evs.append(ev)
        E.finalize(st, final_evs)
    return nc


def _std(W, c0, ncols=256, k0=0, nk=16):
    blk = W[k0 * 128:(k0 + nk) * 128, c0:c0 + ncols]
    return blk.reshape(nk, 128, ncols).transpose(1, 0, 2).reshape(128, nk * ncols)


def pack_layer_tiles(out, w_in, w_a_out, w_b_out, w_kv, w_x_out, w_o, w_up, w_down):
    i = 0

    def put(a):
        nonlocal i
        out[i] = a
        i += 1
    for t in range(8):
        put(_std(w_kv, t * 256))
    for base in (3072, 4096, 2048, 1024, 0, 5120):
        for t in range(4):
            put(_std(w_in, base + t * 256))
    for jp in range(8):
        put(_std(w_in, 8192 + jp * 256))
        put(_std(w_in, 10240 + jp * 256))
        put(np.concatenate([_std(w_b_out, jp * 256, nk=8), _std(w_x_out, jp * 256, nk=8)], axis=1))
    for jq in range(4):
        put(_std(w_in, 6144 + (2 * jq) * 256))
        put(_std(w_in, 6144 + (2 * jq + 1) * 256))
        put(_std(w_a_out, jq * 512, ncols=512, nk=8))
    for t in range(8):
        put(_std(w_o, t * 256))
    for t in range(32):
        put(_std(w_up, t * 256))
    for jc in range(16):
        for kg in range(2):
            put(_std(w_down, jc * 128, ncols=128, k0=kg * 32, nk=32))
    assert i == TPL


def pack_consts(layers, g_mix_pre, g_mix_post, g_mlp_pre, g_mlp_post, g_mem, conv_a_w, conv_a_b, ln_a_g, ln_a_b, conv_b_w):
    cst = np.zeros((128, len(layers) * C_PER), np.float32)
    for li, l in enumerate(layers):
        o = li * C_PER
        for col, g in ((C_GPRE, g_mix_pre), (C_GPOST, g_mix_post), (C_GMPRE, g_mlp_pre), (C_GMPOST, g_mlp_post), (C_GMEM, g_mem)):
            cst[:, o + col:o + col + 16] = g[l].reshape(16, 128).T
        cst[:, o + C_CAW:o + C_CAW + 8 * CK] = conv_a_w[l].reshape(CK, 8, 128).transpose(2, 1, 0).reshape(128, 8 * CK)
        cst[:, o + C_CAB:o + C_CAB + 8] = conv_a_b[l].reshape(8, 128).T
        cst[:, o + C_LNG:o + C_LNG + 8] = ln_a_g[l].reshape(8, 128).T
        cst[:, o + C_LNB:o + C_LNB + 8] = ln_a_b[l].reshape(8, 128).T
        cst[:, o + C_CBW:o + C_CBW + 24] = conv_b_w[l].reshape(3, 8, 128).transpose(2, 1, 0).reshape(128, 24)
    return cst


def shard_x(x2d, core):
    lo = core * TOK - HALO
    if lo < 0:
        blk = np.concatenate([np.zeros((HALO, D), np.float32), x2d[0:TOK]], axis=0)
    else:
        blk = x2d[lo:lo + TOK + HALO]
    return np.ascontiguousarray(blk.T.reshape(16, 128, TOK + HALO).transpose(1, 0, 2))


_PROG_CACHE = {}


def _get_prog(n_layers):
    if n_layers not in _PROG_CACHE:
        _PROG_CACHE[n_layers] = build_program(n_layers)
    return _PROG_CACHE[n_layers]


FUSED = True


def kernel(x, mem, g_mix_pre, w_in, conv_a_w, conv_a_b, ln_a_g, ln_a_b, w_a_out, conv_b_w, w_b_out,
           g_mem, w_kv, w_x_out, w_o, g_mix_post, g_mlp_pre, w_up, w_down, g_mlp_post):
    f = lambda a: np.asarray(a, dtype=np.float32)
    x2d = f(x)[0]
    memT = np.ascontiguousarray(f(mem)[0].T.reshape(16, 128, MEM).transpose(1, 0, 2))
    groups = [list(range(DEPTH))] if FUSED else [[l] for l in range(DEPTH)]
    for layers in groups:
        nl = len(layers)
        wts = np.empty((nl * TPL, 128, TILE), np.float32)
        for li, l in enumerate(layers):
            pack_layer_tiles(wts[li * TPL:(li + 1) * TPL], f(w_in[l]), f(w_a_out[l]), f(w_b_out[l]), f(w_kv[l]),
                             f(w_x_out[l]), f(w_o[l]), f(w_up[l]), f(w_down[l]))
        cst = pack_consts(layers, f(g_mix_pre), f(g_mix_post), f(g_mlp_pre), f(g_mlp_post), f(g_mem),
                          f(conv_a_w), f(conv_a_b), f(ln_a_g), f(ln_a_b), f(conv_b_w))
        nc = _get_prog(nl)
        in_maps = []
        for c in range(NCORE):
            in_maps.append({"xT": shard_x(x2d, c), "wts": wts, "cst": cst, "memT": memT,
                            "hmask": np.full((128, 1), 0.0 if c == 0 else 1.0, np.float32)})
        res = run_bass_kernel_spmd(nc, in_maps, core_ids=list(range(NCORE)))
        outs = []
        for c in range(NCORE):
            o = res.results[c]["outT"]
            outs.append(o.transpose(2, 1, 0).reshape(TOK, D))
        x2d = np.concatenate(outs, axis=0)
    return np.ascontiguousarray(x2d[None]).astype(np.float32)
```

```python
import numpy as np
from contextlib import ExitStack
import concourse.bass as bass
import concourse.mybir as mybir
from concourse.bass_utils import run_bass_kernel_spmd

F32 = mybir.dt.float32
BF16 = mybir.dt.bfloat16
AF = mybir.ActivationFunctionType
ALU = mybir.AluOpType

D = 2048
SEQ = 8192
DEPTH = 4
NCORE = 8
TOK = SEQ // NCORE
HALO = 128
NT = 576
NH = 288
NBLK = 2
MEM = 256
CK = 31
EPS = 1e-6
TPL = 140
TILE = 4096
NSLOT = 4
NTMP = 8
POOL_CHUNKS = 0
LN_AT = 5
TRIM = True
BG_RATE = 1
BALANCE = False
FAST_RECIP = False
LNEXP = True
KV_SPLIT = 3
Q_DRAIN = 4
G_DRAIN = 2
EARLY_CONV = True
LN_SKIP = 6
LN_DEFER = True
A_DRAIN = 4

C_GPRE, C_GPOST, C_GMPRE, C_GMPOST, C_GMEM = 0, 16, 32, 48, 64
C_CAW = 80
C_CAB = C_CAW + 8 * CK
C_LNG = C_CAB + 8
C_LNB = C_LNG + 8
C_CBW = C_LNB + 8
C_PER = C_CBW + 24

ENGS = ("pe", "act", "dve", "pool", "sp")


class Tk:
    __slots__ = ("w", "r")

    def __init__(self):
        self.w = {}
        self.r = {}


class _Rec:
    def __init__(self):
        self.call = None

    def __getattr__(self, name):
        def f(*a, **k):
            assert self.call is None
            self.call = (name, a, k)
        return f


def _eager(fn):
    r = _Rec()
    fn(r)
    name, a, k = r.call
    return lambda e: getattr(e, name)(*a, **k)


class Emitter:
    def __init__(self, nc, strict_same=False):
        self.nc = nc
        self.streams = {e: [] for e in ENGS}
        self.cnt = {e: 0 for e in ENGS}
        self.waited = {e: {} for e in ENGS}
        self.dma_cnt = {}
        self.strict_same = strict_same
        self.sems = {}
        self.bg = {"dve": [], "pool": []}
        self.bg_rate = {"dve": 0, "pool": 0}
        self.in_bg = False

    def _deps(self, rd, wr, extra):
        deps = {}
        for t in rd:
            for k, v in t.w.items():
                if deps.get(k, 0) < v:
                    deps[k] = v
        for t in wr:
            for d in (t.w, t.r):
                for k, v in d.items():
                    if deps.get(k, 0) < v:
                        deps[k] = v
        for d in extra:
            if d is None:
                continue
            for k, v in d.items():
                if deps.get(k, 0) < v:
                    deps[k] = v
        return deps

    def _waits(self, eng, deps, strict):
        waits = []
        for k, v in deps.items():
            if k == eng and not strict:
                continue
            if self.waited[eng].get(k, 0) >= v:
                continue
            self.waited[eng][k] = v
            waits.append((k, v))
        return waits

    def op(self, eng, fn, rd=(), wr=(), sig=True, extra=(), strict=None):
        strict = self.strict_same if strict is None else strict
        deps = self._deps(rd, wr, extra)
        waits = self._waits(eng, deps, strict)
        if sig:
            self.cnt[eng] += 1
            c = self.cnt[eng]
        else:
            c = self.cnt[eng] + 1
        self.streams[eng].append((waits, _eager(fn), 1 if sig else 0, eng))
        for t in rd:
            if t.r.get(eng, 0) < c:
                t.r[eng] = c
        for t in wr:
            t.w = {eng: c}
            t.r = {}
        ev = {eng: c}
        if eng == "dve" and self.bg["dve"] and not self.in_bg:
            self.drain_bg("dve", self.bg_rate["dve"])
        return ev

    def drain_bg(self, eng, n=None):
        self.in_bg = True
        k = 0
        q = self.bg[eng]
        while q and (n is None or k < n):
            q.pop(0)()
            k += 1
        self.in_bg = False

    def dma(self, q, slot, out, in_, rd=(), wr=(), extra=()):
        key = "dma:" + slot
        deps = self._deps(rd, wr, extra)
        waits = self._waits(q, deps, False)
        self.dma_cnt[key] = self.dma_cnt.get(key, 0) + 16
        c = self.dma_cnt[key]
        self.streams[q].append((waits, lambda e, o=out, i=in_: e.dma_start(out=o, in_=i), 16, key))
        for t in rd:
            if t.r.get(key, 0) < c:
                t.r[key] = c
        for t in wr:
            t.w = {key: c}
            t.r = {}
        if q == "pool" and self.bg["pool"] and not self.in_bg:
            self.drain_bg("pool", self.bg_rate["pool"])
        return {key: c}

    def finalize(self, st, final_waits):
        nc = self.nc
        keys = list(ENGS) + sorted(self.dma_cnt.keys())
        for k in keys:
            self.sems[k] = st.enter_context(nc.semaphore("s_" + k.replace(":", "_")))
        block = st.enter_context(nc.Block())
        fin = {}
        for d in final_waits:
            for k, v in d.items():
                fin[k] = max(fin.get(k, 0), v)

        def replay(eng_name):
            def run(e):
                for waits, fn, inc, semkey in self.streams[eng_name]:
                    for k, v in waits:
                        e.wait_ge(self.sems[k], v)
                    ins = fn(e)
                    if inc:
                        ins.then_inc(self.sems[semkey], inc)
                if eng_name == "sp":
                    for k, v in fin.items():
                        e.wait_ge(self.sems[k], v)
            return run

        block.tensor(replay("pe"))
        block.scalar(replay("act"))
        block.vector(replay("dve"))
        block.gpsimd(replay("pool"))
        block.sync(replay("sp"))


def build_program(n_layers, nblk=NBLK, bg_conv=True):
    nc = bass.Bass("TRN2", target_bir_lowering=False)
    xT = nc.dram_tensor("xT", [128, 16, nblk * NT], F32, kind="ExternalInput").ap()
    wts = nc.dram_tensor("wts", [n_layers * TPL, 128, TILE], F32, kind="ExternalInput").ap()
    cstd = nc.dram_tensor("cst", [128, n_layers * C_PER], F32, kind="ExternalInput").ap()
    memTd = nc.dram_tensor("memT", [128, 16, MEM], F32, kind="ExternalInput").ap()
    hmaskd = nc.dram_tensor("hmask", [128, 1], F32, kind="ExternalInput").ap()
    outT = nc.dram_tensor("outT", [128, 16, nblk * NT - HALO], F32, kind="ExternalOutput").ap()

    st = ExitStack()
    with st:
        def sb(name, shape, dt):
            return st.enter_context(nc.sbuf_tensor(name, shape, dt))

        cst = sb("cst_sb", [128, n_layers * C_PER], F32)
        hmask = sb("hmask_sb", [128, 1], F32)
        xs = sb("xs", [128, 16, NT], F32)
        ZA = sb("ZA", [128, 16 * NT], F32)
        HA = sb("HA", [128, 64 * NT], BF16)
        ring = sb("ring", [128, NSLOT, TILE], BF16)
        ones = sb("ones", [128, 128], BF16)
        sq = sb("sq", [128, 8, NH], BF16)
        st_a = sb("st_a", [128, 2, NH], F32)
        st_b = sb("st_b", [128, 2, NH], F32)
        st_c = sb("st_c", [128, 2, NH], F32)
        tmpf = sb("tmpf", [128, NTMP, NH], F32)
        ptmp = sb("ptmp", [128, NH], F32) if POOL_CHUNKS > 0 else None
        ahist = sb("ahist", [128, n_layers, 8, 30], F32)
        uhist = sb("uhist", [128, n_layers, 8, 2], F32)
        ps = st.enter_context(nc.psum_tensor("ps", [128, 8, 512], F32))

        ZAb = ZA[:].bitcast(BF16)
        hb = ZAb[:, 0:16 * NT].rearrange("p (c n) -> p c n", c=16)
        merged = ZAb[:, 16 * NT:32 * NT].rearrange("p (c n) -> p c n", c=16)
        u32 = ZA[:, 8 * NT:16 * NT].rearrange("p (c n) -> p c n", c=8)
        zf = ZA[:].rearrange("p (c n) -> p c n", c=16)
        HAf = HA[:].bitcast(F32)
        o_s1 = 0
        n_s1 = 8 * (NT + 32)
        o_s2 = n_s1
        n_s2 = 8 * NT
        s1 = HAf[:, o_s1:o_s1 + n_s1].rearrange("p (c n) -> p c n", c=8)
        s2 = HAf[:, o_s2:o_s2 + n_s2].rearrange("p (c n) -> p c n", c=8)
        zm = HAf[:, 0:16 * NT].rearrange("p (c n) -> p c n", c=16)
        ob16 = 2 * (n_s1 + n_s2)
        aact = HA[:, ob16:ob16 + 8 * NT].rearrange("p (c n) -> p c n", c=8)
        bact = HA[:, ob16 + 8 * NT:ob16 + 16 * NT].rearrange("p (c n) -> p c n", c=8)
        qo = HA[:, ob16 + 16 * NT:ob16 + 24 * NT].rearrange("p (c n) -> p c n", c=8)
        okv = ob16 + 24 * NT
        kT = HA[:, okv:okv + 8 * MEM].rearrange("p (c n) -> p c n", c=8)
        vv = HA[:, okv + 8 * MEM:okv + 16 * MEM].rearrange("p (c n) -> p c n", c=2)
        assert okv + 16 * MEM <= 64 * NT
        hid = HA[:].rearrange("p (c n) -> p c n", c=64)
        memf = HAf[:, 0:16 * MEM].rearrange("p (c n) -> p c n", c=16)
        memn = HA[:, 2 * o_s2:2 * o_s2 + 16 * MEM].rearrange("p (c n) -> p c n", c=16)

        E = Emitter(nc)
        t_cst = Tk(); txs = [[Tk(), Tk()] for _ in range(16)]; t_xs_all = [t for p in txs for t in p]; thb = [[Tk(), Tk()] for _ in range(16)]; t_hb_all = [t for p in thb for t in p]; t_mrg = Tk(); t_z = Tk()
        t_s1 = Tk(); t_s2 = Tk(); t_aact = Tk(); t_bact = Tk(); t_qo = [Tk() for _ in range(4)]
        t_kv = Tk(); t_hid = Tk(); t_mem = Tk(); t_memn = Tk()
        t_slot = [Tk() for _ in range(NSLOT)]
        t_bank = [Tk() for _ in range(8)]
        t_sq = [Tk() for _ in range(8)]
        t_sta = [Tk(), Tk()]; t_stb = [Tk(), Tk()]; t_stc = [Tk(), Tk()]
        t_tmp = [Tk() for _ in range(NTMP)]
        t_s2p = Tk(); t_ptmp = Tk()
        t_hist = Tk(); t_ones = Tk()
        rr = {"bank": 0, "sq": 0, "tmp": 0, "pt": 0, "tile": 0, "done": 0, "skip": set(), "skipn": 0}

        cur = {"lo": 0}
        LO = [8, 38, 68, 98]

        def MID():
            lo = cur["lo"]
            if lo == 0 or not BALANCE:
                return NH
            return lo + ((NT - lo) // 4) * 2 + ((NT - lo) % 4 > 0) * 2 if False else max(NH, ((lo + NT) // 4) * 2)

        def H(h):
            return slice(cur["lo"], MID()) if h == 0 else slice(MID(), NT)

        def P(h):
            sl = H(h)
            return slice(0, sl.stop - sl.start)

        def cs(l, col):
            c0 = l * C_PER + col
            return cst[:, c0:c0 + 1]

        def nbank():
            while True:
                b = rr["bank"]
                rr["bank"] = (b + 1) % 6
                if rr["skipn"] > 0 and b in rr["skip"]:
                    continue
                break
            if rr["skipn"] > 0:
                rr["skipn"] -= 1
                if rr["skipn"] == 0:
                    rr["skip"] = set()
            return b

        def nsq():
            i = rr["sq"]; rr["sq"] = (i + 1) % 8
            return i

        def ntmp():
            i = rr["tmp"]; rr["tmp"] = (i + 1) % NTMP
            return i


        E.dma("sp", "cst", cst[:], cstd[:], wr=[t_cst])
        E.dma("sp", "hmask", hmask[:], hmaskd[:], wr=[t_cst])
        E.op("dve", lambda e: e.memset(ones[:], 1.0), wr=[t_ones])
        E.op("dve", lambda e: e.memset(HAf[:, 0:n_s1 + n_s2], 0.0), wr=[t_s1, t_s2])
        E.op("dve", lambda e: e.memset(ahist[:], 0.0), wr=[t_hist])
        E.op("dve", lambda e: e.memset(uhist[:], 0.0), wr=[t_hist])

        tile_state = {"next_load": 0, "order": []}

        def prefetch(upto):
            while tile_state["next_load"] < min(upto, len(tile_state["order"])):
                i = tile_state["next_load"]
                s = i % NSLOT
                E.dma("pool", "w%d" % s, ring[:, s, :], wts[tile_state["order"][i]], wr=[t_slot[s]])
                tile_state["next_load"] += 1

        def next_tile():
            i = rr["tile"]
            rr["tile"] += 1
            assert i - rr["done"] < NSLOT
            prefetch(i + 1)
            return i % NSLOT

        def tiles(n):
            for _ in range(n):
                s_ = next_tile()
                yield s_
                rel(1)

        def rel(n=1):
            rr["done"] += n
            prefetch(rr["done"] + NSLOT)

        def group(mms, h=None, nout=None, bank=None):
            b = nbank() if bank is None else bank
            n = len(mms)
            psl = P(h) if h is not None else slice(0, nout)
            for i, (l_ap, r_ap, rdt) in enumerate(mms):
                E.op("pe", lambda e, l_ap=l_ap, r_ap=r_ap, i=i, b=b: e.matmul(
                    ps[:, b, psl], lhsT=l_ap, rhs=r_ap, start=(i == 0), stop=(i == n - 1)),
                    rd=rdt, wr=[t_bank[b]] if i == 0 else [], sig=(i == n - 1))
            t_bank[b].w = {"pe": E.cnt["pe"]}
            return b

        def colsum(srcs, bank, h=None, nout=None):
            return group([(ones[:], s_ap, [t_ones] + tk) for s_ap, tk in srcs], h=h, nout=nout, bank=bank)

        def rstd_from_bank(bank, h, dst, t_dst, inv_n, nout=None):
            sl = P(h) if nout is None else slice(0, nout)
            if LNEXP:
                E.op("act", lambda e: e.activation(out=dst[:, h, sl], in_=ps[:, bank, sl], func=AF.Ln,
                                                   bias=cst_eps[:], scale=inv_n),
                     rd=[t_bank[bank], t_cst], wr=[t_dst])
                E.op("act", lambda e: e.activation(out=dst[:, h, sl], in_=dst[:, h, sl], func=AF.Exp, scale=-0.5),
                     rd=[t_dst], wr=[t_dst])
                return
            E.op("act", lambda e: e.activation(out=dst[:, h, sl], in_=ps[:, bank, sl], func=AF.Sqrt,
                                               bias=cst_eps[:], scale=inv_n),
                 rd=[t_bank[bank], t_cst], wr=[t_dst])
            E.op("dve", lambda e: e.reciprocal(out=dst[:, h, sl], in_=dst[:, h, sl]), rd=[t_dst], wr=[t_dst])

        st_mem = sb("st_mem", [128, 1, MEM], F32)
        t_stmem = Tk()
        cst_eps = sb("cst_eps", [128, 1], F32)
        E.op("dve", lambda e: e.memset(cst_eps[:], EPS), wr=[t_cst])

        def rms_pre(l, gcol):
            for h in range(2):
                b = 6 + h
                for c in range(16):
                    i = nsq()
                    E.op("act", lambda e, c=c, i=i, h=h: e.activation(out=sq[:, i, P(h)], in_=xs[:, c, H(h)], func=AF.Square),
                         rd=[txs[c][h]], wr=[t_sq[i]])
                    E.op("pe", lambda e, i=i, c=c, b=b: e.matmul(
                        ps[:, b, P(h)], lhsT=ones[:], rhs=sq[:, i, P(h)], start=(c == 0), stop=(c == 15)),
                        rd=[t_ones, t_sq[i]], wr=[t_bank[b]] if c == 0 else [], sig=True)
                t_bank[b].w = {"pe": E.cnt["pe"]}
                rstd_from_bank(b, h, st_a, t_sta[h], 1.0 / D)
                for c in range(16):
                    E.op("dve", lambda e, c=c, h=h: e.scalar_tensor_tensor(
                        out=hb[:, c, H(h)], in0=xs[:, c, H(h)], scalar=cs(l, gcol + c), in1=st_a[:, h, P(h)],
                        op0=ALU.mult, op1=ALU.mult), rd=[txs[c][h], t_sta[h], t_cst], wr=[thb[c][h]])

        def std_unit(slot, sc, nk, rhs_fn, rhs_tk, evac, ncols=256, koff=0):
            wv = ring[:, slot, :].rearrange("p (k c) -> p k c", c=ncols)
            for h in range(2):
                b = group([(wv[:, koff + k, sc * 128:(sc + 1) * 128], rhs_fn(k, h),
                            [t_slot[slot]] + (rhs_tk(k, h) if callable(rhs_tk) else rhs_tk))
                           for k in range(nk)], h=h)
                evac(h, b)

        def post_norm_rstd(eps_tile=False):
            for h in range(2):
                if eps_tile:
                    b = 6 + h
                    E.op("dve", lambda e, h=h, b=b: e.scalar_tensor_tensor(
                        out=st_a[:, h, P(h)], in0=ps[:, b, P(h)], scalar=1.0 / D, in1=st_b[:, h, P(h)],
                        op0=ALU.mult, op1=ALU.add), rd=[t_bank[b], t_stb[h]], wr=[t_sta[h]])
                    E.op("act", lambda e, h=h: e.activation(out=st_a[:, h, P(h)], in_=st_a[:, h, P(h)], func=AF.Ln),
                         rd=[t_sta[h]], wr=[t_sta[h]])
                    E.op("act", lambda e, h=h: e.activation(out=st_a[:, h, P(h)], in_=st_a[:, h, P(h)], func=AF.Exp, scale=-0.5),
                         rd=[t_sta[h]], wr=[t_sta[h]])
                else:
                    rstd_from_bank(6 + h, h, st_a, t_sta[h], 1.0 / D)

        def post_norm_xupdate(l, gcol, z, t_zz):
            for h in range(2):
                for c in range(16):
                    i = ntmp()
                    E.op("dve", lambda e, c=c, h=h, i=i: e.scalar_tensor_tensor(
                        out=tmpf[:, i, P(h)], in0=z[:, c, H(h)], scalar=cs(l, gcol + c), in1=st_a[:, h, P(h)],
                        op0=ALU.mult, op1=ALU.mult), rd=[t_zz, t_sta[h], t_cst], wr=[t_tmp[i]])
                    E.op("dve", lambda e, c=c, h=h, i=i: e.tensor_tensor(
                        out=xs[:, c, H(h)], in0=xs[:, c, H(h)], in1=tmpf[:, i, P(h)], op=ALU.add),
                        rd=[t_tmp[i]], wr=[txs[c][h]])

        def post_norm_update(l, gcol, z, t_zz, eps_tile=False):
            post_norm_rstd(eps_tile)
            post_norm_xupdate(l, gcol, z, t_zz)

        def ffn_pre(l):
            for h in range(2):
                for c in range(16):
                    E.op("act", lambda e, c=c, h=h: e.activation(out=hb[:, c, H(h)], in_=xs[:, c, H(h)], func=AF.Copy,
                                                               scale=cs(l, C_GMPRE + c)),
                         rd=[txs[c][h], t_cst], wr=[thb[c][h]])
            th = []
            lag = []

            def pe_one():
                (i, c, h) = lag.pop(0)
                b = 6 + h
                E.op("pe", lambda e: e.matmul(ps[:, b, P(h)], lhsT=ones[:], rhs=sq[:, i, P(h)], start=(c == 0), stop=(c == 15)),
                     rd=[t_ones, t_sq[i]], wr=[t_bank[b]] if c == 0 else [], sig=True)
                t_bank[b].w = {"pe": E.cnt["pe"]}
                if c == 15:
                    E.op("dve", lambda e: e.tensor_scalar(out=st_b[:, h, P(h)], in0=ps[:, b, P(h)], scalar1=1.0 / D,
                                                          scalar2=EPS, op0=ALU.mult, op1=ALU.add),
                         rd=[t_bank[b]], wr=[t_stb[h]])
                    E.op("dve", lambda e: e.tensor_tensor(out=st_b[:, h, P(h)], in0=st_b[:, h, P(h)], in1=st_b[:, h, P(h)], op=ALU.mult),
                         rd=[t_stb[h]], wr=[t_stb[h]])
                    E.op("dve", lambda e: e.tensor_scalar(out=st_b[:, h, P(h)], in0=st_b[:, h, P(h)], scalar1=EPS,
                                                          scalar2=None, op0=ALU.mult),
                         rd=[t_stb[h]], wr=[t_stb[h]])
            for h in range(2):
                for c in range(16):
                    def f(c=c, h=h):
                        i = nsq()
                        E.op("act", lambda e: e.activation(out=sq[:, i, P(h)], in_=xs[:, c, H(h)], func=AF.Square),
                             rd=[txs[c][h]], wr=[t_sq[i]])
                        lag.append((i, c, h))
                        if len(lag) > 3:
                            pe_one()
                    th.append(f)

            def fin():
                while lag:
                    pe_one()
            th.append(fin)
            return th

        def zevac_with_stats(z, t_zz, j, h, b, first, last, pend):
            E.op("act", lambda e: e.activation(out=z[:, j, H(h)], in_=ps[:, b, P(h)], func=AF.Copy),
                 rd=[t_bank[b]], wr=[t_zz])
            i = nsq()
            E.op("act", lambda e: e.activation(out=sq[:, i, P(h)], in_=ps[:, b, P(h)], func=AF.Square),
                 rd=[t_bank[b]], wr=[t_sq[i]])
            pend.append((i, h, first, last))

        def flush_stats(pend, keep=0):
            while len(pend) > keep:
                i, h, first, last = pend.pop(0)
                b = 6 + h
                E.op("pe", lambda e, i=i, b=b, first=first, last=last: e.matmul(
                    ps[:, b, P(h)], lhsT=ones[:], rhs=sq[:, i, P(h)], start=first, stop=last),
                    rd=[t_ones, t_sq[i]], wr=[t_bank[b]] if first else [], sig=True)
                t_bank[b].w = {"pe": E.cnt["pe"]}

        kv_state = {"have_rstd": False}

        def kv_norm(l):
            E.dma("sp", "mem", memf[:], memTd[:], wr=[t_s1])
            if not kv_state["have_rstd"]:
                kv_state["have_rstd"] = True
                b = nbank()
                for c in range(16):
                    i = nsq()
                    E.op("act", lambda e, c=c, i=i: e.activation(out=sq[:, i, 0:MEM], in_=memf[:, c, :], func=AF.Square),
                         rd=[t_s1], wr=[t_sq[i]])
                    E.op("pe", lambda e, i=i, c=c, b=b: e.matmul(
                        ps[:, b, 0:MEM], lhsT=ones[:], rhs=sq[:, i, 0:MEM], start=(c == 0), stop=(c == 15)),
                        rd=[t_ones, t_sq[i]], wr=[t_bank[b]] if c == 0 else [], sig=True)
                t_bank[b].w = {"pe": E.cnt["pe"]}
                rstd_from_bank(b, 0, st_mem, t_stmem, 1.0 / D, nout=MEM)
            for c in range(16):
                E.op("dve", lambda e, c=c, g_ap=cs(l, C_GMEM + c): e.scalar_tensor_tensor(
                    out=memn[:, c, :], in0=memf[:, c, :], scalar=g_ap, in1=st_mem[:, 0, 0:MEM],
                    op0=ALU.mult, op1=ALU.mult), rd=[t_s1, t_stmem, t_cst], wr=[t_s2])

        def kv_tiles(t0, t1):
            for t8 in range(t0, t1):
                s = next_tile()
                wv = ring[:, s, :].rearrange("p (k c) -> p k c", c=256)
                if t8 < 4:
                    t = t8
                    for sc in range(2):
                        dch = t * 2 + sc
                        b = group([(wv[:, k, sc * 128:(sc + 1) * 128], memn[:, k, :], [t_slot[s], t_s2])
                                   for k in range(16)], nout=MEM)
                        E.op("act", lambda e, dch=dch, b=b: e.activation(out=kT[:, dch, :], in_=ps[:, b, 0:MEM], func=AF.Copy),
                             rd=[t_bank[b]], wr=[t_kv])
                else:
                    t = t8 - 4
                    for mc in range(2):
                        b = group([(memn[:, k, mc * 128:(mc + 1) * 128], wv[:, k, :], [t_slot[s], t_s2])
                                   for k in range(16)], nout=256)
                        E.op("act", lambda e, t=t, mc=mc, b=b: e.activation(out=vv[:, mc, t * 256:(t + 1) * 256],
                                                                         in_=ps[:, b, 0:256], func=AF.Copy),
                             rd=[t_bank[b]], wr=[t_kv])
                rel(1)

        final_evs = []
        for blk in range(nblk):
            base_i = len(tile_state["order"])
            tile_state["order"].extend(list(range(n_layers * TPL)))
            prefetch(rr["done"] + NSLOT)
            E.dma("sp", "xin", xs[:], xT[:, :, blk * NT:(blk + 1) * NT], wr=t_xs_all)
            for l in range(n_layers):
                cur["lo"] = LO[l + DEPTH - n_layers] if (blk == 0 and TRIM) else 0
                if l == 0:
                    kv_norm(0)
                kv_tiles(0, KV_SPLIT)
                rms_pre(l, C_GPRE)
                kv_tiles(KV_SPLIT, 8)
                hrhs = lambda k, h: hb[:, k, H(h)]
                hbtk = lambda k, h: [thb[k][h]]
                for t, s in enumerate(tiles(4)):
                    for sc in range(2):
                        c = t * 2 + sc
                        std_unit(s, sc, 16, hrhs, hbtk, lambda h, b, c=c: E.op(
                            "act", lambda e: e.activation(out=s2[:, c, H(h)], in_=ps[:, b, P(h)], func=AF.Copy),
                            rd=[t_bank[b]], wr=[t_s2]))
                E.op("dve", lambda e, l=l: e.tensor_copy(out=s1[:, :, 30:32], in_=uhist[:, l, :, :]),
                     rd=[t_hist], wr=[t_s1], strict=True)
                for t, s in enumerate(tiles(4)):
                    for sc in range(2):
                        c = t * 2 + sc
                        std_unit(s, sc, 16, hrhs, hbtk, lambda h, b, c=c: E.op(
                            "dve", lambda e: e.tensor_tensor(out=s1[:, c, 32 + H(h).start:32 + H(h).stop], in0=ps[:, b, P(h)],
                                                             in1=s2[:, c, H(h)], op=ALU.mult),
                            rd=[t_bank[b], t_s2], wr=[t_s1]))
                E.op("dve", lambda e, l=l: e.tensor_copy(out=uhist[:, l, :, :], in_=s1[:, :, 32 + NT - 2:32 + NT]),
                     rd=[t_s1], wr=[t_hist], strict=True)
                for c in range(8):
                    for h in range(2):
                        E.op("dve", lambda e, c=c, h=h, w_ap=cs(l, C_CBW + c * 3 + 0): e.tensor_scalar(
                            out=u32[:, c, H(h)], in0=s1[:, c, 30 + H(h).start:30 + H(h).stop],
                            scalar1=w_ap, scalar2=None, op0=ALU.mult),
                            rd=[t_s1, t_cst], wr=[t_mrg])
                        for k in (1, 2):
                            E.op("dve", lambda e, c=c, h=h, k=k, w_ap=cs(l, C_CBW + c * 3 + k): e.scalar_tensor_tensor(
                                out=u32[:, c, H(h)], in0=s1[:, c, 30 + k + H(h).start:30 + k + H(h).stop],
                                scalar=w_ap, in1=u32[:, c, H(h)], op0=ALU.mult, op1=ALU.add),
                                rd=[t_s1, t_cst], wr=[t_mrg])
                for t, s in enumerate(tiles(4)):
                    for sc in range(2):
                        c = t * 2 + sc
                        std_unit(s, sc, 16, hrhs, hbtk, lambda h, b, c=c: E.op(
                            "act", lambda e: e.activation(out=s2[:, c, H(h)], in_=ps[:, b, P(h)], func=AF.Sigmoid),
                            rd=[t_bank[b]], wr=[t_s2]))
                E.op("dve", lambda e, l=l: e.tensor_copy(out=s1[:, :, 2:32], in_=ahist[:, l, :, :]),
                     rd=[t_hist], wr=[t_s1], strict=True)
                def conv_chunk_ops(c, l=l):
                    ops = []

                    def first():
                        E.op("dve", lambda e: e.tensor_scalar(
                            out=s2[:, c, cur["lo"]:NT], in0=s1[:, c, 2 + cur["lo"]:2 + NT],
                            scalar1=cs(l, C_CAW + c * CK), scalar2=cs(l, C_CAB + c), op0=ALU.mult, op1=ALU.add),
                            rd=[t_s1, t_cst], wr=[t_s2])
                    ops.append(first)
                    for k in range(1, CK):
                        def tap(k=k):
                            E.op("dve", lambda e: e.scalar_tensor_tensor(
                                out=s2[:, c, cur["lo"]:NT], in0=s1[:, c, 2 + k + cur["lo"]:2 + k + NT],
                                scalar=cs(l, C_CAW + c * CK + k), in1=s2[:, c, cur["lo"]:NT], op0=ALU.mult, op1=ALU.add),
                                rd=[t_s1, t_cst], wr=[t_s2])
                        ops.append(tap)
                    return ops

                early = bg_conv and EARLY_CONV and POOL_CHUNKS == 0
                if early:
                    E.bg_rate["dve"] = BG_RATE
                for t, s in enumerate(tiles(4)):
                    for sc in range(2):
                        c = t * 2 + sc
                        std_unit(s, sc, 16, hrhs, hbtk, lambda h, b, c=c: E.op(
                            "dve", lambda e: e.tensor_tensor(out=s1[:, c, 32 + H(h).start:32 + H(h).stop], in0=ps[:, b, P(h)],
                                                             in1=s2[:, c, H(h)], op=ALU.mult),
                            rd=[t_bank[b], t_s2], wr=[t_s1]))
                        if early:
                            E.drain_bg("dve", A_DRAIN)
                    if early:
                        E.bg["dve"].extend(conv_chunk_ops(2 * t) + conv_chunk_ops(2 * t + 1))
                E.op("dve", lambda e, l=l: e.tensor_copy(out=ahist[:, l, :, :], in_=s1[:, :, 32 + NT - 30:32 + NT]),
                     rd=[t_s1], wr=[t_hist], strict=True)

                if cur["lo"] > 0:
                    cur["lo"] += 30
                NDC = 8 - POOL_CHUNKS
                t_s2p.w = dict(t_s2.w); t_s2p.r = dict(t_s2.r)

                def conv_ops(l=l):
                    ops = []
                    for c in range(NDC):
                        ops += conv_chunk_ops(c)
                    return ops

                def conv_ops_pool(l=l):
                    ops = []
                    for c in range(NDC, 8):
                        for h in range(2):
                            def first(c=c, h=h):
                                E.op("pool", lambda e: e.tensor_scalar(
                                    out=s2[:, c, H(h)], in0=s1[:, c, 2 + H(h).start:2 + H(h).stop],
                                    scalar1=cs(l, C_CAW + c * CK), scalar2=cs(l, C_CAB + c), op0=ALU.mult, op1=ALU.add),
                                    rd=[t_s1, t_cst], wr=[t_s2p])
                            ops.append(first)
                            for k in range(1, CK):
                                def tap(c=c, h=h, k=k):
                                    E.op("pool", lambda e: e.tensor_scalar(
                                        out=ptmp[:], in0=s1[:, c, 2 + k + H(h).start:2 + k + H(h).stop],
                                        scalar1=cs(l, C_CAW + c * CK + k), scalar2=None, op0=ALU.mult),
                                        rd=[t_s1, t_cst], wr=[t_ptmp])
                                    E.op("pool", lambda e: e.tensor_tensor(
                                        out=s2[:, c, H(h)], in0=s2[:, c, H(h)], in1=ptmp[:], op=ALU.add),
                                        rd=[t_ptmp], wr=[t_s2p])
                                ops.append(tap)
                    return ops

                ln_defer = []

                def ln_silu(l=l):
                    bsum = [6, 7]
                    bsq = [nbank(), nbank()]
                    rr["skip"] = set(bsq); rr["skipn"] = LN_SKIP
                    lag = []

                    def pe_flush(keep):
                        while len(lag) > keep:
                            (i, bnk, h, first, last) = lag.pop(0)
                            E.op("pe", lambda e, i=i, bnk=bnk, h=h, first=first, last=last: e.matmul(
                                ps[:, bnk, P(h)], lhsT=ones[:], rhs=sq[:, i, P(h)], start=first, stop=last),
                                rd=[t_ones, t_sq[i]], wr=[t_bank[bnk]] if first else [], sig=True)
                            t_bank[bnk].w = {"pe": E.cnt["pe"]}
                    for h in range(2):
                        for c in range(8):
                            i = nsq()
                            E.op("act", lambda e, c=c, i=i, h=h: e.activation(out=sq[:, i, P(h)], in_=s2[:, c, H(h)], func=AF.Copy),
                                 rd=[t_s2, t_s2p], wr=[t_sq[i]])
                            lag.append((i, bsum[h], h, c == 0, c == 7))
                            j = nsq()
                            E.op("act", lambda e, c=c, j=j, h=h: e.activation(out=sq[:, j, P(h)], in_=s2[:, c, H(h)], func=AF.Square),
                                 rd=[t_s2, t_s2p], wr=[t_sq[j]])
                            lag.append((j, bsq[h], h, c == 0, c == 7))
                            pe_flush(4)
                    pe_flush(0)
                    for h in range(2):
                        E.op("dve", lambda e, h=h: e.tensor_scalar(
                            out=st_b[:, h, P(h)], in0=ps[:, bsum[h], P(h)], scalar1=1.0 / 1024, scalar2=None, op0=ALU.mult),
                            rd=[t_bank[bsum[h]]], wr=[t_stb[h]])
                        E.op("dve", lambda e, h=h: e.tensor_tensor(out=st_a[:, h, P(h)], in0=st_b[:, h, P(h)], in1=st_b[:, h, P(h)], op=ALU.mult),
                             rd=[t_stb[h]], wr=[t_sta[h]])
                        E.op("dve", lambda e, h=h: e.scalar_tensor_tensor(
                            out=st_c[:, h, P(h)], in0=ps[:, bsq[h], P(h)], scalar=1.0 / 1024, in1=st_a[:, h, P(h)],
                            op0=ALU.mult, op1=ALU.subtract), rd=[t_bank[bsq[h]], t_sta[h]], wr=[t_stc[h]])
                        E.op("act", lambda e, h=h: e.activation(out=st_c[:, h, P(h)], in_=st_c[:, h, P(h)], func=AF.Ln,
                                                             bias=cst_eps[:], scale=1.0), rd=[t_stc[h], t_cst], wr=[t_stc[h]])
                        E.op("act", lambda e, h=h: e.activation(out=st_c[:, h, P(h)], in_=st_c[:, h, P(h)], func=AF.Exp, scale=-0.5),
                             rd=[t_stc[h]], wr=[t_stc[h]])
                    for h in range(2):
                        for c in range(8):
                            E.op("dve", lambda e, c=c, h=h: e.tensor_tensor(out=s2[:, c, H(h)], in0=s2[:, c, H(h)],
                                                                          in1=st_b[:, h, P(h)], op=ALU.subtract),
                                 rd=[t_stb[h], t_s2p], wr=[t_s2])
                            E.op("dve", lambda e, c=c, h=h: e.tensor_tensor(out=s2[:, c, H(h)], in0=s2[:, c, H(h)],
                                                                          in1=st_c[:, h, P(h)], op=ALU.mult),
                                 rd=[t_stc[h]], wr=[t_s2])
                            def silu_op(c=c, h=h):
                                E.op("act", lambda e: e.activation(out=aact[:, c, H(h)], in_=s2[:, c, H(h)], func=AF.Silu,
                                                                   bias=cs(l, C_LNB + c), scale=cs(l, C_LNG + c)),
                                     rd=[t_s2, t_cst], wr=[t_aact])
                            if LN_DEFER and bg_conv:
                                ln_defer.append(silu_op)
                            else:
                                silu_op()
                    for k_, v_ in list(t_s2p.r.items()) + list(t_s2p.w.items()):
                        if t_s2.r.get(k_, 0) < v_:
                            t_s2.r[k_] = v_

                cops = [] if early else conv_ops()
                pops = conv_ops_pool()
                if bg_conv:
                    if not early:
                        E.bg["dve"] = cops
                    E.bg_rate["dve"] = BG_RATE
                    E.bg["pool"] = pops
                    E.bg_rate["pool"] = 14
                else:
                    for f in pops:
                        f()
                    for f in cops:
                        f()
                    ln_silu()

                for t, s in enumerate(tiles(4)):
                    for sc in range(2):
                        c = t * 2 + sc
                        std_unit(s, sc, 16, hrhs, hbtk, lambda h, b, c=c: E.op(
                            "dve", lambda e: e.tensor_tensor(out=bact[:, c, H(h)], in0=ps[:, b, P(h)],
                                                             in1=u32[:, c, H(h)], op=ALU.mult),
                            rd=[t_bank[b], t_mrg], wr=[t_bact]))
                        if bg_conv:
                            E.drain_bg("dve", A_DRAIN)
                for t, s in enumerate(tiles(4)):
                    for sc in range(2):
                        c = t * 2 + sc
                        std_unit(s, sc, 16, hrhs, hbtk, lambda h, b, c=c: E.op(
                            "act", lambda e: e.activation(out=qo[:, c, H(h)], in_=ps[:, b, P(h)], func=AF.Copy),
                            rd=[t_bank[b]], wr=[t_qo[c // 2]]))
                        if bg_conv:
                            E.drain_bg("dve", Q_DRAIN)
                units = [(hd, h) for hd in range(4) for h in range(2)]
                upts = {}

                def att_a(u):
                    hd, h = units[u]
                    pts = []
                    for mc in range(2):
                        b = group([(kT[:, hd * 2 + dc, mc * 128:(mc + 1) * 128], qo[:, hd * 2 + dc, H(h)], [t_kv, t_qo[hd]])
                                   for dc in range(2)], h=h)
                        i = nsq()
                        E.op("act", lambda e, i=i, b=b: e.activation(out=sq[:, i, P(h)], in_=ps[:, b, P(h)], func=AF.Exp,
                                                                   scale=1.0 / 16.0),
                             rd=[t_bank[b]], wr=[t_sq[i]])
                        pts.append(i)
                    upts[u] = pts

                def att_b(u):
                    hd, h = units[u]
                    pts = upts[u]
                    bden = colsum([(sq[:, i, P(h)], [t_sq[i]]) for i in pts], 6 + h, h=h)
                    E.op("act", lambda e, h=h, bden=bden: e.activation(out=st_b[:, h, P(h)], in_=ps[:, bden, P(h)], func=AF.Ln),
                         rd=[t_bank[bden]], wr=[t_stb[h]])
                    E.op("act", lambda e, h=h: e.activation(out=st_b[:, h, P(h)], in_=st_b[:, h, P(h)], func=AF.Exp, scale=-1.0),
                         rd=[t_stb[h]], wr=[t_stb[h]])
                    for dc in range(2):
                        b = group([(vv[:, mc, (hd * 2 + dc) * 128:(hd * 2 + dc + 1) * 128], sq[:, pts[mc], P(h)], [t_kv, t_sq[pts[mc]]])
                                   for mc in range(2)], h=h)
                        E.op("dve", lambda e, hd=hd, dc=dc, h=h, b=b: e.tensor_tensor(
                            out=qo[:, hd * 2 + dc, H(h)], in0=ps[:, b, P(h)], in1=st_b[:, h, P(h)], op=ALU.mult),
                            rd=[t_bank[b], t_stb[h]], wr=[t_qo[hd]])

                att_a(0)
                for u in range(8):
                    if u + 1 < 8:
                        att_a(u + 1)
                    att_b(u)
                for jp in range(8):
                    T = {}
                    for br in (1, 2):
                        s = next_tile()
                        wg = ring[:, s, :].rearrange("p (k c) -> p k c", c=256)
                        for sc in range(2):
                            cs_ = slice(sc * 128, (sc + 1) * 128)
                            for h in range(2):
                                bg_ = group([(wg[:, k, cs_], hb[:, k, H(h)], [t_slot[s], thb[k][h]]) for k in range(16)], h=h)
                                i1 = ntmp()
                                E.op("act", lambda e, i1=i1, bg_=bg_: e.activation(out=tmpf[:, i1, P(h)], in_=ps[:, bg_, P(h)], func=AF.Sigmoid),
                                     rd=[t_bank[bg_]], wr=[t_tmp[i1]])
                                T[br, sc, h] = i1
                                if bg_conv:
                                    E.drain_bg("dve", G_DRAIN)
                        rel(1)
                    s = next_tile()
                    wbx = ring[:, s, :].rearrange("p (b k c) -> p b k c", b=2, c=256)
                    for sc in range(2):
                        j = jp * 2 + sc
                        cs_ = slice(sc * 128, (sc + 1) * 128)
                        for h in range(2):
                            i1 = T[1, sc, h]; i2 = T[2, sc, h]
                            byb = group([(wbx[:, 0, k, cs_], bact[:, k, H(h)], [t_slot[s], t_bact]) for k in range(8)], h=h)
                            E.op("dve", lambda e, i1=i1, byb=byb: e.tensor_tensor(out=tmpf[:, i1, P(h)], in0=ps[:, byb, P(h)],
                                                                                in1=tmpf[:, i1, P(h)], op=ALU.mult),
                                 rd=[t_bank[byb], t_tmp[i1]], wr=[t_tmp[i1]])
                            byx = group([(wbx[:, 1, k, cs_], qo[:, k, H(h)], [t_slot[s]] + t_qo) for k in range(8)], h=h)
                            E.op("dve", lambda e, i2=i2, byx=byx: e.tensor_tensor(out=tmpf[:, i2, P(h)], in0=ps[:, byx, P(h)],
                                                                                in1=tmpf[:, i2, P(h)], op=ALU.mult),
                                 rd=[t_bank[byx], t_tmp[i2]], wr=[t_tmp[i2]])
                            E.op("dve", lambda e, i1=i1, i2=i2, j=j, h=h: e.tensor_tensor(
                                out=merged[:, j, H(h)], in0=tmpf[:, i1, P(h)], in1=tmpf[:, i2, P(h)], op=ALU.add),
                                rd=[t_tmp[i1], t_tmp[i2]], wr=[t_mrg])
                    rel(1)
                    if bg_conv and jp == LN_AT:
                        E.drain_bg("pool")
                        E.drain_bg("dve")
                        ln_silu()
                    elif bg_conv and jp == LN_AT + 1:
                        while ln_defer:
                            ln_defer.pop(0)()
                while ln_defer:
                    ln_defer.pop(0)()
                for jq in range(4):
                    T = {}
                    for half4 in range(2):
                        s = next_tile()
                        wg = ring[:, s, :].rearrange("p (k c) -> p k c", c=256)
                        for sc in range(2):
                            q4 = half4 * 2 + sc
                            cg = slice(sc * 128, (sc + 1) * 128)
                            for h in range(2):
                                bg0 = group([(wg[:, k, cg], hb[:, k, H(h)], [t_slot[s], thb[k][h]]) for k in range(16)], h=h)
                                i1 = ntmp()
                                E.op("act", lambda e, i1=i1, bg0=bg0: e.activation(out=tmpf[:, i1, P(h)], in_=ps[:, bg0, P(h)], func=AF.Sigmoid),
                                     rd=[t_bank[bg0]], wr=[t_tmp[i1]])
                                T[q4, h] = i1
                        rel(1)
                    if bg_conv and jq == 0 and LN_AT < 0:
                        E.drain_bg("pool")
                        E.drain_bg("dve")
                        ln_silu()
                    s = next_tile()
                    wa = ring[:, s, :].rearrange("p (k c) -> p k c", c=512)
                    for q4 in range(4):
                        j = jq * 4 + q4
                        ca = slice(q4 * 128, (q4 + 1) * 128)
                        for h in range(2):
                            i1 = T[q4, h]
                            bya = group([(wa[:, k, ca], aact[:, k, H(h)], [t_slot[s], t_aact]) for k in range(8)], h=h)
                            E.op("dve", lambda e, i1=i1, bya=bya: e.tensor_tensor(out=tmpf[:, i1, P(h)], in0=ps[:, bya, P(h)],
                                                                                in1=tmpf[:, i1, P(h)], op=ALU.mult),
                                 rd=[t_bank[bya], t_tmp[i1]], wr=[t_tmp[i1]])
                            E.op("dve", lambda e, i1=i1, j=j, h=h: e.tensor_tensor(
                                out=merged[:, j, H(h)], in0=merged[:, j, H(h)], in1=tmpf[:, i1, P(h)], op=ALU.add),
                                rd=[t_tmp[i1]], wr=[t_mrg])
                    rel(1)
                for k, v in list(t_s2.w.items()) + list(t_s2.r.items()):
                    if t_s1.r.get(k, 0) < v:
                        t_s1.r[k] = v
                pend = []
                for t, s in enumerate(tiles(8)):
                    for sc in range(2):
                        j = t * 2 + sc
                        std_unit(s, sc, 16, lambda k, h: merged[:, k, H(h)], [t_mrg],
                                 lambda h, b, j=j: zevac_with_stats(zm, t_s1, j, h, b, j == 0, j == 15, pend))
                        flush_stats(pend, keep=2)
                flush_stats(pend)
                post_norm_update(l, C_GPOST, zm, t_s1)
                t_s2.w = dict(t_s1.w); t_s2.r = dict(t_s1.r)
                ffn_stats = ffn_pre(l)
                for tk in (t_s1, t_s2, t_aact, t_bact, t_kv) + tuple(t_qo):
                    for k, v in list(tk.w.items()) + list(tk.r.items()):
                        if t_hid.r.get(k, 0) < v:
                            t_hid.r[k] = v
                for t, s in enumerate(tiles(32)):
                    for sc in range(2):
                        c = t * 2 + sc

                        def ev_up(h, b, c=c):
                            i = ntmp()
                            E.op("act", lambda e: e.activation(out=tmpf[:, i, P(h)], in_=ps[:, b, P(h)], func=AF.Relu),
                                 rd=[t_bank[b]], wr=[t_tmp[i]])
                            E.op("dve", lambda e: e.tensor_tensor(out=hid[:, c, H(h)], in0=tmpf[:, i, P(h)], in1=tmpf[:, i, P(h)], op=ALU.mult),
                                 rd=[t_tmp[i]], wr=[t_hid])
                        std_unit(s, sc, 16, hrhs, hbtk, ev_up)
                    if t >= 2:
                        for _ in range(2):
                            if ffn_stats:
                                ffn_stats.pop(0)()
                for tk in t_hb_all + [t_mrg]:
                    for k, v in list(tk.w.items()) + list(tk.r.items()):
                        if t_z.r.get(k, 0) < v:
                            t_z.r[k] = v
                pend = []
                for jc in range(16):
                    sl = [next_tile(), next_tile()]
                    for h in range(2):
                        mms = []
                        for kg in range(2):
                            wv = ring[:, sl[kg], :].rearrange("p (k c) -> p k c", c=128)
                            mms += [(wv[:, k, :], hid[:, kg * 32 + k, H(h)], [t_slot[sl[kg]], t_hid]) for k in range(32)]
                        b = group(mms, h=h)
                        zevac_with_stats(zf, t_z, jc, h, b, jc == 0, jc == 15, pend)
                    flush_stats(pend, keep=2)
                    rel(2)
                flush_stats(pend)
                for tk in (t_s1, t_s2, t_aact, t_bact, t_kv) + tuple(t_qo):
                    tk.w = dict(t_hid.w); tk.r = dict(t_hid.r)
                post_norm_rstd(eps_tile=True)
                if l + 1 < n_layers:
                    kv_norm(l + 1)
                post_norm_xupdate(l, C_GMPOST, zf, t_z)
                for tk in t_hb_all:
                    tk.w = dict(t_z.w); tk.r = dict(t_z.r)
                t_mrg.w = dict(t_z.w); t_mrg.r = dict(t_z.r)
                if blk == 0:
                    for c in range(16):
                        E.op("dve", lambda e, c=c: e.tensor_scalar(out=xs[:, c, 0:HALO], in0=xs[:, c, 0:HALO],
                                                                  scalar1=hmask[:, 0:1], scalar2=None, op0=ALU.mult),
                             rd=[t_cst], wr=txs[c])
            if blk == 0:
                ev = E.dma("sp", "out", outT[:, :, 0:NT - HALO], xs[:, :, HALO:NT], rd=t_xs_all)
            else:
                ev = E.dma("sp", "out", outT[:, :, blk * NT - HALO:(blk + 1) * NT - HALO], xs[:], rd=t_xs_all)
            final_evs.append(ev)
        E.finalize(st, final_evs)
    return nc


def _std(W, c0, ncols=256, k0=0, nk=16):
    blk = W[k0 * 128:(k0 + nk) * 128, c0:c0 + ncols]
    return blk.reshape(nk, 128, ncols).transpose(1, 0, 2).reshape(128, nk * ncols)


def pack_layer_tiles(out, w_in, w_a_out, w_b_out, w_kv, w_x_out, w_o, w_up, w_down):
    i = 0

    def put(a):
        nonlocal i
        out[i] = a
        i += 1
    for t in range(8):
        put(_std(w_kv, t * 256))
    for base in (3072, 4096, 1024, 0, 2048, 5120):
        for t in range(4):
            put(_std(w_in, base + t * 256))
    for jp in range(8):
        put(_std(w_in, 8192 + jp * 256))
        put(_std(w_in, 10240 + jp * 256))
        put(np.concatenate([_std(w_b_out, jp * 256, nk=8), _std(w_x_out, jp * 256, nk=8)], axis=1))
    for jq in range(4):
        put(_std(w_in, 6144 + (2 * jq) * 256))
        put(_std(w_in, 6144 + (2 * jq + 1) * 256))
        put(_std(w_a_out, jq * 512, ncols=512, nk=8))
    for t in range(8):
        put(_std(w_o, t * 256))
    for t in range(32):
        put(_std(w_up, t * 256))
    for jc in range(16):
        for kg in range(2):
            put(_std(w_down, jc * 128, ncols=128, k0=kg * 32, nk=32))
    assert i == TPL


def pack_consts(layers, g_mix_pre, g_mix_post, g_mlp_pre, g_mlp_post, g_mem, conv_a_w, conv_a_b, ln_a_g, ln_a_b, conv_b_w):
    cst = np.zeros((128, len(layers) * C_PER), np.float32)
    for li, l in enumerate(layers):
        o = li * C_PER
        for col, g in ((C_GPRE, g_mix_pre), (C_GPOST, g_mix_post), (C_GMPRE, g_mlp_pre), (C_GMPOST, g_mlp_post), (C_GMEM, g_mem)):
            cst[:, o + col:o + col + 16] = g[l].reshape(16, 128).T
        cst[:, o + C_CAW:o + C_CAW + 8 * CK] = conv_a_w[l].reshape(CK, 8, 128).transpose(2, 1, 0).reshape(128, 8 * CK)
        cst[:, o + C_CAB:o + C_CAB + 8] = conv_a_b[l].reshape(8, 128).T
        cst[:, o + C_LNG:o + C_LNG + 8] = ln_a_g[l].reshape(8, 128).T
        cst[:, o + C_LNB:o + C_LNB + 8] = ln_a_b[l].reshape(8, 128).T
        cst[:, o + C_CBW:o + C_CBW + 24] = conv_b_w[l].reshape(3, 8, 128).transpose(2, 1, 0).reshape(128, 24)
    return cst


def shard_x(x2d, core):
    lo = core * TOK - HALO
    if lo < 0:
        blk = np.concatenate([np.zeros((HALO, D), np.float32), x2d[0:TOK]], axis=0)
    else:
        blk = x2d[lo:lo + TOK + HALO]
    return np.ascontiguousarray(blk.T.reshape(16, 128, TOK + HALO).transpose(1, 0, 2))


_PROG_CACHE = {}


def _get_prog(n_layers):
    if n_layers not in _PROG_CACHE:
        _PROG_CACHE[n_layers] = build_program(n_layers)
    return _PROG_CACHE[n_layers]


FUSED = True


def kernel(x, mem, g_mix_pre, w_in, conv_a_w, conv_a_b, ln_a_g, ln_a_b, w_a_out, conv_b_w, w_b_out,
           g_mem, w_kv, w_x_out, w_o, g_mix_post, g_mlp_pre, w_up, w_down, g_mlp_post):
    f = lambda a: np.asarray(a, dtype=np.float32)
    x2d = f(x)[0]
    memT = np.ascontiguousarray(f(mem)[0].T.reshape(16, 128, MEM).transpose(1, 0, 2))
    groups = [list(range(DEPTH))] if FUSED else [[l] for l in range(DEPTH)]
    for layers in groups:
        nl = len(layers)
        wts = np.empty((nl * TPL, 128, TILE), np.float32)
        for li, l in enumerate(layers):
            pack_layer_tiles(wts[li * TPL:(li + 1) * TPL], f(w_in[l]), f(w_a_out[l]), f(w_b_out[l]), f(w_kv[l]),
                             f(w_x_out[l]), f(w_o[l]), f(w_up[l]), f(w_down[l]))
        cst = pack_consts(layers, f(g_mix_pre), f(g_mix_post), f(g_mlp_pre), f(g_mlp_post), f(g_mem),
                          f(conv_a_w), f(conv_a_b), f(ln_a_g), f(ln_a_b), f(conv_b_w))
        nc = _get_prog(nl)
        in_maps = []
        for c in range(NCORE):
            in_maps.append({"xT": shard_x(x2d, c), "wts": wts, "cst": cst, "memT": memT,
                            "hmask": np.full((128, 1), 0.0 if c == 0 else 1.0, np.float32)})
        res = run_bass_kernel_spmd(nc, in_maps, core_ids=list(range(NCORE)))
        outs = []
        for c in range(NCORE):
            o = res.results[c]["outT"]
            outs.append(o.transpose(2, 1, 0).reshape(TOK, D))
        x2d = np.concatenate(outs, axis=0)
    return np.ascontiguousarray(x2d[None]).astype(np.float32)
```

```python
import numpy as np
from contextlib import ExitStack
import concourse.bass as bass
import concourse.mybir as mybir
from concourse.bass_utils import run_bass_kernel_spmd

F32 = mybir.dt.float32
BF16 = mybir.dt.bfloat16
AF = mybir.ActivationFunctionType
ALU = mybir.AluOpType

D = 2048
SEQ = 8192
DEPTH = 4
NCORE = 8
TOK = SEQ // NCORE
HALO = 128
NT = 576
NH = 288
NBLK = 2
MEM = 256
CK = 31
EPS = 1e-6
TPL = 140
TILE = 4096
NSLOT = 4
NTMP = 8
POOL_CHUNKS = 0
LN_AT = 5
TRIM = True
BG_RATE = 1
BALANCE = False
FAST_RECIP = False
LNEXP = True
KV_SPLIT = 3
Q_DRAIN = 4
G_DRAIN = 2
EARLY_CONV = True
LN_SKIP = 6
LN_DEFER = True
A_DRAIN = 4

C_GPRE, C_GPOST, C_GMPRE, C_GMPOST, C_GMEM = 0, 16, 32, 48, 64
C_CAW = 80
C_CAB = C_CAW + 8 * CK
C_LNG = C_CAB + 8
C_LNB = C_LNG + 8
C_CBW = C_LNB + 8
C_PER = C_CBW + 24

ENGS = ("pe", "act", "dve", "pool", "sp")


class Tk:
    __slots__ = ("w", "r")

    def __init__(self):
        self.w = {}
        self.r = {}


class _Rec:
    def __init__(self):
        self.call = None

    def __getattr__(self, name):
        def f(*a, **k):
            assert self.call is None
            self.call = (name, a, k)
        return f


def _eager(fn):
    r = _Rec()
    fn(r)
    name, a, k = r.call
    return lambda e: getattr(e, name)(*a, **k)


class Emitter:
    def __init__(self, nc, strict_same=False):
        self.nc = nc
        self.streams = {e: [] for e in ENGS}
        self.cnt = {e: 0 for e in ENGS}
        self.waited = {e: {} for e in ENGS}
        self.dma_cnt = {}
        self.strict_same = strict_same
        self.sems = {}
        self.bg = {"dve": [], "pool": []}
        self.bg_rate = {"dve": 0, "pool": 0}
        self.in_bg = False

    def _deps(self, rd, wr, extra):
        deps = {}
        for t in rd:
            for k, v in t.w.items():
                if deps.get(k, 0) < v:
                    deps[k] = v
        for t in wr:
            for d in (t.w, t.r):
                for k, v in d.items():
                    if deps.get(k, 0) < v:
                        deps[k] = v
        for d in extra:
            if d is None:
                continue
            for k, v in d.items():
                if deps.get(k, 0) < v:
                    deps[k] = v
        return deps

    def _waits(self, eng, deps, strict):
        waits = []
        for k, v in deps.items():
            if k == eng and not strict:
                continue
            if self.waited[eng].get(k, 0) >= v:
                continue
            self.waited[eng][k] = v
            waits.append((k, v))
        return waits

    def op(self, eng, fn, rd=(), wr=(), sig=True, extra=(), strict=None):
        strict = self.strict_same if strict is None else strict
        deps = self._deps(rd, wr, extra)
        waits = self._waits(eng, deps, strict)
        if sig:
            self.cnt[eng] += 1
            c = self.cnt[eng]
        else:
            c = self.cnt[eng] + 1
        self.streams[eng].append((waits, _eager(fn), 1 if sig else 0, eng))
        for t in rd:
            if t.r.get(eng, 0) < c:
                t.r[eng] = c
        for t in wr:
            t.w = {eng: c}
            t.r = {}
        ev = {eng: c}
        if eng == "dve" and self.bg["dve"] and not self.in_bg:
            self.drain_bg("dve", self.bg_rate["dve"])
        return ev

    def drain_bg(self, eng, n=None):
        self.in_bg = True
        k = 0
        q = self.bg[eng]
        while q and (n is None or k < n):
            q.pop(0)()
            k += 1
        self.in_bg = False

    def dma(self, q, slot, out, in_, rd=(), wr=(), extra=()):
        key = "dma:" + slot
        deps = self._deps(rd, wr, extra)
        waits = self._waits(q, deps, False)
        self.dma_cnt[key] = self.dma_cnt.get(key, 0) + 16
        c = self.dma_cnt[key]
        self.streams[q].append((waits, lambda e, o=out, i=in_: e.dma_start(out=o, in_=i), 16, key))
        for t in rd:
            if t.r.get(key, 0) < c:
                t.r[key] = c
        for t in wr:
            t.w = {key: c}
            t.r = {}
        if q == "pool" and self.bg["pool"] and not self.in_bg:
            self.drain_bg("pool", self.bg_rate["pool"])
        return {key: c}

    def finalize(self, st, final_waits):
        nc = self.nc
        keys = list(ENGS) + sorted(self.dma_cnt.keys())
        for k in keys:
            self.sems[k] = st.enter_context(nc.semaphore("s_" + k.replace(":", "_")))
        block = st.enter_context(nc.Block())
        fin = {}
        for d in final_waits:
            for k, v in d.items():
                fin[k] = max(fin.get(k, 0), v)

        def replay(eng_name):
            def run(e):
                for waits, fn, inc, semkey in self.streams[eng_name]:
                    for k, v in waits:
                        e.wait_ge(self.sems[k], v)
                    ins = fn(e)
                    if inc:
                        ins.then_inc(self.sems[semkey], inc)
                if eng_name == "sp":
                    for k, v in fin.items():
                        e.wait_ge(self.sems[k], v)
            return run

        block.tensor(replay("pe"))
        block.scalar(replay("act"))
        block.vector(replay("dve"))
        block.gpsimd(replay("pool"))
        block.sync(replay("sp"))


def build_program(n_layers, nblk=NBLK, bg_conv=True):
    nc = bass.Bass("TRN2", target_bir_lowering=False)
    xT = nc.dram_tensor("xT", [128, 16, nblk * NT], F32, kind="ExternalInput").ap()
    wts = nc.dram_tensor("wts", [n_layers * TPL, 128, TILE], F32, kind="ExternalInput").ap()
    cstd = nc.dram_tensor("cst", [128, n_layers * C_PER], F32, kind="ExternalInput").ap()
    memTd = nc.dram_tensor("memT", [128, 16, MEM], F32, kind="ExternalInput").ap()
    hmaskd = nc.dram_tensor("hmask", [128, 1], F32, kind="ExternalInput").ap()
    outT = nc.dram_tensor("outT", [128, 16, nblk * NT - HALO], F32, kind="ExternalOutput").ap()

    st = ExitStack()
    with st:
        def sb(name, shape, dt):
            return st.enter_context(nc.sbuf_tensor(name, shape, dt))

        cst = sb("cst_sb", [128, n_layers * C_PER], F32)
        hmask = sb("hmask_sb", [128, 1], F32)
        xs = sb("xs", [128, 16, NT], F32)
        ZA = sb("ZA", [128, 16 * NT], F32)
        HA = sb("HA", [128, 64 * NT], BF16)
        ring = sb("ring", [128, NSLOT, TILE], BF16)
        ones = sb("ones", [128, 128], BF16)
        sq = sb("sq", [128, 8, NH], BF16)
        st_a = sb("st_a", [128, 2, NH], F32)
        st_b = sb("st_b", [128, 2, NH], F32)
        st_c = sb("st_c", [128, 2, NH], F32)
        tmpf = sb("tmpf", [128, NTMP, NH], F32)
        ptmp = sb("ptmp", [128, NH], F32) if POOL_CHUNKS > 0 else None
        ahist = sb("ahist", [128, n_layers, 8, 30], F32)
        uhist = sb("uhist", [128, n_layers, 8, 2], F32)
        ps = st.enter_context(nc.psum_tensor("ps", [128, 8, 512], F32))

        ZAb = ZA[:].bitcast(BF16)
        hb = ZAb[:, 0:16 * NT].rearrange("p (c n) -> p c n", c=16)
        merged = ZAb[:, 16 * NT:32 * NT].rearrange("p (c n) -> p c n", c=16)
        u32 = ZA[:, 8 * NT:16 * NT].rearrange("p (c n) -> p c n", c=8)
        zf = ZA[:].rearrange("p (c n) -> p c n", c=16)
        HAf = HA[:].bitcast(F32)
        o_s1 = 0
        n_s1 = 8 * (NT + 32)
        o_s2 = n_s1
        n_s2 = 8 * NT
        s1 = HAf[:, o_s1:o_s1 + n_s1].rearrange("p (c n) -> p c n", c=8)
        s2 = HAf[:, o_s2:o_s2 + n_s2].rearrange("p (c n) -> p c n", c=8)
        zm = HAf[:, 0:16 * NT].rearrange("p (c n) -> p c n", c=16)
        ob16 = 2 * (n_s1 + n_s2)
        aact = HA[:, ob16:ob16 + 8 * NT].rearrange("p (c n) -> p c n", c=8)
        bact = HA[:, ob16 + 8 * NT:ob16 + 16 * NT].rearrange("p (c n) -> p c n", c=8)
        qo = HA[:, ob16 + 16 * NT:ob16 + 24 * NT].rearrange("p (c n) -> p c n", c=8)
        okv = ob16 + 24 * NT
        kT = HA[:, okv:okv + 8 * MEM].rearrange("p (c n) -> p c n", c=8)
        vv = HA[:, okv + 8 * MEM:okv + 16 * MEM].rearrange("p (c n) -> p c n", c=2)
        assert okv + 16 * MEM <= 64 * NT
        hid = HA[:].rearrange("p (c n) -> p c n", c=64)
        memf = HAf[:, 0:16 * MEM].rearrange("p (c n) -> p c n", c=16)
        memn = HA[:, 2 * o_s2:2 * o_s2 + 16 * MEM].rearrange("p (c n) -> p c n", c=16)

        E = Emitter(nc)
        t_cst = Tk(); txs = [[Tk(), Tk()] for _ in range(16)]; t_xs_all = [t for p in txs for t in p]; thb = [[Tk(), Tk()] for _ in range(16)]; t_hb_all = [t for p in thb for t in p]; t_mrg = Tk(); t_z = Tk()
        t_s1 = Tk(); t_s2 = Tk(); t_aact = Tk(); t_bact = Tk(); t_qo = [Tk() for _ in range(4)]
        t_kv = Tk(); t_hid = Tk(); t_mem = Tk(); t_memn = Tk()
        t_slot = [Tk() for _ in range(NSLOT)]
        t_bank = [Tk() for _ in range(8)]
        t_sq = [Tk() for _ in range(8)]
        t_sta = [Tk(), Tk()]; t_stb = [Tk(), Tk()]; t_stc = [Tk(), Tk()]
        t_tmp = [Tk() for _ in range(NTMP)]
        t_s2p = Tk(); t_ptmp = Tk()
        t_hist = Tk(); t_ones = Tk()
        rr = {"bank": 0, "sq": 0, "tmp": 0, "pt": 0, "tile": 0, "done": 0, "skip": set(), "skipn": 0}

        cur = {"lo": 0}
        LO = [8, 38, 68, 98]

        def MID():
            lo = cur["lo"]
            if lo == 0 or not BALANCE:
                return NH
            return lo + ((NT - lo) // 4) * 2 + ((NT - lo) % 4 > 0) * 2 if False else max(NH, ((lo + NT) // 4) * 2)

        def H(h):
            return slice(cur["lo"], MID()) if h == 0 else slice(MID(), NT)

        def P(h):
            sl = H(h)
            return slice(0, sl.stop - sl.start)

        def cs(l, col):
            c0 = l * C_PER + col
            return cst[:, c0:c0 + 1]

        def nbank():
            while True:
                b = rr["bank"]
                rr["bank"] = (b + 1) % 6
                if rr["skipn"] > 0 and b in rr["skip"]:
                    continue
                break
            if rr["skipn"] > 0:
                rr["skipn"] -= 1
                if rr["skipn"] == 0:
                    rr["skip"] = set()
            return b

        def nsq():
            i = rr["sq"]; rr["sq"] = (i + 1) % 8
            return i

        def ntmp():
            i = rr["tmp"]; rr["tmp"] = (i + 1) % NTMP
            return i


        E.dma("sp", "cst", cst[:], cstd[:], wr=[t_cst])
        E.dma("sp", "hmask", hmask[:], hmaskd[:], wr=[t_cst])
        E.op("dve", lambda e: e.memset(ones[:], 1.0), wr=[t_ones])
        E.op("dve", lambda e: e.memset(HAf[:, 0:n_s1 + n_s2], 0.0), wr=[t_s1, t_s2])
        E.op("dve", lambda e: e.memset(ahist[:], 0.0), wr=[t_hist])
        E.op("dve", lambda e: e.memset(uhist[:], 0.0), wr=[t_hist])

        tile_state = {"next_load": 0, "order": []}

        def prefetch(upto):
            while tile_state["next_load"] < min(upto, len(tile_state["order"])):
                i = tile_state["next_load"]
                s = i % NSLOT
                E.dma("pool", "w%d" % s, ring[:, s, :], wts[tile_state["order"][i]], wr=[t_slot[s]])
                tile_state["next_load"] += 1

        def next_tile():
            i = rr["tile"]
            rr["tile"] += 1
            assert i - rr["done"] < NSLOT
            prefetch(i + 1)
            return i % NSLOT

        def tiles(n):
            for _ in range(n):
                s_ = next_tile()
                yield s_
                rel(1)

        def rel(n=1):
            rr["done"] += n
            prefetch(rr["done"] + NSLOT)

        def group(mms, h=None, nout=None, bank=None):
            b = nbank() if bank is None else bank
            n = len(mms)
            psl = P(h) if h is not None else slice(0, nout)
            for i, (l_ap, r_ap, rdt) in enumerate(mms):
                E.op("pe", lambda e, l_ap=l_ap, r_ap=r_ap, i=i, b=b: e.matmul(
                    ps[:, b, psl], lhsT=l_ap, rhs=r_ap, start=(i == 0), stop=(i == n - 1)),
                    rd=rdt, wr=[t_bank[b]] if i == 0 else [], sig=(i == n - 1))
            t_bank[b].w = {"pe": E.cnt["pe"]}
            return b

        def colsum(srcs, bank, h=None, nout=None):
            return group([(ones[:], s_ap, [t_ones] + tk) for s_ap, tk in srcs], h=h, nout=nout, bank=bank)

        def rstd_from_bank(bank, h, dst, t_dst, inv_n, nout=None):
            sl = P(h) if nout is None else slice(0, nout)
            if LNEXP:
                E.op("act", lambda e: e.activation(out=dst[:, h, sl], in_=ps[:, bank, sl], func=AF.Ln,
                                                   bias=cst_eps[:], scale=inv_n),
                     rd=[t_bank[bank], t_cst], wr=[t_dst])
                E.op("act", lambda e: e.activation(out=dst[:, h, sl], in_=dst[:, h, sl], func=AF.Exp, scale=-0.5),
                     rd=[t_dst], wr=[t_dst])
                return
            E.op("act", lambda e: e.activation(out=dst[:, h, sl], in_=ps[:, bank, sl], func=AF.Sqrt,
                                               bias=cst_eps[:], scale=inv_n),
                 rd=[t_bank[bank], t_cst], wr=[t_dst])
            E.op("dve", lambda e: e.reciprocal(out=dst[:, h, sl], in_=dst[:, h, sl]), rd=[t_dst], wr=[t_dst])

        st_mem = sb("st_mem", [128, 1, MEM], F32)
        t_stmem = Tk()
        cst_eps = sb("cst_eps", [128, 1], F32)
        E.op("dve", lambda e: e.memset(cst_eps[:], EPS), wr=[t_cst])

        def rms_pre(l, gcol):
            for h in range(2):
                b = 6 + h
                for c in range(16):
                    i = nsq()
                    E.op("act", lambda e, c=c, i=i, h=h: e.activation(out=sq[:, i, P(h)], in_=xs[:, c, H(h)], func=AF.Square),
                         rd=[txs[c][h]], wr=[t_sq[i]])
                    E.op("pe", lambda e, i=i, c=c, b=b: e.matmul(
                        ps[:, b, P(h)], lhsT=ones[:], rhs=sq[:, i, P(h)], start=(c == 0), stop=(c == 15)),
                        rd=[t_ones, t_sq[i]], wr=[t_bank[b]] if c == 0 else [], sig=True)
                t_bank[b].w = {"pe": E.cnt["pe"]}
                rstd_from_bank(b, h, st_a, t_sta[h], 1.0 / D)
                for c in range(16):
                    E.op("dve", lambda e, c=c, h=h: e.scalar_tensor_tensor(
                        out=hb[:, c, H(h)], in0=xs[:, c, H(h)], scalar=cs(l, gcol + c), in1=st_a[:, h, P(h)],
                        op0=ALU.mult, op1=ALU.mult), rd=[txs[c][h], t_sta[h], t_cst], wr=[thb[c][h]])

        def std_unit(slot, sc, nk, rhs_fn, rhs_tk, evac, ncols=256, koff=0):
            wv = ring[:, slot, :].rearrange("p (k c) -> p k c", c=ncols)
            for h in range(2):
                b = group([(wv[:, koff + k, sc * 128:(sc + 1) * 128], rhs_fn(k, h),
                            [t_slot[slot]] + (rhs_tk(k, h) if callable(rhs_tk) else rhs_tk))
                           for k in range(nk)], h=h)
                evac(h, b)

        def post_norm_rstd(eps_tile=False):
            for h in range(2):
                if eps_tile:
                    b = 6 + h
                    E.op("dve", lambda e, h=h, b=b: e.scalar_tensor_tensor(
                        out=st_a[:, h, P(h)], in0=ps[:, b, P(h)], scalar=1.0 / D, in1=st_b[:, h, P(h)],
                        op0=ALU.mult, op1=ALU.add), rd=[t_bank[b], t_stb[h]], wr=[t_sta[h]])
                    E.op("act", lambda e, h=h: e.activation(out=st_a[:, h, P(h)], in_=st_a[:, h, P(h)], func=AF.Ln),
                         rd=[t_sta[h]], wr=[t_sta[h]])
                    E.op("act", lambda e, h=h: e.activation(out=st_a[:, h, P(h)], in_=st_a[:, h, P(h)], func=AF.Exp, scale=-0.5),
                         rd=[t_sta[h]], wr=[t_sta[h]])
                else:
                    rstd_from_bank(6 + h, h, st_a, t_sta[h], 1.0 / D)

        def post_norm_xupdate(l, gcol, z, t_zz, halves=(0, 1)):
            for h in halves:
                for c in range(16):
                    i = ntmp()
                    E.op("dve", lambda e, c=c, h=h, i=i: e.scalar_tensor_tensor(
                        out=tmpf[:, i, P(h)], in0=z[:, c, H(h)], scalar=cs(l, gcol + c), in1=st_a[:, h, P(h)],
                        op0=ALU.mult, op1=ALU.mult), rd=[t_zz, t_sta[h], t_cst], wr=[t_tmp[i]])
                    E.op("dve", lambda e, c=c, h=h, i=i: e.tensor_tensor(
                        out=xs[:, c, H(h)], in0=xs[:, c, H(h)], in1=tmpf[:, i, P(h)], op=ALU.add),
                        rd=[t_tmp[i]], wr=[txs[c][h]])

        def post_norm_update(l, gcol, z, t_zz, eps_tile=False):
            post_norm_rstd(eps_tile)
            post_norm_xupdate(l, gcol, z, t_zz)

        def ffn_pre(l):
            for h in range(2):
                for c in range(16):
                    E.op("act", lambda e, c=c, h=h: e.activation(out=hb[:, c, H(h)], in_=xs[:, c, H(h)], func=AF.Copy,
                                                               scale=cs(l, C_GMPRE + c)),
                         rd=[txs[c][h], t_cst], wr=[thb[c][h]])
            th = []
            lag = []

            def pe_one():
                (i, c, h) = lag.pop(0)
                b = 6 + h
                E.op("pe", lambda e: e.matmul(ps[:, b, P(h)], lhsT=ones[:], rhs=sq[:, i, P(h)], start=(c == 0), stop=(c == 15)),
                     rd=[t_ones, t_sq[i]], wr=[t_bank[b]] if c == 0 else [], sig=True)
                t_bank[b].w = {"pe": E.cnt["pe"]}
                if c == 15:
                    E.op("dve", lambda e: e.tensor_scalar(out=st_b[:, h, P(h)], in0=ps[:, b, P(h)], scalar1=1.0 / D,
                                                          scalar2=EPS, op0=ALU.mult, op1=ALU.add),
                         rd=[t_bank[b]], wr=[t_stb[h]])
                    E.op("dve", lambda e: e.tensor_tensor(out=st_b[:, h, P(h)], in0=st_b[:, h, P(h)], in1=st_b[:, h, P(h)], op=ALU.mult),
                         rd=[t_stb[h]], wr=[t_stb[h]])
                    E.op("dve", lambda e: e.tensor_scalar(out=st_b[:, h, P(h)], in0=st_b[:, h, P(h)], scalar1=EPS,
                                                          scalar2=None, op0=ALU.mult),
                         rd=[t_stb[h]], wr=[t_stb[h]])
            for h in range(2):
                for c in range(16):
                    def f(c=c, h=h):
                        i = nsq()
                        E.op("act", lambda e: e.activation(out=sq[:, i, P(h)], in_=xs[:, c, H(h)], func=AF.Square),
                             rd=[txs[c][h]], wr=[t_sq[i]])
                        lag.append((i, c, h))
                        if len(lag) > 3:
                            pe_one()
                    th.append(f)

            def fin():
                while lag:
                    pe_one()
            th.append(fin)
            return th

        def zevac_with_stats(z, t_zz, j, h, b, first, last, pend):
            E.op("act", lambda e: e.activation(out=z[:, j, H(h)], in_=ps[:, b, P(h)], func=AF.Copy),
                 rd=[t_bank[b]], wr=[t_zz])
            i = nsq()
            E.op("act", lambda e: e.activation(out=sq[:, i, P(h)], in_=ps[:, b, P(h)], func=AF.Square),
                 rd=[t_bank[b]], wr=[t_sq[i]])
            pend.append((i, h, first, last))

        def flush_stats(pend, keep=0):
            while len(pend) > keep:
                i, h, first, last = pend.pop(0)
                b = 6 + h
                E.op("pe", lambda e, i=i, b=b, first=first, last=last: e.matmul(
                    ps[:, b, P(h)], lhsT=ones[:], rhs=sq[:, i, P(h)], start=first, stop=last),
                    rd=[t_ones, t_sq[i]], wr=[t_bank[b]] if first else [], sig=True)
                t_bank[b].w = {"pe": E.cnt["pe"]}

        kv_state = {"have_rstd": False}

        def kv_norm(l):
            E.dma("sp", "mem", memf[:], memTd[:], wr=[t_s1])
            if not kv_state["have_rstd"]:
                kv_state["have_rstd"] = True
                b = nbank()
                for c in range(16):
                    i = nsq()
                    E.op("act", lambda e, c=c, i=i: e.activation(out=sq[:, i, 0:MEM], in_=memf[:, c, :], func=AF.Square),
                         rd=[t_s1], wr=[t_sq[i]])
                    E.op("pe", lambda e, i=i, c=c, b=b: e.matmul(
                        ps[:, b, 0:MEM], lhsT=ones[:], rhs=sq[:, i, 0:MEM], start=(c == 0), stop=(c == 15)),
                        rd=[t_ones, t_sq[i]], wr=[t_bank[b]] if c == 0 else [], sig=True)
                t_bank[b].w = {"pe": E.cnt["pe"]}
                rstd_from_bank(b, 0, st_mem, t_stmem, 1.0 / D, nout=MEM)
            for c in range(16):
                E.op("dve", lambda e, c=c, g_ap=cs(l, C_GMEM + c): e.scalar_tensor_tensor(
                    out=memn[:, c, :], in0=memf[:, c, :], scalar=g_ap, in1=st_mem[:, 0, 0:MEM],
                    op0=ALU.mult, op1=ALU.mult), rd=[t_s1, t_stmem, t_cst], wr=[t_s2])

        def kv_tiles(t0, t1):
            for t8 in range(t0, t1):
                s = next_tile()
                wv = ring[:, s, :].rearrange("p (k c) -> p k c", c=256)
                if t8 < 4:
                    t = t8
                    for sc in range(2):
                        dch = t * 2 + sc
                        b = group([(wv[:, k, sc * 128:(sc + 1) * 128], memn[:, k, :], [t_slot[s], t_s2])
                                   for k in range(16)], nout=MEM)
                        E.op("act", lambda e, dch=dch, b=b: e.activation(out=kT[:, dch, :], in_=ps[:, b, 0:MEM], func=AF.Copy),
                             rd=[t_bank[b]], wr=[t_kv])
                else:
                    t = t8 - 4
                    for mc in range(2):
                        b = group([(memn[:, k, mc * 128:(mc + 1) * 128], wv[:, k, :], [t_slot[s], t_s2])
                                   for k in range(16)], nout=256)
                        E.op("act", lambda e, t=t, mc=mc, b=b: e.activation(out=vv[:, mc, t * 256:(t + 1) * 256],
                                                                         in_=ps[:, b, 0:256], func=AF.Copy),
                             rd=[t_bank[b]], wr=[t_kv])
                rel(1)

        final_evs = []
        for blk in range(nblk):
            base_i = len(tile_state["order"])
            tile_state["order"].extend(list(range(n_layers * TPL)))
            prefetch(rr["done"] + NSLOT)
            E.dma("sp", "xin", xs[:], xT[:, :, blk * NT:(blk + 1) * NT], wr=t_xs_all)
            for l in range(n_layers):
                cur["lo"] = LO[l + DEPTH - n_layers] if (blk == 0 and TRIM) else 0
                if l == 0:
                    kv_norm(0)
                kv_tiles(0, KV_SPLIT)
                rms_pre(l, C_GPRE)
                kv_tiles(KV_SPLIT, 8)
                hrhs = lambda k, h: hb[:, k, H(h)]
                hbtk = lambda k, h: [thb[k][h]]
                for t, s in enumerate(tiles(4)):
                    for sc in range(2):
                        c = t * 2 + sc
                        std_unit(s, sc, 16, hrhs, hbtk, lambda h, b, c=c: E.op(
                            "act", lambda e: e.activation(out=s2[:, c, H(h)], in_=ps[:, b, P(h)], func=AF.Copy),
                            rd=[t_bank[b]], wr=[t_s2]))
                E.op("dve", lambda e, l=l: e.tensor_copy(out=s1[:, :, 30:32], in_=uhist[:, l, :, :]),
                     rd=[t_hist], wr=[t_s1], strict=True)
                for t, s in enumerate(tiles(4)):
                    for sc in range(2):
                        c = t * 2 + sc
                        std_unit(s, sc, 16, hrhs, hbtk, lambda h, b, c=c: E.op(
                            "dve", lambda e: e.tensor_tensor(out=s1[:, c, 32 + H(h).start:32 + H(h).stop], in0=ps[:, b, P(h)],
                                                             in1=s2[:, c, H(h)], op=ALU.mult),
                            rd=[t_bank[b], t_s2], wr=[t_s1]))
                E.op("dve", lambda e, l=l: e.tensor_copy(out=uhist[:, l, :, :], in_=s1[:, :, 32 + NT - 2:32 + NT]),
                     rd=[t_s1], wr=[t_hist], strict=True)
                for c in range(8):
                    for h in range(2):
                        E.op("dve", lambda e, c=c, h=h, w_ap=cs(l, C_CBW + c * 3 + 0): e.tensor_scalar(
                            out=u32[:, c, H(h)], in0=s1[:, c, 30 + H(h).start:30 + H(h).stop],
                            scalar1=w_ap, scalar2=None, op0=ALU.mult),
                            rd=[t_s1, t_cst], wr=[t_mrg])
                        for k in (1, 2):
                            E.op("dve", lambda e, c=c, h=h, k=k, w_ap=cs(l, C_CBW + c * 3 + k): e.scalar_tensor_tensor(
                                out=u32[:, c, H(h)], in0=s1[:, c, 30 + k + H(h).start:30 + k + H(h).stop],
                                scalar=w_ap, in1=u32[:, c, H(h)], op0=ALU.mult, op1=ALU.add),
                                rd=[t_s1, t_cst], wr=[t_mrg])
                for t, s in enumerate(tiles(4)):
                    for sc in range(2):
                        c = t * 2 + sc
                        std_unit(s, sc, 16, hrhs, hbtk, lambda h, b, c=c: E.op(
                            "act", lambda e: e.activation(out=s2[:, c, H(h)], in_=ps[:, b, P(h)], func=AF.Sigmoid),
                            rd=[t_bank[b]], wr=[t_s2]))
                E.op("dve", lambda e, l=l: e.tensor_copy(out=s1[:, :, 2:32], in_=ahist[:, l, :, :]),
                     rd=[t_hist], wr=[t_s1], strict=True)
                def conv_chunk_ops(c, l=l):
                    ops = []

                    def first():
                        E.op("dve", lambda e: e.tensor_scalar(
                            out=s2[:, c, cur["lo"]:NT], in0=s1[:, c, 2 + cur["lo"]:2 + NT],
                            scalar1=cs(l, C_CAW + c * CK), scalar2=cs(l, C_CAB + c), op0=ALU.mult, op1=ALU.add),
                            rd=[t_s1, t_cst], wr=[t_s2])
                    ops.append(first)
                    for k in range(1, CK):
                        def tap(k=k):
                            E.op("dve", lambda e: e.scalar_tensor_tensor(
                                out=s2[:, c, cur["lo"]:NT], in0=s1[:, c, 2 + k + cur["lo"]:2 + k + NT],
                                scalar=cs(l, C_CAW + c * CK + k), in1=s2[:, c, cur["lo"]:NT], op0=ALU.mult, op1=ALU.add),
                                rd=[t_s1, t_cst], wr=[t_s2])
                        ops.append(tap)
                    return ops

                early = bg_conv and EARLY_CONV and POOL_CHUNKS == 0
                if early:
                    E.bg_rate["dve"] = BG_RATE
                for t, s in enumerate(tiles(4)):
                    for sc in range(2):
                        c = t * 2 + sc
                        std_unit(s, sc, 16, hrhs, hbtk, lambda h, b, c=c: E.op(
                            "dve", lambda e: e.tensor_tensor(out=s1[:, c, 32 + H(h).start:32 + H(h).stop], in0=ps[:, b, P(h)],
                                                             in1=s2[:, c, H(h)], op=ALU.mult),
                            rd=[t_bank[b], t_s2], wr=[t_s1]))
                        if early:
                            E.drain_bg("dve", A_DRAIN)
                    if early:
                        E.bg["dve"].extend(conv_chunk_ops(2 * t) + conv_chunk_ops(2 * t + 1))
                E.op("dve", lambda e, l=l: e.tensor_copy(out=ahist[:, l, :, :], in_=s1[:, :, 32 + NT - 30:32 + NT]),
                     rd=[t_s1], wr=[t_hist], strict=True)

                if cur["lo"] > 0:
                    cur["lo"] += 30
                NDC = 8 - POOL_CHUNKS
                t_s2p.w = dict(t_s2.w); t_s2p.r = dict(t_s2.r)

                def conv_ops(l=l):
                    ops = []
                    for c in range(NDC):
                        ops += conv_chunk_ops(c)
                    return ops

                def conv_ops_pool(l=l):
                    ops = []
                    for c in range(NDC, 8):
                        for h in range(2):
                            def first(c=c, h=h):
                                E.op("pool", lambda e: e.tensor_scalar(
                                    out=s2[:, c, H(h)], in0=s1[:, c, 2 + H(h).start:2 + H(h).stop],
                                    scalar1=cs(l, C_CAW + c * CK), scalar2=cs(l, C_CAB + c), op0=ALU.mult, op1=ALU.add),
                                    rd=[t_s1, t_cst], wr=[t_s2p])
                            ops.append(first)
                            for k in range(1, CK):
                                def tap(c=c, h=h, k=k):
                                    E.op("pool", lambda e: e.tensor_scalar(
                                        out=ptmp[:], in0=s1[:, c, 2 + k + H(h).start:2 + k + H(h).stop],
                                        scalar1=cs(l, C_CAW + c * CK + k), scalar2=None, op0=ALU.mult),
                                        rd=[t_s1, t_cst], wr=[t_ptmp])
                                    E.op("pool", lambda e: e.tensor_tensor(
                                        out=s2[:, c, H(h)], in0=s2[:, c, H(h)], in1=ptmp[:], op=ALU.add),
                                        rd=[t_ptmp], wr=[t_s2p])
                                ops.append(tap)
                    return ops

                ln_defer = []

                def ln_silu(l=l):
                    bsum = [6, 7]
                    bsq = [nbank(), nbank()]
                    rr["skip"] = set(bsq); rr["skipn"] = LN_SKIP
                    lag = []

                    def pe_flush(keep):
                        while len(lag) > keep:
                            (i, bnk, h, first, last) = lag.pop(0)
                            E.op("pe", lambda e, i=i, bnk=bnk, h=h, first=first, last=last: e.matmul(
                                ps[:, bnk, P(h)], lhsT=ones[:], rhs=sq[:, i, P(h)], start=first, stop=last),
                                rd=[t_ones, t_sq[i]], wr=[t_bank[bnk]] if first else [], sig=True)
                            t_bank[bnk].w = {"pe": E.cnt["pe"]}
                    for h in range(2):
                        for c in range(8):
                            i = nsq()
                            E.op("act", lambda e, c=c, i=i, h=h: e.activation(out=sq[:, i, P(h)], in_=s2[:, c, H(h)], func=AF.Copy),
                                 rd=[t_s2, t_s2p], wr=[t_sq[i]])
                            lag.append((i, bsum[h], h, c == 0, c == 7))
                            j = nsq()
                            E.op("act", lambda e, c=c, j=j, h=h: e.activation(out=sq[:, j, P(h)], in_=s2[:, c, H(h)], func=AF.Square),
                                 rd=[t_s2, t_s2p], wr=[t_sq[j]])
                            lag.append((j, bsq[h], h, c == 0, c == 7))
                            pe_flush(4)
                    pe_flush(0)
                    for h in range(2):
                        E.op("dve", lambda e, h=h: e.tensor_scalar(
                            out=st_b[:, h, P(h)], in0=ps[:, bsum[h], P(h)], scalar1=1.0 / 1024, scalar2=None, op0=ALU.mult),
                            rd=[t_bank[bsum[h]]], wr=[t_stb[h]])
                        E.op("dve", lambda e, h=h: e.tensor_tensor(out=st_a[:, h, P(h)], in0=st_b[:, h, P(h)], in1=st_b[:, h, P(h)], op=ALU.mult),
                             rd=[t_stb[h]], wr=[t_sta[h]])
                        E.op("dve", lambda e, h=h: e.scalar_tensor_tensor(
                            out=st_c[:, h, P(h)], in0=ps[:, bsq[h], P(h)], scalar=1.0 / 1024, in1=st_a[:, h, P(h)],
                            op0=ALU.mult, op1=ALU.subtract), rd=[t_bank[bsq[h]], t_sta[h]], wr=[t_stc[h]])
                        E.op("act", lambda e, h=h: e.activation(out=st_c[:, h, P(h)], in_=st_c[:, h, P(h)], func=AF.Ln,
                                                             bias=cst_eps[:], scale=1.0), rd=[t_stc[h], t_cst], wr=[t_stc[h]])
                        E.op("act", lambda e, h=h: e.activation(out=st_c[:, h, P(h)], in_=st_c[:, h, P(h)], func=AF.Exp, scale=-0.5),
                             rd=[t_stc[h]], wr=[t_stc[h]])
                    for h in range(2):
                        for c in range(8):
                            E.op("dve", lambda e, c=c, h=h: e.tensor_tensor(out=s2[:, c, H(h)], in0=s2[:, c, H(h)],
                                                                          in1=st_b[:, h, P(h)], op=ALU.subtract),
                                 rd=[t_stb[h], t_s2p], wr=[t_s2])
                            E.op("dve", lambda e, c=c, h=h: e.tensor_tensor(out=s2[:, c, H(h)], in0=s2[:, c, H(h)],
                                                                          in1=st_c[:, h, P(h)], op=ALU.mult),
                                 rd=[t_stc[h]], wr=[t_s2])
                            def silu_op(c=c, h=h):
                                E.op("act", lambda e: e.activation(out=aact[:, c, H(h)], in_=s2[:, c, H(h)], func=AF.Silu,
                                                                   bias=cs(l, C_LNB + c), scale=cs(l, C_LNG + c)),
                                     rd=[t_s2, t_cst], wr=[t_aact])
                            if LN_DEFER and bg_conv:
                                ln_defer.append(silu_op)
                            else:
                                silu_op()
                    for k_, v_ in list(t_s2p.r.items()) + list(t_s2p.w.items()):
                        if t_s2.r.get(k_, 0) < v_:
                            t_s2.r[k_] = v_

                cops = [] if early else conv_ops()
                pops = conv_ops_pool()
                if bg_conv:
                    if not early:
                        E.bg["dve"] = cops
                    E.bg_rate["dve"] = BG_RATE
                    E.bg["pool"] = pops
                    E.bg_rate["pool"] = 14
                else:
                    for f in pops:
                        f()
                    for f in cops:
                        f()
                    ln_silu()

                for t, s in enumerate(tiles(4)):
                    for sc in range(2):
                        c = t * 2 + sc
                        std_unit(s, sc, 16, hrhs, hbtk, lambda h, b, c=c: E.op(
                            "dve", lambda e: e.tensor_tensor(out=bact[:, c, H(h)], in0=ps[:, b, P(h)],
                                                             in1=u32[:, c, H(h)], op=ALU.mult),
                            rd=[t_bank[b], t_mrg], wr=[t_bact]))
                        if bg_conv:
                            E.drain_bg("dve", A_DRAIN)
                for t, s in enumerate(tiles(4)):
                    for sc in range(2):
                        c = t * 2 + sc
                        std_unit(s, sc, 16, hrhs, hbtk, lambda h, b, c=c: E.op(
                            "act", lambda e: e.activation(out=qo[:, c, H(h)], in_=ps[:, b, P(h)], func=AF.Copy),
                            rd=[t_bank[b]], wr=[t_qo[c // 2]]))
                        if bg_conv:
                            E.drain_bg("dve", Q_DRAIN)
                units = [(hd, h) for hd in range(4) for h in range(2)]
                upts = {}

                def att_a(u):
                    hd, h = units[u]
                    pts = []
                    for mc in range(2):
                        b = group([(kT[:, hd * 2 + dc, mc * 128:(mc + 1) * 128], qo[:, hd * 2 + dc, H(h)], [t_kv, t_qo[hd]])
                                   for dc in range(2)], h=h)
                        i = nsq()
                        E.op("act", lambda e, i=i, b=b: e.activation(out=sq[:, i, P(h)], in_=ps[:, b, P(h)], func=AF.Exp,
                                                                   scale=1.0 / 16.0),
                             rd=[t_bank[b]], wr=[t_sq[i]])
                        pts.append(i)
                    upts[u] = pts

                def att_b(u):
                    hd, h = units[u]
                    pts = upts[u]
                    bden = colsum([(sq[:, i, P(h)], [t_sq[i]]) for i in pts], 6 + h, h=h)
                    E.op("act", lambda e, h=h, bden=bden: e.activation(out=st_b[:, h, P(h)], in_=ps[:, bden, P(h)], func=AF.Ln),
                         rd=[t_bank[bden]], wr=[t_stb[h]])
                    E.op("act", lambda e, h=h: e.activation(out=st_b[:, h, P(h)], in_=st_b[:, h, P(h)], func=AF.Exp, scale=-1.0),
                         rd=[t_stb[h]], wr=[t_stb[h]])
                    for dc in range(2):
                        b = group([(vv[:, mc, (hd * 2 + dc) * 128:(hd * 2 + dc + 1) * 128], sq[:, pts[mc], P(h)], [t_kv, t_sq[pts[mc]]])
                                   for mc in range(2)], h=h)
                        E.op("dve", lambda e, hd=hd, dc=dc, h=h, b=b: e.tensor_tensor(
                            out=qo[:, hd * 2 + dc, H(h)], in0=ps[:, b, P(h)], in1=st_b[:, h, P(h)], op=ALU.mult),
                            rd=[t_bank[b], t_stb[h]], wr=[t_qo[hd]])

                att_a(0)
                for u in range(8):
                    if u + 1 < 8:
                        att_a(u + 1)
                    att_b(u)
                for jp in range(8):
                    T = {}
                    for br in (1, 2):
                        s = next_tile()
                        wg = ring[:, s, :].rearrange("p (k c) -> p k c", c=256)
                        for sc in range(2):
                            cs_ = slice(sc * 128, (sc + 1) * 128)
                            for h in range(2):
                                bg_ = group([(wg[:, k, cs_], hb[:, k, H(h)], [t_slot[s], thb[k][h]]) for k in range(16)], h=h)
                                i1 = ntmp()
                                E.op("act", lambda e, i1=i1, bg_=bg_: e.activation(out=tmpf[:, i1, P(h)], in_=ps[:, bg_, P(h)], func=AF.Sigmoid),
                                     rd=[t_bank[bg_]], wr=[t_tmp[i1]])
                                T[br, sc, h] = i1
                                if bg_conv:
                                    E.drain_bg("dve", G_DRAIN)
                        rel(1)
                    s = next_tile()
                    wbx = ring[:, s, :].rearrange("p (b k c) -> p b k c", b=2, c=256)
                    for sc in range(2):
                        j = jp * 2 + sc
                        cs_ = slice(sc * 128, (sc + 1) * 128)
                        for h in range(2):
                            i1 = T[1, sc, h]; i2 = T[2, sc, h]
                            byb = group([(wbx[:, 0, k, cs_], bact[:, k, H(h)], [t_slot[s], t_bact]) for k in range(8)], h=h)
                            E.op("dve", lambda e, i1=i1, byb=byb: e.tensor_tensor(out=tmpf[:, i1, P(h)], in0=ps[:, byb, P(h)],
                                                                                in1=tmpf[:, i1, P(h)], op=ALU.mult),
                                 rd=[t_bank[byb], t_tmp[i1]], wr=[t_tmp[i1]])
                            byx = group([(wbx[:, 1, k, cs_], qo[:, k, H(h)], [t_slot[s]] + t_qo) for k in range(8)], h=h)
                            E.op("dve", lambda e, i2=i2, byx=byx: e.tensor_tensor(out=tmpf[:, i2, P(h)], in0=ps[:, byx, P(h)],
                                                                                in1=tmpf[:, i2, P(h)], op=ALU.mult),
                                 rd=[t_bank[byx], t_tmp[i2]], wr=[t_tmp[i2]])
                            E.op("dve", lambda e, i1=i1, i2=i2, j=j, h=h: e.tensor_tensor(
                                out=merged[:, j, H(h)], in0=tmpf[:, i1, P(h)], in1=tmpf[:, i2, P(h)], op=ALU.add),
                                rd=[t_tmp[i1], t_tmp[i2]], wr=[t_mrg])
                    rel(1)
                    if bg_conv and jp == LN_AT:
                        E.drain_bg("pool")
                        E.drain_bg("dve")
                        ln_silu()
                    elif bg_conv and jp == LN_AT + 1:
                        while ln_defer:
                            ln_defer.pop(0)()
                while ln_defer:
                    ln_defer.pop(0)()
                for jq in range(4):
                    T = {}
                    for half4 in range(2):
                        s = next_tile()
                        wg = ring[:, s, :].rearrange("p (k c) -> p k c", c=256)
                        for sc in range(2):
                            q4 = half4 * 2 + sc
                            cg = slice(sc * 128, (sc + 1) * 128)
                            for h in range(2):
                                bg0 = group([(wg[:, k, cg], hb[:, k, H(h)], [t_slot[s], thb[k][h]]) for k in range(16)], h=h)
                                i1 = ntmp()
                                E.op("act", lambda e, i1=i1, bg0=bg0: e.activation(out=tmpf[:, i1, P(h)], in_=ps[:, bg0, P(h)], func=AF.Sigmoid),
                                     rd=[t_bank[bg0]], wr=[t_tmp[i1]])
                                T[q4, h] = i1
                        rel(1)
                    if bg_conv and jq == 0 and LN_AT < 0:
                        E.drain_bg("pool")
                        E.drain_bg("dve")
                        ln_silu()
                    s = next_tile()
                    wa = ring[:, s, :].rearrange("p (k c) -> p k c", c=512)
                    for q4 in range(4):
                        j = jq * 4 + q4
                        ca = slice(q4 * 128, (q4 + 1) * 128)
                        for h in range(2):
                            i1 = T[q4, h]
                            bya = group([(wa[:, k, ca], aact[:, k, H(h)], [t_slot[s], t_aact]) for k in range(8)], h=h)
                            E.op("dve", lambda e, i1=i1, bya=bya: e.tensor_tensor(out=tmpf[:, i1, P(h)], in0=ps[:, bya, P(h)],
                                                                                in1=tmpf[:, i1, P(h)], op=ALU.mult),
                                 rd=[t_bank[bya], t_tmp[i1]], wr=[t_tmp[i1]])
                            E.op("dve", lambda e, i1=i1, j=j, h=h: e.tensor_tensor(
                                out=merged[:, j, H(h)], in0=merged[:, j, H(h)], in1=tmpf[:, i1, P(h)], op=ALU.add),
                                rd=[t_tmp[i1]], wr=[t_mrg])
                    rel(1)
                for k, v in list(t_s2.w.items()) + list(t_s2.r.items()):
                    if t_s1.r.get(k, 0) < v:
                        t_s1.r[k] = v
                pend = []
                for t, s in enumerate(tiles(8)):
                    for sc in range(2):
                        j = t * 2 + sc
                        std_unit(s, sc, 16, lambda k, h: merged[:, k, H(h)], [t_mrg],
                                 lambda h, b, j=j: zevac_with_stats(zm, t_s1, j, h, b, j == 0, j == 15, pend))
                        flush_stats(pend, keep=2)
                flush_stats(pend)
                post_norm_update(l, C_GPOST, zm, t_s1)
                t_s2.w = dict(t_s1.w); t_s2.r = dict(t_s1.r)
                ffn_stats = ffn_pre(l)
                for tk in (t_s1, t_s2, t_aact, t_bact, t_kv) + tuple(t_qo):
                    for k, v in list(tk.w.items()) + list(tk.r.items()):
                        if t_hid.r.get(k, 0) < v:
                            t_hid.r[k] = v
                for t, s in enumerate(tiles(32)):
                    for sc in range(2):
                        c = t * 2 + sc

                        def ev_up(h, b, c=c):
                            i = ntmp()
                            E.op("act", lambda e: e.activation(out=tmpf[:, i, P(h)], in_=ps[:, b, P(h)], func=AF.Relu),
                                 rd=[t_bank[b]], wr=[t_tmp[i]])
                            E.op("dve", lambda e: e.tensor_tensor(out=hid[:, c, H(h)], in0=tmpf[:, i, P(h)], in1=tmpf[:, i, P(h)], op=ALU.mult),
                                 rd=[t_tmp[i]], wr=[t_hid])
                        std_unit(s, sc, 16, hrhs, hbtk, ev_up)
                    if t >= 2:
                        for _ in range(2):
                            if ffn_stats:
                                ffn_stats.pop(0)()
                for tk in t_hb_all + [t_mrg]:
                    for k, v in list(tk.w.items()) + list(tk.r.items()):
                        if t_z.r.get(k, 0) < v:
                            t_z.r[k] = v
                pend = []
                for jc in range(16):
                    sl = [next_tile(), next_tile()]
                    for h in range(2):
                        mms = []
                        for kg in range(2):
                            wv = ring[:, sl[kg], :].rearrange("p (k c) -> p k c", c=128)
                            mms += [(wv[:, k, :], hid[:, kg * 32 + k, H(h)], [t_slot[sl[kg]], t_hid]) for k in range(32)]
                        b = group(mms, h=h)
                        zevac_with_stats(zf, t_z, jc, h, b, jc == 0, jc == 15, pend)
                    flush_stats(pend, keep=2)
                    rel(2)
                flush_stats(pend)
                for tk in (t_s1, t_s2, t_aact, t_bact, t_kv) + tuple(t_qo):
                    tk.w = dict(t_hid.w); tk.r = dict(t_hid.r)
                post_norm_rstd(eps_tile=True)
                post_norm_xupdate(l, C_GMPOST, zf, t_z, halves=(0,))
                if l + 1 < n_layers:
                    kv_norm(l + 1)
                post_norm_xupdate(l, C_GMPOST, zf, t_z, halves=(1,))
                for tk in t_hb_all:
                    tk.w = dict(t_z.w); tk.r = dict(t_z.r)
                t_mrg.w = dict(t_z.w); t_mrg.r = dict(t_z.r)
                if blk == 0:
                    for c in range(16):
                        E.op("dve", lambda e, c=c: e.tensor_scalar(out=xs[:, c, 0:HALO], in0=xs[:, c, 0:HALO],
                                                                  scalar1=hmask[:, 0:1], scalar2=None, op0=ALU.mult),
                             rd=[t_cst], wr=txs[c])
            if blk == 0:
                ev = E.dma("sp", "out", outT[:, :, 0:NT - HALO], xs[:, :, HALO:NT], rd=t_xs_all)
            else:
                ev = E.dma("sp", "out", outT[:, :, blk * NT - HALO:(blk + 1) * NT - HALO], xs[:], rd=t_xs_all)
            final_evs.append(ev)
        E.finalize(st, final_evs)
    return nc


def _std(W, c0, ncols=256, k0=0, nk=16):
    blk = W[k0 * 128:(k0 + nk) * 128, c0:c0 + ncols]
    return blk.reshape(nk, 128, ncols).transpose(1, 0, 2).reshape(128, nk * ncols)


def pack_layer_tiles(out, w_in, w_a_out, w_b_out, w_kv, w_x_out, w_o, w_up, w_down):
    i = 0

    def put(a):
        nonlocal i
        out[i] = a
        i += 1
    for t in range(8):
        put(_std(w_kv, t * 256))
    for base in (3072, 4096, 1024, 0, 2048, 5120):
        for t in range(4):
            put(_std(w_in, base + t * 256))
    for jp in range(8):
        put(_std(w_in, 8192 + jp * 256))
        put(_std(w_in, 10240 + jp * 256))
        put(np.concatenate([_std(w_b_out, jp * 256, nk=8), _std(w_x_out, jp * 256, nk=8)], axis=1))
    for jq in range(4):
        put(_std(w_in, 6144 + (2 * jq) * 256))
        put(_std(w_in, 6144 + (2 * jq + 1) * 256))
        put(_std(w_a_out, jq * 512, ncols=512, nk=8))
    for t in range(8):
        put(_std(w_o, t * 256))
    for t in range(32):
        put(_std(w_up, t * 256))
    for jc in range(16):
        for kg in range(2):
            put(_std(w_down, jc * 128, ncols=128, k0=kg * 32, nk=32))
    assert i == TPL


def pack_consts(layers, g_mix_pre, g_mix_post, g_mlp_pre, g_mlp_post, g_mem, conv_a_w, conv_a_b, ln_a_g, ln_a_b, conv_b_w):
    cst = np.zeros((128, len(layers) * C_PER), np.float32)
    for li, l in enumerate(layers):
        o = li * C_PER
        for col, g in ((C_GPRE, g_mix_pre), (C_GPOST, g_mix_post), (C_GMPRE, g_mlp_pre), (C_GMPOST, g_mlp_post), (C_GMEM, g_mem)):
            cst[:, o + col:o + col + 16] = g[l].reshape(16, 128).T
        cst[:, o + C_CAW:o + C_CAW + 8 * CK] = conv_a_w[l].reshape(CK, 8, 128).transpose(2, 1, 0).reshape(128, 8 * CK)
        cst[:, o + C_CAB:o + C_CAB + 8] = conv_a_b[l].reshape(8, 128).T
        cst[:, o + C_LNG:o + C_LNG + 8] = ln_a_g[l].reshape(8, 128).T
        cst[:, o + C_LNB:o + C_LNB + 8] = ln_a_b[l].reshape(8, 128).T
        cst[:, o + C_CBW:o + C_CBW + 24] = conv_b_w[l].reshape(3, 8, 128).transpose(2, 1, 0).reshape(128, 24)
    return cst


def shard_x(x2d, core):
    lo = core * TOK - HALO
    if lo < 0:
        blk = np.concatenate([np.zeros((HALO, D), np.float32), x2d[0:TOK]], axis=0)
    else:
        blk = x2d[lo:lo + TOK + HALO]
    return np.ascontiguousarray(blk.T.reshape(16, 128, TOK + HALO).transpose(1, 0, 2))


_PROG_CACHE = {}


def _get_prog(n_layers):
    if n_layers not in _PROG_CACHE:
        _PROG_CACHE[n_layers] = build_program(n_layers)
    return _PROG_CACHE[n_layers]


FUSED = True


def kernel(x, mem, g_mix_pre, w_in, conv_a_w, conv_a_b, ln_a_g, ln_a_b, w_a_out, conv_b_w, w_b_out,
           g_mem, w_kv, w_x_out, w_o, g_mix_post, g_mlp_pre, w_up, w_down, g_mlp_post):
    f = lambda a: np.asarray(a, dtype=np.float32)
    x2d = f(x)[0]
    memT = np.ascontiguousarray(f(mem)[0].T.reshape(16, 128, MEM).transpose(1, 0, 2))
    groups = [list(range(DEPTH))] if FUSED else [[l] for l in range(DEPTH)]
    for layers in groups:
        nl = len(layers)
        wts = np.empty((nl * TPL, 128, TILE), np.float32)
        for li, l in enumerate(layers):
            pack_layer_tiles(wts[li * TPL:(li + 1) * TPL], f(w_in[l]), f(w_a_out[l]), f(w_b_out[l]), f(w_kv[l]),
                             f(w_x_out[l]), f(w_o[l]), f(w_up[l]), f(w_down[l]))
        cst = pack_consts(layers, f(g_mix_pre), f(g_mix_post), f(g_mlp_pre), f(g_mlp_post), f(g_mem),
                          f(conv_a_w), f(conv_a_b), f(ln_a_g), f(ln_a_b), f(conv_b_w))
        nc = _get_prog(nl)
        in_maps = []
        for c in range(NCORE):
            in_maps.append({"xT": shard_x(x2d, c), "wts": wts, "cst": cst, "memT": memT,
                            "hmask": np.full((128, 1), 0.0 if c == 0 else 1.0, np.float32)})
        res = run_bass_kernel_spmd(nc, in_maps, core_ids=list(range(NCORE)))
        outs = []
        for c in range(NCORE):
            o = res.results[c]["outT"]
            outs.append(o.transpose(2, 1, 0).reshape(TOK, D))
        x2d = np.concatenate(outs, axis=0)
    return np.ascontiguousarray(x2d[None]).astype(np.float32)
```

```python
import numpy as np
from contextlib import ExitStack
import concourse.bass as bass
import concourse.mybir as mybir
from concourse.bass_utils import run_bass_kernel_spmd

F32 = mybir.dt.float32
BF16 = mybir.dt.bfloat16
AF = mybir.ActivationFunctionType
ALU = mybir.AluOpType

D = 2048
SEQ = 8192
DEPTH = 4
NCORE = 8
TOK = SEQ // NCORE
HALO = 128
NT = 576
NH = 288
NBLK = 2
MEM = 256
CK = 31
EPS = 1e-6
TPL = 140
TILE = 4096
NSLOT = 4
NTMP = 8
POOL_CHUNKS = 0
LN_AT = 5
TRIM = True
BG_RATE = 1
BALANCE = False
FAST_RECIP = False
LNEXP = True
KV_SPLIT = 3
Q_DRAIN = 4
G_DRAIN = 2
EARLY_CONV = True
LN_SKIP = 6
LN_DEFER = True
A_DRAIN = 4

C_GPRE, C_GPOST, C_GMPRE, C_GMPOST, C_GMEM = 0, 16, 32, 48, 64
C_CAW = 80
C_CAB = C_CAW + 8 * CK
C_LNG = C_CAB + 8
C_LNB = C_LNG + 8
C_CBW = C_LNB + 8
C_PER = C_CBW + 24

ENGS = ("pe", "act", "dve", "pool", "sp")


class Tk:
    __slots__ = ("w", "r")

    def __init__(self):
        self.w = {}
        self.r = {}


class _Rec:
    def __init__(self):
        self.call = None

    def __getattr__(self, name):
        def f(*a, **k):
            assert self.call is None
            self.call = (name, a, k)
        return f


def _eager(fn):
    r = _Rec()
    fn(r)
    name, a, k = r.call
    return lambda e: getattr(e, name)(*a, **k)


class Emitter:
    def __init__(self, nc, strict_same=False):
        self.nc = nc
        self.streams = {e: [] for e in ENGS}
        self.cnt = {e: 0 for e in ENGS}
        self.waited = {e: {} for e in ENGS}
        self.dma_cnt = {}
        self.strict_same = strict_same
        self.sems = {}
        self.bg = {"dve": [], "pool": []}
        self.bg_rate = {"dve": 0, "pool": 0}
        self.in_bg = False

    def _deps(self, rd, wr, extra):
        deps = {}
        for t in rd:
            for k, v in t.w.items():
                if deps.get(k, 0) < v:
                    deps[k] = v
        for t in wr:
            for d in (t.w, t.r):
                for k, v in d.items():
                    if deps.get(k, 0) < v:
                        deps[k] = v
        for d in extra:
            if d is None:
                continue
            for k, v in d.items():
                if deps.get(k, 0) < v:
                    deps[k] = v
        return deps

    def _waits(self, eng, deps, strict):
        waits = []
        for k, v in deps.items():
            if k == eng and not strict:
                continue
            if self.waited[eng].get(k, 0) >= v:
                continue
            self.waited[eng][k] = v
            waits.append((k, v))
        return waits

    def op(self, eng, fn, rd=(), wr=(), sig=True, extra=(), strict=None):
        strict = self.strict_same if strict is None else strict
        deps = self._deps(rd, wr, extra)
        waits = self._waits(eng, deps, strict)
        if sig:
            self.cnt[eng] += 1
            c = self.cnt[eng]
        else:
            c = self.cnt[eng] + 1
        self.streams[eng].append((waits, _eager(fn), 1 if sig else 0, eng))
        for t in rd:
            if t.r.get(eng, 0) < c:
                t.r[eng] = c
        for t in wr:
            t.w = {eng: c}
            t.r = {}
        ev = {eng: c}
        if eng == "dve" and self.bg["dve"] and not self.in_bg:
            self.drain_bg("dve", self.bg_rate["dve"])
        return ev

    def drain_bg(self, eng, n=None):
        self.in_bg = True
        k = 0
        q = self.bg[eng]
        while q and (n is None or k < n):
            q.pop(0)()
            k += 1
        self.in_bg = False

    def dma(self, q, slot, out, in_, rd=(), wr=(), extra=()):
        key = "dma:" + slot
        deps = self._deps(rd, wr, extra)
        waits = self._waits(q, deps, False)
        self.dma_cnt[key] = self.dma_cnt.get(key, 0) + 16
        c = self.dma_cnt[key]
        self.streams[q].append((waits, lambda e, o=out, i=in_: e.dma_start(out=o, in_=i), 16, key))
        for t in rd:
            if t.r.get(key, 0) < c:
                t.r[key] = c
        for t in wr:
            t.w = {key: c}
            t.r = {}
        if q == "pool" and self.bg["pool"] and not self.in_bg:
            self.drain_bg("pool", self.bg_rate["pool"])
        return {key: c}

    def finalize(self, st, final_waits):
        nc = self.nc
        keys = list(ENGS) + sorted(self.dma_cnt.keys())
        for k in keys:
            self.sems[k] = st.enter_context(nc.semaphore("s_" + k.replace(":", "_")))
        block = st.enter_context(nc.Block())
        fin = {}
        for d in final_waits:
            for k, v in d.items():
                fin[k] = max(fin.get(k, 0), v)

        def replay(eng_name):
            def run(e):
                for waits, fn, inc, semkey in self.streams[eng_name]:
                    for k, v in waits:
                        e.wait_ge(self.sems[k], v)
                    ins = fn(e)
                    if inc:
                        ins.then_inc(self.sems[semkey], inc)
                if eng_name == "sp":
                    for k, v in fin.items():
                        e.wait_ge(self.sems[k], v)
            return run

        block.tensor(replay("pe"))
        block.scalar(replay("act"))
        block.vector(replay("dve"))
        block.gpsimd(replay("pool"))
        block.sync(replay("sp"))


def build_program(n_layers, nblk=NBLK, bg_conv=True):
    nc = bass.Bass("TRN2", target_bir_lowering=False)
    xT = nc.dram_tensor("xT", [128, 16, nblk * NT], F32, kind="ExternalInput").ap()
    wts = nc.dram_tensor("wts", [n_layers * TPL, 128, TILE], F32, kind="ExternalInput").ap()
    cstd = nc.dram_tensor("cst", [128, n_layers * C_PER], F32, kind="ExternalInput").ap()
    memTd = nc.dram_tensor("memT", [128, 16, MEM], F32, kind="ExternalInput").ap()
    hmaskd = nc.dram_tensor("hmask", [128, 1], F32, kind="ExternalInput").ap()
    outT = nc.dram_tensor("outT", [128, 16, nblk * NT - HALO], F32, kind="ExternalOutput").ap()

    st = ExitStack()
    with st:
        def sb(name, shape, dt):
            return st.enter_context(nc.sbuf_tensor(name, shape, dt))

        cst = sb("cst_sb", [128, n_layers * C_PER], F32)
        hmask = sb("hmask_sb", [128, 1], F32)
        xs = sb("xs", [128, 16, NT], F32)
        ZA = sb("ZA", [128, 16 * NT], F32)
        HA = sb("HA", [128, 64 * NT], BF16)
        ring = sb("ring", [128, NSLOT, TILE], BF16)
        ones = sb("ones", [128, 128], BF16)
        sq = sb("sq", [128, 8, NH], BF16)
        st_a = sb("st_a", [128, 2, NH], F32)
        st_b = sb("st_b", [128, 2, NH], F32)
        st_c = sb("st_c", [128, 2, NH], F32)
        tmpf = sb("tmpf", [128, NTMP, NH], F32)
        ptmp = sb("ptmp", [128, NH], F32) if POOL_CHUNKS > 0 else None
        ahist = sb("ahist", [128, n_layers, 8, 30], F32)
        uhist = sb("uhist", [128, n_layers, 8, 2], F32)
        ps = st.enter_context(nc.psum_tensor("ps", [128, 8, 512], F32))

        ZAb = ZA[:].bitcast(BF16)
        hb = ZAb[:, 0:16 * NT].rearrange("p (c n) -> p c n", c=16)
        merged = ZAb[:, 16 * NT:32 * NT].rearrange("p (c n) -> p c n", c=16)
        u32 = ZA[:, 8 * NT:16 * NT].rearrange("p (c n) -> p c n", c=8)
        zf = ZA[:].rearrange("p (c n) -> p c n", c=16)
        HAf = HA[:].bitcast(F32)
        o_s1 = 0
        n_s1 = 8 * (NT + 32)
        o_s2 = n_s1
        n_s2 = 8 * NT
        s1 = HAf[:, o_s1:o_s1 + n_s1].rearrange("p (c n) -> p c n", c=8)
        s2 = HAf[:, o_s2:o_s2 + n_s2].rearrange("p (c n) -> p c n", c=8)
        zm = HAf[:, 0:16 * NT].rearrange("p (c n) -> p c n", c=16)
        ob16 = 2 * (n_s1 + n_s2)
        aact = HA[:, ob16:ob16 + 8 * NT].rearrange("p (c n) -> p c n", c=8)
        bact = HA[:, ob16 + 8 * NT:ob16 + 16 * NT].rearrange("p (c n) -> p c n", c=8)
        qo = HA[:, ob16 + 16 * NT:ob16 + 24 * NT].rearrange("p (c n) -> p c n", c=8)
        okv = ob16 + 24 * NT
        kT = HA[:, okv:okv + 8 * MEM].rearrange("p (c n) -> p c n", c=8)
        vv = HA[:, okv + 8 * MEM:okv + 16 * MEM].rearrange("p (c n) -> p c n", c=2)
        assert okv + 16 * MEM <= 64 * NT
        hid = HA[:].rearrange("p (c n) -> p c n", c=64)
        memf = HAf[:, 0:16 * MEM].rearrange("p (c n) -> p c n", c=16)
        memn = HA[:, 2 * o_s2:2 * o_s2 + 16 * MEM].rearrange("p (c n) -> p c n", c=16)

        E = Emitter(nc)
        t_cst = Tk(); txs = [[Tk(), Tk()] for _ in range(16)]; t_xs_all = [t for p in txs for t in p]; thb = [[Tk(), Tk()] for _ in range(16)]; t_hb_all = [t for p in thb for t in p]; t_mrg = Tk(); t_z = Tk()
        t_s1 = Tk(); t_s2 = Tk(); t_aact = Tk(); t_bact = Tk(); t_qo = [Tk() for _ in range(4)]
        t_kv = Tk(); t_hid = Tk(); t_mem = Tk(); t_memn = Tk()
        t_slot = [Tk() for _ in range(NSLOT)]
        t_bank = [Tk() for _ in range(8)]
        t_sq = [Tk() for _ in range(8)]
        t_sta = [Tk(), Tk()]; t_stb = [Tk(), Tk()]; t_stc = [Tk(), Tk()]
        t_tmp = [Tk() for _ in range(NTMP)]
        t_s2p = Tk(); t_ptmp = Tk()
        t_hist = Tk(); t_ones = Tk()
        rr = {"bank": 0, "sq": 0, "tmp": 0, "pt": 0, "tile": 0, "done": 0, "skip": set(), "skipn": 0}

        cur = {"lo": 0}
        LO = [8, 38, 68, 98]

        def MID():
            lo = cur["lo"]
            if lo == 0 or not BALANCE:
                return NH
            return lo + ((NT - lo) // 4) * 2 + ((NT - lo) % 4 > 0) * 2 if False else max(NH, ((lo + NT) // 4) * 2)

        def H(h):
            return slice(cur["lo"], MID()) if h == 0 else slice(MID(), NT)

        def P(h):
            sl = H(h)
            return slice(0, sl.stop - sl.start)

        def cs(l, col):
            c0 = l * C_PER + col
            return cst[:, c0:c0 + 1]

        def nbank():
            while True:
                b = rr["bank"]
                rr["bank"] = (b + 1) % 6
                if rr["skipn"] > 0 and b in rr["skip"]:
                    continue
                break
            if rr["skipn"] > 0:
                rr["skipn"] -= 1
                if rr["skipn"] == 0:
                    rr["skip"] = set()
            return b

        def nsq():
            i = rr["sq"]; rr["sq"] = (i + 1) % 8
            return i

        def ntmp():
            i = rr["tmp"]; rr["tmp"] = (i + 1) % NTMP
            return i


        E.dma("sp", "cst", cst[:], cstd[:], wr=[t_cst])
        E.dma("sp", "hmask", hmask[:], hmaskd[:], wr=[t_cst])
        E.op("dve", lambda e: e.memset(ones[:], 1.0), wr=[t_ones])
        E.op("dve", lambda e: e.memset(HAf[:, 0:n_s1 + n_s2], 0.0), wr=[t_s1, t_s2])
        E.op("dve", lambda e: e.memset(ahist[:], 0.0), wr=[t_hist])
        E.op("dve", lambda e: e.memset(uhist[:], 0.0), wr=[t_hist])

        tile_state = {"next_load": 0, "order": []}

        def prefetch(upto):
            while tile_state["next_load"] < min(upto, len(tile_state["order"])):
                i = tile_state["next_load"]
                s = i % NSLOT
                E.dma("pool", "w%d" % s, ring[:, s, :], wts[tile_state["order"][i]], wr=[t_slot[s]])
                tile_state["next_load"] += 1

        def next_tile():
            i = rr["tile"]
            rr["tile"] += 1
            assert i - rr["done"] < NSLOT
            prefetch(i + 1)
            return i % NSLOT

        def tiles(n):
            for _ in range(n):
                s_ = next_tile()
                yield s_
                rel(1)

        def rel(n=1):
            rr["done"] += n
            prefetch(rr["done"] + NSLOT)

        def group(mms, h=None, nout=None, bank=None):
            b = nbank() if bank is None else bank
            n = len(mms)
            psl = P(h) if h is not None else slice(0, nout)
            for i, (l_ap, r_ap, rdt) in enumerate(mms):
                E.op("pe", lambda e, l_ap=l_ap, r_ap=r_ap, i=i, b=b: e.matmul(
                    ps[:, b, psl], lhsT=l_ap, rhs=r_ap, start=(i == 0), stop=(i == n - 1)),
                    rd=rdt, wr=[t_bank[b]] if i == 0 else [], sig=(i == n - 1))
            t_bank[b].w = {"pe": E.cnt["pe"]}
            return b

        def colsum(srcs, bank, h=None, nout=None):
            return group([(ones[:], s_ap, [t_ones] + tk) for s_ap, tk in srcs], h=h, nout=nout, bank=bank)

        def rstd_from_bank(bank, h, dst, t_dst, inv_n, nout=None):
            sl = P(h) if nout is None else slice(0, nout)
            if LNEXP:
                E.op("act", lambda e: e.activation(out=dst[:, h, sl], in_=ps[:, bank, sl], func=AF.Ln,
                                                   bias=cst_eps[:], scale=inv_n),
                     rd=[t_bank[bank], t_cst], wr=[t_dst])
                E.op("act", lambda e: e.activation(out=dst[:, h, sl], in_=dst[:, h, sl], func=AF.Exp, scale=-0.5),
                     rd=[t_dst], wr=[t_dst])
                return
            E.op("act", lambda e: e.activation(out=dst[:, h, sl], in_=ps[:, bank, sl], func=AF.Sqrt,
                                               bias=cst_eps[:], scale=inv_n),
                 rd=[t_bank[bank], t_cst], wr=[t_dst])
            E.op("dve", lambda e: e.reciprocal(out=dst[:, h, sl], in_=dst[:, h, sl]), rd=[t_dst], wr=[t_dst])

        st_mem = sb("st_mem", [128, 1, MEM], F32)
        t_stmem = Tk()
        cst_eps = sb("cst_eps", [128, 1], F32)
        E.op("dve", lambda e: e.memset(cst_eps[:], EPS), wr=[t_cst])

        def rms_pre(l, gcol):
            for h in range(2):
                b = 6 + h
                for c in range(16):
                    i = nsq()
                    E.op("act", lambda e, c=c, i=i, h=h: e.activation(out=sq[:, i, P(h)], in_=xs[:, c, H(h)], func=AF.Square),
                         rd=[txs[c][h]], wr=[t_sq[i]])
                    E.op("pe", lambda e, i=i, c=c, b=b: e.matmul(
                        ps[:, b, P(h)], lhsT=ones[:], rhs=sq[:, i, P(h)], start=(c == 0), stop=(c == 15)),
                        rd=[t_ones, t_sq[i]], wr=[t_bank[b]] if c == 0 else [], sig=True)
                t_bank[b].w = {"pe": E.cnt["pe"]}
                rstd_from_bank(b, h, st_a, t_sta[h], 1.0 / D)
                for c in range(16):
                    E.op("dve", lambda e, c=c, h=h: e.scalar_tensor_tensor(
                        out=hb[:, c, H(h)], in0=xs[:, c, H(h)], scalar=cs(l, gcol + c), in1=st_a[:, h, P(h)],
                        op0=ALU.mult, op1=ALU.mult), rd=[txs[c][h], t_sta[h], t_cst], wr=[thb[c][h]])

        def std_unit(slot, sc, nk, rhs_fn, rhs_tk, evac, ncols=256, koff=0):
            wv = ring[:, slot, :].rearrange("p (k c) -> p k c", c=ncols)
            for h in range(2):
                b = group([(wv[:, koff + k, sc * 128:(sc + 1) * 128], rhs_fn(k, h),
                            [t_slot[slot]] + (rhs_tk(k, h) if callable(rhs_tk) else rhs_tk))
                           for k in range(nk)], h=h)
                evac(h, b)

        def post_norm_rstd(eps_tile=False):
            for h in range(2):
                if eps_tile:
                    b = 6 + h
                    E.op("dve", lambda e, h=h, b=b: e.scalar_tensor_tensor(
                        out=st_a[:, h, P(h)], in0=ps[:, b, P(h)], scalar=1.0 / D, in1=st_b[:, h, P(h)],
                        op0=ALU.mult, op1=ALU.add), rd=[t_bank[b], t_stb[h]], wr=[t_sta[h]])
                    E.op("act", lambda e, h=h: e.activation(out=st_a[:, h, P(h)], in_=st_a[:, h, P(h)], func=AF.Ln),
                         rd=[t_sta[h]], wr=[t_sta[h]])
                    E.op("act", lambda e, h=h: e.activation(out=st_a[:, h, P(h)], in_=st_a[:, h, P(h)], func=AF.Exp, scale=-0.5),
                         rd=[t_sta[h]], wr=[t_sta[h]])
                else:
                    rstd_from_bank(6 + h, h, st_a, t_sta[h], 1.0 / D)

        def post_norm_xupdate(l, gcol, z, t_zz, halves=(0, 1)):
            for h in halves:
                for c in range(16):
                    i = ntmp()
                    E.op("dve", lambda e, c=c, h=h, i=i: e.scalar_tensor_tensor(
                        out=tmpf[:, i, P(h)], in0=z[:, c, H(h)], scalar=cs(l, gcol + c), in1=st_a[:, h, P(h)],
                        op0=ALU.mult, op1=ALU.mult), rd=[t_zz, t_sta[h], t_cst], wr=[t_tmp[i]])
                    E.op("dve", lambda e, c=c, h=h, i=i: e.tensor_tensor(
                        out=xs[:, c, H(h)], in0=xs[:, c, H(h)], in1=tmpf[:, i, P(h)], op=ALU.add),
                        rd=[t_tmp[i]], wr=[txs[c][h]])

        def post_norm_update(l, gcol, z, t_zz, eps_tile=False):
            post_norm_rstd(eps_tile)
            post_norm_xupdate(l, gcol, z, t_zz)

        def ffn_pre(l):
            for h in range(2):
                for c in range(16):
                    E.op("act", lambda e, c=c, h=h: e.activation(out=hb[:, c, H(h)], in_=xs[:, c, H(h)], func=AF.Copy,
                                                               scale=cs(l, C_GMPRE + c)),
                         rd=[txs[c][h], t_cst], wr=[thb[c][h]])
            th = []
            lag = []

            def pe_one():
                (i, c, h) = lag.pop(0)
                b = 6 + h
                E.op("pe", lambda e: e.matmul(ps[:, b, P(h)], lhsT=ones[:], rhs=sq[:, i, P(h)], start=(c == 0), stop=(c == 15)),
                     rd=[t_ones, t_sq[i]], wr=[t_bank[b]] if c == 0 else [], sig=True)
                t_bank[b].w = {"pe": E.cnt["pe"]}
                if c == 15:
                    E.op("dve", lambda e: e.tensor_scalar(out=st_b[:, h, P(h)], in0=ps[:, b, P(h)], scalar1=1.0 / D,
                                                          scalar2=EPS, op0=ALU.mult, op1=ALU.add),
                         rd=[t_bank[b]], wr=[t_stb[h]])
                    E.op("dve", lambda e: e.tensor_tensor(out=st_b[:, h, P(h)], in0=st_b[:, h, P(h)], in1=st_b[:, h, P(h)], op=ALU.mult),
                         rd=[t_stb[h]], wr=[t_stb[h]])
                    E.op("dve", lambda e: e.tensor_scalar(out=st_b[:, h, P(h)], in0=st_b[:, h, P(h)], scalar1=EPS,
                                                          scalar2=None, op0=ALU.mult),
                         rd=[t_stb[h]], wr=[t_stb[h]])
            for h in range(2):
                for c in range(16):
                    def f(c=c, h=h):
                        i = nsq()
                        E.op("act", lambda e: e.activation(out=sq[:, i, P(h)], in_=xs[:, c, H(h)], func=AF.Square),
                             rd=[txs[c][h]], wr=[t_sq[i]])
                        lag.append((i, c, h))
                        if len(lag) > 3:
                            pe_one()
                    th.append(f)

            def fin():
                while lag:
                    pe_one()
            th.append(fin)
            return th

        def zevac_with_stats(z, t_zz, j, h, b, first, last, pend):
            E.op("act", lambda e: e.activation(out=z[:, j, H(h)], in_=ps[:, b, P(h)], func=AF.Copy),
                 rd=[t_bank[b]], wr=[t_zz])
            i = nsq()
            E.op("act", lambda e: e.activation(out=sq[:, i, P(h)], in_=ps[:, b, P(h)], func=AF.Square),
                 rd=[t_bank[b]], wr=[t_sq[i]])
            pend.append((i, h, first, last))

        def flush_stats(pend, keep=0):
            while len(pend) > keep:
                i, h, first, last = pend.pop(0)
                b = 6 + h
                E.op("pe", lambda e, i=i, b=b, first=first, last=last: e.matmul(
                    ps[:, b, P(h)], lhsT=ones[:], rhs=sq[:, i, P(h)], start=first, stop=last),
                    rd=[t_ones, t_sq[i]], wr=[t_bank[b]] if first else [], sig=True)
                t_bank[b].w = {"pe": E.cnt["pe"]}

        kv_state = {"have_rstd": False}

        def kv_norm(l):
            E.dma("sp", "mem", memf[:], memTd[:], wr=[t_s1])
            if not kv_state["have_rstd"]:
                kv_state["have_rstd"] = True
                b = nbank()
                for c in range(16):
                    i = nsq()
                    E.op("act", lambda e, c=c, i=i: e.activation(out=sq[:, i, 0:MEM], in_=memf[:, c, :], func=AF.Square),
                         rd=[t_s1], wr=[t_sq[i]])
                    E.op("pe", lambda e, i=i, c=c, b=b: e.matmul(
                        ps[:, b, 0:MEM], lhsT=ones[:], rhs=sq[:, i, 0:MEM], start=(c == 0), stop=(c == 15)),
                        rd=[t_ones, t_sq[i]], wr=[t_bank[b]] if c == 0 else [], sig=True)
                t_bank[b].w = {"pe": E.cnt["pe"]}
                rstd_from_bank(b, 0, st_mem, t_stmem, 1.0 / D, nout=MEM)
            for c in range(16):
                E.op("dve", lambda e, c=c, g_ap=cs(l, C_GMEM + c): e.scalar_tensor_tensor(
                    out=memn[:, c, :], in0=memf[:, c, :], scalar=g_ap, in1=st_mem[:, 0, 0:MEM],
                    op0=ALU.mult, op1=ALU.mult), rd=[t_s1, t_stmem, t_cst], wr=[t_s2])

        def kv_tiles(t0, t1):
            for t8 in range(t0, t1):
                s = next_tile()
                wv = ring[:, s, :].rearrange("p (k c) -> p k c", c=256)
                if t8 < 4:
                    t = t8
                    for sc in range(2):
                        dch = t * 2 + sc
                        b = group([(wv[:, k, sc * 128:(sc + 1) * 128], memn[:, k, :], [t_slot[s], t_s2])
                                   for k in range(16)], nout=MEM)
                        E.op("act", lambda e, dch=dch, b=b: e.activation(out=kT[:, dch, :], in_=ps[:, b, 0:MEM], func=AF.Copy),
                             rd=[t_bank[b]], wr=[t_kv])
                else:
                    t = t8 - 4
                    for mc in range(2):
                        b = group([(memn[:, k, mc * 128:(mc + 1) * 128], wv[:, k, :], [t_slot[s], t_s2])
                                   for k in range(16)], nout=256)
                        E.op("act", lambda e, t=t, mc=mc, b=b: e.activation(out=vv[:, mc, t * 256:(t + 1) * 256],
                                                                         in_=ps[:, b, 0:256], func=AF.Copy),
                             rd=[t_bank[b]], wr=[t_kv])
                rel(1)

        final_evs = []
        for blk in range(nblk):
            base_i = len(tile_state["order"])
            tile_state["order"].extend(list(range(n_layers * TPL)))
            prefetch(rr["done"] + NSLOT)
            E.dma("sp", "xin", xs[:], xT[:, :, blk * NT:(blk + 1) * NT], wr=t_xs_all)
            for l in range(n_layers):
                cur["lo"] = LO[l + DEPTH - n_layers] if (blk == 0 and TRIM) else 0
                if l == 0 and blk == 0:
                    kv_norm(0)
                kv_tiles(0, KV_SPLIT)
                rms_pre(l, C_GPRE)
                kv_tiles(KV_SPLIT, 8)
                hrhs = lambda k, h: hb[:, k, H(h)]
                hbtk = lambda k, h: [thb[k][h]]
                for t, s in enumerate(tiles(4)):
                    for sc in range(2):
                        c = t * 2 + sc
                        std_unit(s, sc, 16, hrhs, hbtk, lambda h, b, c=c: E.op(
                            "act", lambda e: e.activation(out=s2[:, c, H(h)], in_=ps[:, b, P(h)], func=AF.Copy),
                            rd=[t_bank[b]], wr=[t_s2]))
                E.op("dve", lambda e, l=l: e.tensor_copy(out=s1[:, :, 30:32], in_=uhist[:, l, :, :]),
                     rd=[t_hist], wr=[t_s1], strict=True)
                for t, s in enumerate(tiles(4)):
                    for sc in range(2):
                        c = t * 2 + sc
                        std_unit(s, sc, 16, hrhs, hbtk, lambda h, b, c=c: E.op(
                            "dve", lambda e: e.tensor_tensor(out=s1[:, c, 32 + H(h).start:32 + H(h).stop], in0=ps[:, b, P(h)],
                                                             in1=s2[:, c, H(h)], op=ALU.mult),
                            rd=[t_bank[b], t_s2], wr=[t_s1]))
                E.op("dve", lambda e, l=l: e.tensor_copy(out=uhist[:, l, :, :], in_=s1[:, :, 32 + NT - 2:32 + NT]),
                     rd=[t_s1], wr=[t_hist], strict=True)
                for c in range(8):
                    for h in range(2):
                        E.op("dve", lambda e, c=c, h=h, w_ap=cs(l, C_CBW + c * 3 + 0): e.tensor_scalar(
                            out=u32[:, c, H(h)], in0=s1[:, c, 30 + H(h).start:30 + H(h).stop],
                            scalar1=w_ap, scalar2=None, op0=ALU.mult),
                            rd=[t_s1, t_cst], wr=[t_mrg])
                        for k in (1, 2):
                            E.op("dve", lambda e, c=c, h=h, k=k, w_ap=cs(l, C_CBW + c * 3 + k): e.scalar_tensor_tensor(
                                out=u32[:, c, H(h)], in0=s1[:, c, 30 + k + H(h).start:30 + k + H(h).stop],
                                scalar=w_ap, in1=u32[:, c, H(h)], op0=ALU.mult, op1=ALU.add),
                                rd=[t_s1, t_cst], wr=[t_mrg])
                for t, s in enumerate(tiles(4)):
                    for sc in range(2):
                        c = t * 2 + sc
                        std_unit(s, sc, 16, hrhs, hbtk, lambda h, b, c=c: E.op(
                            "act", lambda e: e.activation(out=s2[:, c, H(h)], in_=ps[:, b, P(h)], func=AF.Sigmoid),
                            rd=[t_bank[b]], wr=[t_s2]))
                E.op("dve", lambda e, l=l: e.tensor_copy(out=s1[:, :, 2:32], in_=ahist[:, l, :, :]),
                     rd=[t_hist], wr=[t_s1], strict=True)
                def conv_chunk_ops(c, l=l):
                    ops = []

                    def first():
                        E.op("dve", lambda e: e.tensor_scalar(
                            out=s2[:, c, cur["lo"]:NT], in0=s1[:, c, 2 + cur["lo"]:2 + NT],
                            scalar1=cs(l, C_CAW + c * CK), scalar2=cs(l, C_CAB + c), op0=ALU.mult, op1=ALU.add),
                            rd=[t_s1, t_cst], wr=[t_s2])
                    ops.append(first)
                    for k in range(1, CK):
                        def tap(k=k):
                            E.op("dve", lambda e: e.scalar_tensor_tensor(
                                out=s2[:, c, cur["lo"]:NT], in0=s1[:, c, 2 + k + cur["lo"]:2 + k + NT],
                                scalar=cs(l, C_CAW + c * CK + k), in1=s2[:, c, cur["lo"]:NT], op0=ALU.mult, op1=ALU.add),
                                rd=[t_s1, t_cst], wr=[t_s2])
                        ops.append(tap)
                    return ops

                early = bg_conv and EARLY_CONV and POOL_CHUNKS == 0
                if early:
                    E.bg_rate["dve"] = BG_RATE
                for t, s in enumerate(tiles(4)):
                    for sc in range(2):
                        c = t * 2 + sc
                        std_unit(s, sc, 16, hrhs, hbtk, lambda h, b, c=c: E.op(
                            "dve", lambda e: e.tensor_tensor(out=s1[:, c, 32 + H(h).start:32 + H(h).stop], in0=ps[:, b, P(h)],
                                                             in1=s2[:, c, H(h)], op=ALU.mult),
                            rd=[t_bank[b], t_s2], wr=[t_s1]))
                        if early:
                            E.drain_bg("dve", A_DRAIN)
                    if early:
                        E.bg["dve"].extend(conv_chunk_ops(2 * t) + conv_chunk_ops(2 * t + 1))
                E.op("dve", lambda e, l=l: e.tensor_copy(out=ahist[:, l, :, :], in_=s1[:, :, 32 + NT - 30:32 + NT]),
                     rd=[t_s1], wr=[t_hist], strict=True)

                if cur["lo"] > 0:
                    cur["lo"] += 30
                NDC = 8 - POOL_CHUNKS
                t_s2p.w = dict(t_s2.w); t_s2p.r = dict(t_s2.r)

                def conv_ops(l=l):
                    ops = []
                    for c in range(NDC):
                        ops += conv_chunk_ops(c)
                    return ops

                def conv_ops_pool(l=l):
                    ops = []
                    for c in range(NDC, 8):
                        for h in range(2):
                            def first(c=c, h=h):
                                E.op("pool", lambda e: e.tensor_scalar(
                                    out=s2[:, c, H(h)], in0=s1[:, c, 2 + H(h).start:2 + H(h).stop],
                                    scalar1=cs(l, C_CAW + c * CK), scalar2=cs(l, C_CAB + c), op0=ALU.mult, op1=ALU.add),
                                    rd=[t_s1, t_cst], wr=[t_s2p])
                            ops.append(first)
                            for k in range(1, CK):
                                def tap(c=c, h=h, k=k):
                                    E.op("pool", lambda e: e.tensor_scalar(
                                        out=ptmp[:], in0=s1[:, c, 2 + k + H(h).start:2 + k + H(h).stop],
                                        scalar1=cs(l, C_CAW + c * CK + k), scalar2=None, op0=ALU.mult),
                                        rd=[t_s1, t_cst], wr=[t_ptmp])
                                    E.op("pool", lambda e: e.tensor_tensor(
                                        out=s2[:, c, H(h)], in0=s2[:, c, H(h)], in1=ptmp[:], op=ALU.add),
                                        rd=[t_ptmp], wr=[t_s2p])
                                ops.append(tap)
                    return ops

                ln_defer = []

                def ln_silu(l=l):
                    bsum = [6, 7]
                    bsq = [nbank(), nbank()]
                    rr["skip"] = set(bsq); rr["skipn"] = LN_SKIP
                    lag = []

                    def pe_flush(keep):
                        while len(lag) > keep:
                            (i, bnk, h, first, last) = lag.pop(0)
                            E.op("pe", lambda e, i=i, bnk=bnk, h=h, first=first, last=last: e.matmul(
                                ps[:, bnk, P(h)], lhsT=ones[:], rhs=sq[:, i, P(h)], start=first, stop=last),
                                rd=[t_ones, t_sq[i]], wr=[t_bank[bnk]] if first else [], sig=True)
                            t_bank[bnk].w = {"pe": E.cnt["pe"]}
                    for h in range(2):
                        for c in range(8):
                            i = nsq()
                            E.op("act", lambda e, c=c, i=i, h=h: e.activation(out=sq[:, i, P(h)], in_=s2[:, c, H(h)], func=AF.Copy),
                                 rd=[t_s2, t_s2p], wr=[t_sq[i]])
                            lag.append((i, bsum[h], h, c == 0, c == 7))
                            j = nsq()
                            E.op("act", lambda e, c=c, j=j, h=h: e.activation(out=sq[:, j, P(h)], in_=s2[:, c, H(h)], func=AF.Square),
                                 rd=[t_s2, t_s2p], wr=[t_sq[j]])
                            lag.append((j, bsq[h], h, c == 0, c == 7))
                            pe_flush(4)
                    pe_flush(0)
                    for h in range(2):
                        E.op("dve", lambda e, h=h: e.tensor_scalar(
                            out=st_b[:, h, P(h)], in0=ps[:, bsum[h], P(h)], scalar1=1.0 / 1024, scalar2=None, op0=ALU.mult),
                            rd=[t_bank[bsum[h]]], wr=[t_stb[h]])
                        E.op("dve", lambda e, h=h: e.tensor_tensor(out=st_a[:, h, P(h)], in0=st_b[:, h, P(h)], in1=st_b[:, h, P(h)], op=ALU.mult),
                             rd=[t_stb[h]], wr=[t_sta[h]])
                        E.op("dve", lambda e, h=h: e.scalar_tensor_tensor(
                            out=st_c[:, h, P(h)], in0=ps[:, bsq[h], P(h)], scalar=1.0 / 1024, in1=st_a[:, h, P(h)],
                            op0=ALU.mult, op1=ALU.subtract), rd=[t_bank[bsq[h]], t_sta[h]], wr=[t_stc[h]])
                        E.op("act", lambda e, h=h: e.activation(out=st_c[:, h, P(h)], in_=st_c[:, h, P(h)], func=AF.Ln,
                                                             bias=cst_eps[:], scale=1.0), rd=[t_stc[h], t_cst], wr=[t_stc[h]])
                        E.op("act", lambda e, h=h: e.activation(out=st_c[:, h, P(h)], in_=st_c[:, h, P(h)], func=AF.Exp, scale=-0.5),
                             rd=[t_stc[h]], wr=[t_stc[h]])
                    for h in range(2):
                        for c in range(8):
                            E.op("dve", lambda e, c=c, h=h: e.tensor_tensor(out=s2[:, c, H(h)], in0=s2[:, c, H(h)],
                                                                          in1=st_b[:, h, P(h)], op=ALU.subtract),
                                 rd=[t_stb[h], t_s2p], wr=[t_s2])
                            E.op("dve", lambda e, c=c, h=h: e.tensor_tensor(out=s2[:, c, H(h)], in0=s2[:, c, H(h)],
                                                                          in1=st_c[:, h, P(h)], op=ALU.mult),
                                 rd=[t_stc[h]], wr=[t_s2])
                            def silu_op(c=c, h=h):
                                E.op("act", lambda e: e.activation(out=aact[:, c, H(h)], in_=s2[:, c, H(h)], func=AF.Silu,
                                                                   bias=cs(l, C_LNB + c), scale=cs(l, C_LNG + c)),
                                     rd=[t_s2, t_cst], wr=[t_aact])
                            if LN_DEFER and bg_conv:
                                ln_defer.append(silu_op)
                            else:
                                silu_op()
                    for k_, v_ in list(t_s2p.r.items()) + list(t_s2p.w.items()):
                        if t_s2.r.get(k_, 0) < v_:
                            t_s2.r[k_] = v_

                cops = [] if early else conv_ops()
                pops = conv_ops_pool()
                if bg_conv:
                    if not early:
                        E.bg["dve"] = cops
                    E.bg_rate["dve"] = BG_RATE
                    E.bg["pool"] = pops
                    E.bg_rate["pool"] = 14
                else:
                    for f in pops:
                        f()
                    for f in cops:
                        f()
                    ln_silu()

                for t, s in enumerate(tiles(4)):
                    for sc in range(2):
                        c = t * 2 + sc
                        std_unit(s, sc, 16, hrhs, hbtk, lambda h, b, c=c: E.op(
                            "dve", lambda e: e.tensor_tensor(out=bact[:, c, H(h)], in0=ps[:, b, P(h)],
                                                             in1=u32[:, c, H(h)], op=ALU.mult),
                            rd=[t_bank[b], t_mrg], wr=[t_bact]))
                        if bg_conv:
                            E.drain_bg("dve", A_DRAIN)
                for t, s in enumerate(tiles(4)):
                    for sc in range(2):
                        c = t * 2 + sc
                        std_unit(s, sc, 16, hrhs, hbtk, lambda h, b, c=c: E.op(
                            "act", lambda e: e.activation(out=qo[:, c, H(h)], in_=ps[:, b, P(h)], func=AF.Copy),
                            rd=[t_bank[b]], wr=[t_qo[c // 2]]))
                        if bg_conv:
                            E.drain_bg("dve", Q_DRAIN)
                units = [(hd, h) for hd in range(4) for h in range(2)]
                upts = {}

                def att_a(u):
                    hd, h = units[u]
                    pts = []
                    for mc in range(2):
                        b = group([(kT[:, hd * 2 + dc, mc * 128:(mc + 1) * 128], qo[:, hd * 2 + dc, H(h)], [t_kv, t_qo[hd]])
                                   for dc in range(2)], h=h)
                        i = nsq()
                        E.op("act", lambda e, i=i, b=b: e.activation(out=sq[:, i, P(h)], in_=ps[:, b, P(h)], func=AF.Exp,
                                                                   scale=1.0 / 16.0),
                             rd=[t_bank[b]], wr=[t_sq[i]])
                        pts.append(i)
                    upts[u] = pts

                def att_b(u):
                    hd, h = units[u]
                    pts = upts[u]
                    bden = colsum([(sq[:, i, P(h)], [t_sq[i]]) for i in pts], 6 + h, h=h)
                    E.op("act", lambda e, h=h, bden=bden: e.activation(out=st_b[:, h, P(h)], in_=ps[:, bden, P(h)], func=AF.Ln),
                         rd=[t_bank[bden]], wr=[t_stb[h]])
                    E.op("act", lambda e, h=h: e.activation(out=st_b[:, h, P(h)], in_=st_b[:, h, P(h)], func=AF.Exp, scale=-1.0),
                         rd=[t_stb[h]], wr=[t_stb[h]])
                    for dc in range(2):
                        b = group([(vv[:, mc, (hd * 2 + dc) * 128:(hd * 2 + dc + 1) * 128], sq[:, pts[mc], P(h)], [t_kv, t_sq[pts[mc]]])
                                   for mc in range(2)], h=h)
                        E.op("dve", lambda e, hd=hd, dc=dc, h=h, b=b: e.tensor_tensor(
                            out=qo[:, hd * 2 + dc, H(h)], in0=ps[:, b, P(h)], in1=st_b[:, h, P(h)], op=ALU.mult),
                            rd=[t_bank[b], t_stb[h]], wr=[t_qo[hd]])

                att_a(0)
                for u in range(8):
                    if u + 1 < 8:
                        att_a(u + 1)
                    att_b(u)
                for jp in range(8):
                    T = {}
                    for br in (1, 2):
                        s = next_tile()
                        wg = ring[:, s, :].rearrange("p (k c) -> p k c", c=256)
                        for sc in range(2):
                            cs_ = slice(sc * 128, (sc + 1) * 128)
                            for h in range(2):
                                bg_ = group([(wg[:, k, cs_], hb[:, k, H(h)], [t_slot[s], thb[k][h]]) for k in range(16)], h=h)
                                i1 = ntmp()
                                E.op("act", lambda e, i1=i1, bg_=bg_: e.activation(out=tmpf[:, i1, P(h)], in_=ps[:, bg_, P(h)], func=AF.Sigmoid),
                                     rd=[t_bank[bg_]], wr=[t_tmp[i1]])
                                T[br, sc, h] = i1
                                if bg_conv:
                                    E.drain_bg("dve", G_DRAIN)
                        rel(1)
                    s = next_tile()
                    wbx = ring[:, s, :].rearrange("p (b k c) -> p b k c", b=2, c=256)
                    for sc in range(2):
                        j = jp * 2 + sc
                        cs_ = slice(sc * 128, (sc + 1) * 128)
                        for h in range(2):
                            i1 = T[1, sc, h]; i2 = T[2, sc, h]
                            byb = group([(wbx[:, 0, k, cs_], bact[:, k, H(h)], [t_slot[s], t_bact]) for k in range(8)], h=h)
                            E.op("dve", lambda e, i1=i1, byb=byb: e.tensor_tensor(out=tmpf[:, i1, P(h)], in0=ps[:, byb, P(h)],
                                                                                in1=tmpf[:, i1, P(h)], op=ALU.mult),
                                 rd=[t_bank[byb], t_tmp[i1]], wr=[t_tmp[i1]])
                            byx = group([(wbx[:, 1, k, cs_], qo[:, k, H(h)], [t_slot[s]] + t_qo) for k in range(8)], h=h)
                            E.op("dve", lambda e, i2=i2, byx=byx: e.tensor_tensor(out=tmpf[:, i2, P(h)], in0=ps[:, byx, P(h)],
                                                                                in1=tmpf[:, i2, P(h)], op=ALU.mult),
                                 rd=[t_bank[byx], t_tmp[i2]], wr=[t_tmp[i2]])
                            E.op("dve", lambda e, i1=i1, i2=i2, j=j, h=h: e.tensor_tensor(
                                out=merged[:, j, H(h)], in0=tmpf[:, i1, P(h)], in1=tmpf[:, i2, P(h)], op=ALU.add),
                                rd=[t_tmp[i1], t_tmp[i2]], wr=[t_mrg])
                    rel(1)
                    if bg_conv and jp == LN_AT:
                        E.drain_bg("pool")
                        E.drain_bg("dve")
                        ln_silu()
                    elif bg_conv and jp == LN_AT + 1:
                        while ln_defer:
                            ln_defer.pop(0)()
                while ln_defer:
                    ln_defer.pop(0)()
                for jq in range(4):
                    T = {}
                    for half4 in range(2):
                        s = next_tile()
                        wg = ring[:, s, :].rearrange("p (k c) -> p k c", c=256)
                        for sc in range(2):
                            q4 = half4 * 2 + sc
                            cg = slice(sc * 128, (sc + 1) * 128)
                            for h in range(2):
                                bg0 = group([(wg[:, k, cg], hb[:, k, H(h)], [t_slot[s], thb[k][h]]) for k in range(16)], h=h)
                                i1 = ntmp()
                                E.op("act", lambda e, i1=i1, bg0=bg0: e.activation(out=tmpf[:, i1, P(h)], in_=ps[:, bg0, P(h)], func=AF.Sigmoid),
                                     rd=[t_bank[bg0]], wr=[t_tmp[i1]])
                                T[q4, h] = i1
                        rel(1)
                    if bg_conv and jq == 0 and LN_AT < 0:
                        E.drain_bg("pool")
                        E.drain_bg("dve")
                        ln_silu()
                    s = next_tile()
                    wa = ring[:, s, :].rearrange("p (k c) -> p k c", c=512)
                    for q4 in range(4):
                        j = jq * 4 + q4
                        ca = slice(q4 * 128, (q4 + 1) * 128)
                        for h in range(2):
                            i1 = T[q4, h]
                            bya = group([(wa[:, k, ca], aact[:, k, H(h)], [t_slot[s], t_aact]) for k in range(8)], h=h)
                            E.op("dve", lambda e, i1=i1, bya=bya: e.tensor_tensor(out=tmpf[:, i1, P(h)], in0=ps[:, bya, P(h)],
                                                                                in1=tmpf[:, i1, P(h)], op=ALU.mult),
                                 rd=[t_bank[bya], t_tmp[i1]], wr=[t_tmp[i1]])
                            E.op("dve", lambda e, i1=i1, j=j, h=h: e.tensor_tensor(
                                out=merged[:, j, H(h)], in0=merged[:, j, H(h)], in1=tmpf[:, i1, P(h)], op=ALU.add),
                                rd=[t_tmp[i1]], wr=[t_mrg])
                    rel(1)
                for k, v in list(t_s2.w.items()) + list(t_s2.r.items()):
                    if t_s1.r.get(k, 0) < v:
                        t_s1.r[k] = v
                pend = []
                for t, s in enumerate(tiles(8)):
                    for sc in range(2):
                        j = t * 2 + sc
                        std_unit(s, sc, 16, lambda k, h: merged[:, k, H(h)], [t_mrg],
                                 lambda h, b, j=j: zevac_with_stats(zm, t_s1, j, h, b, j == 0, j == 15, pend))
                        flush_stats(pend, keep=2)
                flush_stats(pend)
                post_norm_update(l, C_GPOST, zm, t_s1)
                t_s2.w = dict(t_s1.w); t_s2.r = dict(t_s1.r)
                ffn_stats = ffn_pre(l)
                for tk in (t_s1, t_s2, t_aact, t_bact, t_kv) + tuple(t_qo):
                    for k, v in list(tk.w.items()) + list(tk.r.items()):
                        if t_hid.r.get(k, 0) < v:
                            t_hid.r[k] = v
                for t, s in enumerate(tiles(32)):
                    for sc in range(2):
                        c = t * 2 + sc

                        def ev_up(h, b, c=c):
                            i = ntmp()
                            E.op("act", lambda e: e.activation(out=tmpf[:, i, P(h)], in_=ps[:, b, P(h)], func=AF.Relu),
                                 rd=[t_bank[b]], wr=[t_tmp[i]])
                            E.op("dve", lambda e: e.tensor_tensor(out=hid[:, c, H(h)], in0=tmpf[:, i, P(h)], in1=tmpf[:, i, P(h)], op=ALU.mult),
                                 rd=[t_tmp[i]], wr=[t_hid])
                        std_unit(s, sc, 16, hrhs, hbtk, ev_up)
                    if t >= 2:
                        for _ in range(2):
                            if ffn_stats:
                                ffn_stats.pop(0)()
                for tk in t_hb_all + [t_mrg]:
                    for k, v in list(tk.w.items()) + list(tk.r.items()):
                        if t_z.r.get(k, 0) < v:
                            t_z.r[k] = v
                pend = []
                for jc in range(16):
                    sl = [next_tile(), next_tile()]
                    for h in range(2):
                        mms = []
                        for kg in range(2):
                            wv = ring[:, sl[kg], :].rearrange("p (k c) -> p k c", c=128)
                            mms += [(wv[:, k, :], hid[:, kg * 32 + k, H(h)], [t_slot[sl[kg]], t_hid]) for k in range(32)]
                        b = group(mms, h=h)
                        zevac_with_stats(zf, t_z, jc, h, b, jc == 0, jc == 15, pend)
                    flush_stats(pend, keep=2)
                    rel(2)
                flush_stats(pend)
                for tk in (t_s1, t_s2, t_aact, t_bact, t_kv) + tuple(t_qo):
                    tk.w = dict(t_hid.w); tk.r = dict(t_hid.r)
                post_norm_rstd(eps_tile=True)
                post_norm_xupdate(l, C_GMPOST, zf, t_z, halves=(0,))
                if l + 1 < n_layers:
                    kv_norm(l + 1)
                elif blk + 1 < nblk:
                    kv_norm(0)
                post_norm_xupdate(l, C_GMPOST, zf, t_z, halves=(1,))
                for tk in t_hb_all:
                    tk.w = dict(t_z.w); tk.r = dict(t_z.r)
                t_mrg.w = dict(t_z.w); t_mrg.r = dict(t_z.r)
                if blk == 0:
                    for c in range(16):
                        E.op("dve", lambda e, c=c: e.tensor_scalar(out=xs[:, c, 0:HALO], in0=xs[:, c, 0:HALO],
                                                                  scalar1=hmask[:, 0:1], scalar2=None, op0=ALU.mult),
                             rd=[t_cst], wr=txs[c])
            if blk == 0:
                ev = E.dma("sp", "out", outT[:, :, 0:NT - HALO], xs[:, :, HALO:NT], rd=t_xs_all)
            else:
                ev = E.dma("sp", "out", outT[:, :, blk * NT - HALO:(blk + 1) * NT - HALO], xs[:], rd=t_xs_all)
            final_evs.append(ev)
        E.finalize(st, final_evs)
    return nc


def _std(W, c0, ncols=256, k0=0, nk=16):
    blk = W[k0 * 128:(k0 + nk) * 128, c0:c0 + ncols]
    return blk.reshape(nk, 128, ncols).transpose(1, 0, 2).reshape(128, nk * ncols)


def pack_layer_tiles(out, w_in, w_a_out, w_b_out, w_kv, w_x_out, w_o, w_up, w_down):
    i = 0

    def put(a):
        nonlocal i
        out[i] = a
        i += 1
    for t in range(8):
        put(_std(w_kv, t * 256))
    for base in (3072, 4096, 1024, 0, 2048, 5120):
        for t in range(4):
            put(_std(w_in, base + t * 256))
    for jp in range(8):
        put(_std(w_in, 8192 + jp * 256))
        put(_std(w_in, 10240 + jp * 256))
        put(np.concatenate([_std(w_b_out, jp * 256, nk=8), _std(w_x_out, jp * 256, nk=8)], axis=1))
    for jq in range(4):
        put(_std(w_in, 6144 + (2 * jq) * 256))
        put(_std(w_in, 6144 + (2 * jq + 1) * 256))
        put(_std(w_a_out, jq * 512, ncols=512, nk=8))
    for t in range(8):
        put(_std(w_o, t * 256))
    for t in range(32):
        put(_std(w_up, t * 256))
    for jc in range(16):
        for kg in range(2):
            put(_std(w_down, jc * 128, ncols=128, k0=kg * 32, nk=32))
    assert i == TPL


def pack_consts(layers, g_mix_pre, g_mix_post, g_mlp_pre, g_mlp_post, g_mem, conv_a_w, conv_a_b, ln_a_g, ln_a_b, conv_b_w):
    cst = np.zeros((128, len(layers) * C_PER), np.float32)
    for li, l in enumerate(layers):
        o = li * C_PER
        for col, g in ((C_GPRE, g_mix_pre), (C_GPOST, g_mix_post), (C_GMPRE, g_mlp_pre), (C_GMPOST, g_mlp_post), (C_GMEM, g_mem)):
            cst[:, o + col:o + col + 16] = g[l].reshape(16, 128).T
        cst[:, o + C_CAW:o + C_CAW + 8 * CK] = conv_a_w[l].reshape(CK, 8, 128).transpose(2, 1, 0).reshape(128, 8 * CK)
        cst[:, o + C_CAB:o + C_CAB + 8] = conv_a_b[l].reshape(8, 128).T
        cst[:, o + C_LNG:o + C_LNG + 8] = ln_a_g[l].reshape(8, 128).T
        cst[:, o + C_LNB:o + C_LNB + 8] = ln_a_b[l].reshape(8, 128).T
        cst[:, o + C_CBW:o + C_CBW + 24] = conv_b_w[l].reshape(3, 8, 128).transpose(2, 1, 0).reshape(128, 24)
    return cst


def shard_x(x2d, core):
    lo = core * TOK - HALO
    if lo < 0:
        blk = np.concatenate([np.zeros((HALO, D), np.float32), x2d[0:TOK]], axis=0)
    else:
        blk = x2d[lo:lo + TOK + HALO]
    return np.ascontiguousarray(blk.T.reshape(16, 128, TOK + HALO).transpose(1, 0, 2))


_PROG_CACHE = {}


def _get_prog(n_layers):
    if n_layers not in _PROG_CACHE:
        _PROG_CACHE[n_layers] = build_program(n_layers)
    return _PROG_CACHE[n_layers]


FUSED = True


def kernel(x, mem, g_mix_pre, w_in, conv_a_w, conv_a_b, ln_a_g, ln_a_b, w_a_out, conv_b_w, w_b_out,
           g_mem, w_kv, w_x_out, w_o, g_mix_post, g_mlp_pre, w_up, w_down, g_mlp_post):
    f = lambda a: np.asarray(a, dtype=np.float32)
    x2d = f(x)[0]
    memT = np.ascontiguousarray(f(mem)[0].T.reshape(16, 128, MEM).transpose(1, 0, 2))
    groups = [list(range(DEPTH))] if FUSED else [[l] for l in range(DEPTH)]
    for layers in groups:
        nl = len(layers)
        wts = np.empty((nl * TPL, 128, TILE), np.float32)
        for li, l in enumerate(layers):
            pack_layer_tiles(wts[li * TPL:(li + 1) * TPL], f(w_in[l]), f(w_a_out[l]), f(w_b_out[l]), f(w_kv[l]),
                             f(w_x_out[l]), f(w_o[l]), f(w_up[l]), f(w_down[l]))
        cst = pack_consts(layers, f(g_mix_pre), f(g_mix_post), f(g_mlp_pre), f(g_mlp_post), f(g_mem),
                          f(conv_a_w), f(conv_a_b), f(ln_a_g), f(ln_a_b), f(conv_b_w))
        nc = _get_prog(nl)
        in_maps = []
        for c in range(NCORE):
            in_maps.append({"xT": shard_x(x2d, c), "wts": wts, "cst": cst, "memT": memT,
                            "hmask": np.full((128, 1), 0.0 if c == 0 else 1.0, np.float32)})
        res = run_bass_kernel_spmd(nc, in_maps, core_ids=list(range(NCORE)))
        outs = []
        for c in range(NCORE):
            o = res.results[c]["outT"]
            outs.append(o.transpose(2, 1, 0).reshape(TOK, D))
        x2d = np.concatenate(outs, axis=0)
    return np.ascontiguousarray(x2d[None]).astype(np.float32)
```

```python
import numpy as np
from contextlib import ExitStack
import concourse.bass as bass
import concourse.mybir as mybir
from concourse.bass_utils import run_bass_kernel_spmd

F32 = mybir.dt.float32
BF16 = mybir.dt.bfloat16
AF = mybir.ActivationFunctionType
ALU = mybir.AluOpType

D = 2048
SEQ = 8192
DEPTH = 4
NCORE = 8
TOK = SEQ // NCORE
HALO = 128
NT = 576
NH = 288
NBLK = 2
MEM = 256
CK = 31
EPS = 1e-6
TPL = 140
TILE = 4096
NSLOT = 4
NTMP = 8
POOL_CHUNKS = 0
LN_AT = 5
TRIM = True
BG_RATE = 1
BALANCE = False
FAST_RECIP = False
LNEXP = True
KV_SPLIT = 4
Q_DRAIN = 4
G_DRAIN = 2
EARLY_CONV = True
LN_SKIP = 6
LN_DEFER = True
A_DRAIN = 4

C_GPRE, C_GPOST, C_GMPRE, C_GMPOST, C_GMEM = 0, 16, 32, 48, 64
C_CAW = 80
C_CAB = C_CAW + 8 * CK
C_LNG = C_CAB + 8
C_LNB = C_LNG + 8
C_CBW = C_LNB + 8
C_PER = C_CBW + 24

ENGS = ("pe", "act", "dve", "pool", "sp")


class Tk:
    __slots__ = ("w", "r")

    def __init__(self):
        self.w = {}
        self.r = {}


class _Rec:
    def __init__(self):
        self.call = None

    def __getattr__(self, name):
        def f(*a, **k):
            assert self.call is None
            self.call = (name, a, k)
        return f


def _eager(fn):
    r = _Rec()
    fn(r)
    name, a, k = r.call
    return lambda e: getattr(e, name)(*a, **k)


class Emitter:
    def __init__(self, nc, strict_same=False):
        self.nc = nc
        self.streams = {e: [] for e in ENGS}
        self.cnt = {e: 0 for e in ENGS}
        self.waited = {e: {} for e in ENGS}
        self.dma_cnt = {}
        self.strict_same = strict_same
        self.sems = {}
        self.bg = {"dve": [], "pool": []}
        self.bg_rate = {"dve": 0, "pool": 0}
        self.in_bg = False

    def _deps(self, rd, wr, extra):
        deps = {}
        for t in rd:
            for k, v in t.w.items():
                if deps.get(k, 0) < v:
                    deps[k] = v
        for t in wr:
            for d in (t.w, t.r):
                for k, v in d.items():
                    if deps.get(k, 0) < v:
                        deps[k] = v
        for d in extra:
            if d is None:
                continue
            for k, v in d.items():
                if deps.get(k, 0) < v:
                    deps[k] = v
        return deps

    def _waits(self, eng, deps, strict):
        waits = []
        for k, v in deps.items():
            if k == eng and not strict:
                continue
            if self.waited[eng].get(k, 0) >= v:
                continue
            self.waited[eng][k] = v
            waits.append((k, v))
        return waits

    def op(self, eng, fn, rd=(), wr=(), sig=True, extra=(), strict=None):
        strict = self.strict_same if strict is None else strict
        deps = self._deps(rd, wr, extra)
        waits = self._waits(eng, deps, strict)
        if sig:
            self.cnt[eng] += 1
            c = self.cnt[eng]
        else:
            c = self.cnt[eng] + 1
        self.streams[eng].append((waits, _eager(fn), 1 if sig else 0, eng))
        for t in rd:
            if t.r.get(eng, 0) < c:
                t.r[eng] = c
        for t in wr:
            t.w = {eng: c}
            t.r = {}
        ev = {eng: c}
        if eng == "dve" and self.bg["dve"] and not self.in_bg:
            self.drain_bg("dve", self.bg_rate["dve"])
        return ev

    def drain_bg(self, eng, n=None):
        self.in_bg = True
        k = 0
        q = self.bg[eng]
        while q and (n is None or k < n):
            q.pop(0)()
            k += 1
        self.in_bg = False

    def dma(self, q, slot, out, in_, rd=(), wr=(), extra=()):
        key = "dma:" + slot
        deps = self._deps(rd, wr, extra)
        waits = self._waits(q, deps, False)
        self.dma_cnt[key] = self.dma_cnt.get(key, 0) + 16
        c = self.dma_cnt[key]
        self.streams[q].append((waits, lambda e, o=out, i=in_: e.dma_start(out=o, in_=i), 16, key))
        for t in rd:
            if t.r.get(key, 0) < c:
                t.r[key] = c
        for t in wr:
            t.w = {key: c}
            t.r = {}
        if q == "pool" and self.bg["pool"] and not self.in_bg:
            self.drain_bg("pool", self.bg_rate["pool"])
        return {key: c}

    def finalize(self, st, final_waits):
        nc = self.nc
        keys = list(ENGS) + sorted(self.dma_cnt.keys())
        for k in keys:
            self.sems[k] = st.enter_context(nc.semaphore("s_" + k.replace(":", "_")))
        block = st.enter_context(nc.Block())
        fin = {}
        for d in final_waits:
            for k, v in d.items():
                fin[k] = max(fin.get(k, 0), v)

        def replay(eng_name):
            def run(e):
                for waits, fn, inc, semkey in self.streams[eng_name]:
                    for k, v in waits:
                        e.wait_ge(self.sems[k], v)
                    ins = fn(e)
                    if inc:
                        ins.then_inc(self.sems[semkey], inc)
                if eng_name == "sp":
                    for k, v in fin.items():
                        e.wait_ge(self.sems[k], v)
            return run

        block.tensor(replay("pe"))
        block.scalar(replay("act"))
        block.vector(replay("dve"))
        block.gpsimd(replay("pool"))
        block.sync(replay("sp"))


def build_program(n_layers, nblk=NBLK, bg_conv=True):
    nc = bass.Bass("TRN2", target_bir_lowering=False)
    xT = nc.dram_tensor("xT", [128, 16, nblk * NT], F32, kind="ExternalInput").ap()
    wts = nc.dram_tensor("wts", [n_layers * TPL, 128, TILE], F32, kind="ExternalInput").ap()
    cstd = nc.dram_tensor("cst", [128, n_layers * C_PER], F32, kind="ExternalInput").ap()
    memTd = nc.dram_tensor("memT", [128, 16, MEM], F32, kind="ExternalInput").ap()
    hmaskd = nc.dram_tensor("hmask", [128, 1], F32, kind="ExternalInput").ap()
    outT = nc.dram_tensor("outT", [128, 16, nblk * NT - HALO], F32, kind="ExternalOutput").ap()

    st = ExitStack()
    with st:
        def sb(name, shape, dt):
            return st.enter_context(nc.sbuf_tensor(name, shape, dt))

        cst = sb("cst_sb", [128, n_layers * C_PER], F32)
        hmask = sb("hmask_sb", [128, 1], F32)
        xs = sb("xs", [128, 16, NT], F32)
        ZA = sb("ZA", [128, 16 * NT], F32)
        HA = sb("HA", [128, 64 * NT], BF16)
        ring = sb("ring", [128, NSLOT, TILE], BF16)
        ones = sb("ones", [128, 128], BF16)
        sq = sb("sq", [128, 8, NH], BF16)
        st_a = sb("st_a", [128, 2, NH], F32)
        st_b = sb("st_b", [128, 2, NH], F32)
        st_c = sb("st_c", [128, 2, NH], F32)
        tmpf = sb("tmpf", [128, NTMP, NH], F32)
        ptmp = sb("ptmp", [128, NH], F32) if POOL_CHUNKS > 0 else None
        ahist = sb("ahist", [128, n_layers, 8, 30], F32)
        uhist = sb("uhist", [128, n_layers, 8, 2], F32)
        ps = st.enter_context(nc.psum_tensor("ps", [128, 8, 512], F32))

        ZAb = ZA[:].bitcast(BF16)
        hb = ZAb[:, 0:16 * NT].rearrange("p (c n) -> p c n", c=16)
        merged = ZAb[:, 16 * NT:32 * NT].rearrange("p (c n) -> p c n", c=16)
        u32 = ZA[:, 8 * NT:16 * NT].rearrange("p (c n) -> p c n", c=8)
        zf = ZA[:].rearrange("p (c n) -> p c n", c=16)
        HAf = HA[:].bitcast(F32)
        o_s1 = 0
        n_s1 = 8 * (NT + 32)
        o_s2 = n_s1
        n_s2 = 8 * NT
        s1 = HAf[:, o_s1:o_s1 + n_s1].rearrange("p (c n) -> p c n", c=8)
        s2 = HAf[:, o_s2:o_s2 + n_s2].rearrange("p (c n) -> p c n", c=8)
        zm = HAf[:, 0:16 * NT].rearrange("p (c n) -> p c n", c=16)
        ob16 = 2 * (n_s1 + n_s2)
        aact = HA[:, ob16:ob16 + 8 * NT].rearrange("p (c n) -> p c n", c=8)
        bact = HA[:, ob16 + 8 * NT:ob16 + 16 * NT].rearrange("p (c n) -> p c n", c=8)
        qo = HA[:, ob16 + 16 * NT:ob16 + 24 * NT].rearrange("p (c n) -> p c n", c=8)
        okv = ob16 + 24 * NT
        kT = HA[:, okv:okv + 8 * MEM].rearrange("p (c n) -> p c n", c=8)
        vv = HA[:, okv + 8 * MEM:okv + 16 * MEM].rearrange("p (c n) -> p c n", c=2)
        assert okv + 16 * MEM <= 64 * NT
        hid = HA[:].rearrange("p (c n) -> p c n", c=64)
        memf = HAf[:, 0:16 * MEM].rearrange("p (c n) -> p c n", c=16)
        memn = HA[:, 2 * o_s2:2 * o_s2 + 16 * MEM].rearrange("p (c n) -> p c n", c=16)

        E = Emitter(nc)
        t_cst = Tk(); txs = [[Tk(), Tk()] for _ in range(16)]; t_xs_all = [t for p in txs for t in p]; thb = [[Tk(), Tk()] for _ in range(16)]; t_hb_all = [t for p in thb for t in p]; t_mrg = Tk(); t_z = Tk()
        t_s1 = Tk(); t_s2 = Tk(); t_aact = Tk(); t_bact = Tk(); t_qo = [Tk() for _ in range(4)]
        t_kv = Tk(); t_hid = Tk(); t_mem = Tk(); t_memn = Tk()
        t_slot = [Tk() for _ in range(NSLOT)]
        t_bank = [Tk() for _ in range(8)]
        t_sq = [Tk() for _ in range(8)]
        t_sta = [Tk(), Tk()]; t_stb = [Tk(), Tk()]; t_stc = [Tk(), Tk()]
        t_tmp = [Tk() for _ in range(NTMP)]
        t_s2p = Tk(); t_ptmp = Tk()
        t_hist = Tk(); t_ones = Tk()
        rr = {"bank": 0, "sq": 0, "tmp": 0, "pt": 0, "tile": 0, "done": 0, "skip": set(), "skipn": 0}

        cur = {"lo": 0}
        LO = [8, 38, 68, 98]

        def MID():
            lo = cur["lo"]
            if lo == 0 or not BALANCE:
                return NH
            return lo + ((NT - lo) // 4) * 2 + ((NT - lo) % 4 > 0) * 2 if False else max(NH, ((lo + NT) // 4) * 2)

        def H(h):
            return slice(cur["lo"], MID()) if h == 0 else slice(MID(), NT)

        def P(h):
            sl = H(h)
            return slice(0, sl.stop - sl.start)

        def cs(l, col):
            c0 = l * C_PER + col
            return cst[:, c0:c0 + 1]

        def nbank():
            while True:
                b = rr["bank"]
                rr["bank"] = (b + 1) % 6
                if rr["skipn"] > 0 and b in rr["skip"]:
                    continue
                break
            if rr["skipn"] > 0:
                rr["skipn"] -= 1
                if rr["skipn"] == 0:
                    rr["skip"] = set()
            return b

        def nsq():
            i = rr["sq"]; rr["sq"] = (i + 1) % 8
            return i

        def ntmp():
            i = rr["tmp"]; rr["tmp"] = (i + 1) % NTMP
            return i


        E.dma("sp", "cst", cst[:], cstd[:], wr=[t_cst])
        E.dma("sp", "hmask", hmask[:], hmaskd[:], wr=[t_cst])
        E.op("dve", lambda e: e.memset(ones[:], 1.0), wr=[t_ones])
        E.op("dve", lambda e: e.memset(HAf[:, 0:n_s1 + n_s2], 0.0), wr=[t_s1, t_s2])
        E.op("dve", lambda e: e.memset(ahist[:], 0.0), wr=[t_hist])
        E.op("dve", lambda e: e.memset(uhist[:], 0.0), wr=[t_hist])

        tile_state = {"next_load": 0, "order": []}

        def prefetch(upto):
            while tile_state["next_load"] < min(upto, len(tile_state["order"])):
                i = tile_state["next_load"]
                s = i % NSLOT
                E.dma("pool", "w%d" % s, ring[:, s, :], wts[tile_state["order"][i]], wr=[t_slot[s]])
                tile_state["next_load"] += 1

        def next_tile():
            i = rr["tile"]
            rr["tile"] += 1
            assert i - rr["done"] < NSLOT
            prefetch(i + 1)
            return i % NSLOT

        def tiles(n):
            for _ in range(n):
                s_ = next_tile()
                yield s_
                rel(1)

        def rel(n=1):
            rr["done"] += n
            prefetch(rr["done"] + NSLOT)

        def group(mms, h=None, nout=None, bank=None):
            b = nbank() if bank is None else bank
            n = len(mms)
            psl = P(h) if h is not None else slice(0, nout)
            for i, (l_ap, r_ap, rdt) in enumerate(mms):
                E.op("pe", lambda e, l_ap=l_ap, r_ap=r_ap, i=i, b=b: e.matmul(
                    ps[:, b, psl], lhsT=l_ap, rhs=r_ap, start=(i == 0), stop=(i == n - 1)),
                    rd=rdt, wr=[t_bank[b]] if i == 0 else [], sig=(i == n - 1))
            t_bank[b].w = {"pe": E.cnt["pe"]}
            return b

        def colsum(srcs, bank, h=None, nout=None):
            return group([(ones[:], s_ap, [t_ones] + tk) for s_ap, tk in srcs], h=h, nout=nout, bank=bank)

        def rstd_from_bank(bank, h, dst, t_dst, inv_n, nout=None):
            sl = P(h) if nout is None else slice(0, nout)
            if LNEXP:
                E.op("act", lambda e: e.activation(out=dst[:, h, sl], in_=ps[:, bank, sl], func=AF.Ln,
                                                   bias=cst_eps[:], scale=inv_n),
                     rd=[t_bank[bank], t_cst], wr=[t_dst])
                E.op("act", lambda e: e.activation(out=dst[:, h, sl], in_=dst[:, h, sl], func=AF.Exp, scale=-0.5),
                     rd=[t_dst], wr=[t_dst])
                return
            E.op("act", lambda e: e.activation(out=dst[:, h, sl], in_=ps[:, bank, sl], func=AF.Sqrt,
                                               bias=cst_eps[:], scale=inv_n),
                 rd=[t_bank[bank], t_cst], wr=[t_dst])
            E.op("dve", lambda e: e.reciprocal(out=dst[:, h, sl], in_=dst[:, h, sl]), rd=[t_dst], wr=[t_dst])

        st_mem = sb("st_mem", [128, 1, MEM], F32)
        t_stmem = Tk()
        cst_eps = sb("cst_eps", [128, 1], F32)
        E.op("dve", lambda e: e.memset(cst_eps[:], EPS), wr=[t_cst])

        def rms_pre(l, gcol):
            for h in range(2):
                b = 6 + h
                for c in range(16):
                    i = nsq()
                    E.op("act", lambda e, c=c, i=i, h=h: e.activation(out=sq[:, i, P(h)], in_=xs[:, c, H(h)], func=AF.Square),
                         rd=[txs[c][h]], wr=[t_sq[i]])
                    E.op("pe", lambda e, i=i, c=c, b=b: e.matmul(
                        ps[:, b, P(h)], lhsT=ones[:], rhs=sq[:, i, P(h)], start=(c == 0), stop=(c == 15)),
                        rd=[t_ones, t_sq[i]], wr=[t_bank[b]] if c == 0 else [], sig=True)
                t_bank[b].w = {"pe": E.cnt["pe"]}
                rstd_from_bank(b, h, st_a, t_sta[h], 1.0 / D)
                for c in range(16):
                    E.op("dve", lambda e, c=c, h=h: e.scalar_tensor_tensor(
                        out=hb[:, c, H(h)], in0=xs[:, c, H(h)], scalar=cs(l, gcol + c), in1=st_a[:, h, P(h)],
                        op0=ALU.mult, op1=ALU.mult), rd=[txs[c][h], t_sta[h], t_cst], wr=[thb[c][h]])

        def std_unit(slot, sc, nk, rhs_fn, rhs_tk, evac, ncols=256, koff=0):
            wv = ring[:, slot, :].rearrange("p (k c) -> p k c", c=ncols)
            for h in range(2):
                b = group([(wv[:, koff + k, sc * 128:(sc + 1) * 128], rhs_fn(k, h),
                            [t_slot[slot]] + (rhs_tk(k, h) if callable(rhs_tk) else rhs_tk))
                           for k in range(nk)], h=h)
                evac(h, b)

        def post_norm_rstd(eps_tile=False):
            for h in range(2):
                if eps_tile:
                    b = 6 + h
                    E.op("dve", lambda e, h=h, b=b: e.scalar_tensor_tensor(
                        out=st_a[:, h, P(h)], in0=ps[:, b, P(h)], scalar=1.0 / D, in1=st_b[:, h, P(h)],
                        op0=ALU.mult, op1=ALU.add), rd=[t_bank[b], t_stb[h]], wr=[t_sta[h]])
                    E.op("act", lambda e, h=h: e.activation(out=st_a[:, h, P(h)], in_=st_a[:, h, P(h)], func=AF.Ln),
                         rd=[t_sta[h]], wr=[t_sta[h]])
                    E.op("act", lambda e, h=h: e.activation(out=st_a[:, h, P(h)], in_=st_a[:, h, P(h)], func=AF.Exp, scale=-0.5),
                         rd=[t_sta[h]], wr=[t_sta[h]])
                else:
                    rstd_from_bank(6 + h, h, st_a, t_sta[h], 1.0 / D)

        def post_norm_xupdate(l, gcol, z, t_zz, halves=(0, 1)):
            for h in halves:
                for c in range(16):
                    i = ntmp()
                    E.op("dve", lambda e, c=c, h=h, i=i: e.scalar_tensor_tensor(
                        out=tmpf[:, i, P(h)], in0=z[:, c, H(h)], scalar=cs(l, gcol + c), in1=st_a[:, h, P(h)],
                        op0=ALU.mult, op1=ALU.mult), rd=[t_zz, t_sta[h], t_cst], wr=[t_tmp[i]])
                    E.op("dve", lambda e, c=c, h=h, i=i: e.tensor_tensor(
                        out=xs[:, c, H(h)], in0=xs[:, c, H(h)], in1=tmpf[:, i, P(h)], op=ALU.add),
                        rd=[t_tmp[i]], wr=[txs[c][h]])

        def post_norm_update(l, gcol, z, t_zz, eps_tile=False):
            post_norm_rstd(eps_tile)
            post_norm_xupdate(l, gcol, z, t_zz)

        def ffn_pre(l):
            for h in range(2):
                for c in range(16):
                    E.op("act", lambda e, c=c, h=h: e.activation(out=hb[:, c, H(h)], in_=xs[:, c, H(h)], func=AF.Copy,
                                                               scale=cs(l, C_GMPRE + c)),
                         rd=[txs[c][h], t_cst], wr=[thb[c][h]])
            th = []
            lag = []

            def pe_one():
                (i, c, h) = lag.pop(0)
                b = 6 + h
                E.op("pe", lambda e: e.matmul(ps[:, b, P(h)], lhsT=ones[:], rhs=sq[:, i, P(h)], start=(c == 0), stop=(c == 15)),
                     rd=[t_ones, t_sq[i]], wr=[t_bank[b]] if c == 0 else [], sig=True)
                t_bank[b].w = {"pe": E.cnt["pe"]}
                if c == 15:
                    E.op("dve", lambda e: e.tensor_scalar(out=st_b[:, h, P(h)], in0=ps[:, b, P(h)], scalar1=1.0 / D,
                                                          scalar2=EPS, op0=ALU.mult, op1=ALU.add),
                         rd=[t_bank[b]], wr=[t_stb[h]])
                    E.op("dve", lambda e: e.tensor_tensor(out=st_b[:, h, P(h)], in0=st_b[:, h, P(h)], in1=st_b[:, h, P(h)], op=ALU.mult),
                         rd=[t_stb[h]], wr=[t_stb[h]])
                    E.op("dve", lambda e: e.tensor_scalar(out=st_b[:, h, P(h)], in0=st_b[:, h, P(h)], scalar1=EPS,
                                                          scalar2=None, op0=ALU.mult),
                         rd=[t_stb[h]], wr=[t_stb[h]])
            for h in range(2):
                for c in range(16):
                    def f(c=c, h=h):
                        i = nsq()
                        E.op("act", lambda e: e.activation(out=sq[:, i, P(h)], in_=xs[:, c, H(h)], func=AF.Square),
                             rd=[txs[c][h]], wr=[t_sq[i]])
                        lag.append((i, c, h))
                        if len(lag) > 3:
                            pe_one()
                    th.append(f)

            def fin():
                while lag:
                    pe_one()
            th.append(fin)
            return th

        def zevac_with_stats(z, t_zz, j, h, b, first, last, pend):
            E.op("act", lambda e: e.activation(out=z[:, j, H(h)], in_=ps[:, b, P(h)], func=AF.Copy),
                 rd=[t_bank[b]], wr=[t_zz])
            i = nsq()
            E.op("act", lambda e: e.activation(out=sq[:, i, P(h)], in_=ps[:, b, P(h)], func=AF.Square),
                 rd=[t_bank[b]], wr=[t_sq[i]])
            pend.append((i, h, first, last))

        def flush_stats(pend, keep=0):
            while len(pend) > keep:
                i, h, first, last = pend.pop(0)
                b = 6 + h
                E.op("pe", lambda e, i=i, b=b, first=first, last=last: e.matmul(
                    ps[:, b, P(h)], lhsT=ones[:], rhs=sq[:, i, P(h)], start=first, stop=last),
                    rd=[t_ones, t_sq[i]], wr=[t_bank[b]] if first else [], sig=True)
                t_bank[b].w = {"pe": E.cnt["pe"]}

        kv_state = {"have_rstd": False}

        def kv_norm(l):
            E.dma("sp", "mem", memf[:], memTd[:], wr=[t_s1])
            if not kv_state["have_rstd"]:
                kv_state["have_rstd"] = True
                b = nbank()
                for c in range(16):
                    i = nsq()
                    E.op("act", lambda e, c=c, i=i: e.activation(out=sq[:, i, 0:MEM], in_=memf[:, c, :], func=AF.Square),
                         rd=[t_s1], wr=[t_sq[i]])
                    E.op("pe", lambda e, i=i, c=c, b=b: e.matmul(
                        ps[:, b, 0:MEM], lhsT=ones[:], rhs=sq[:, i, 0:MEM], start=(c == 0), stop=(c == 15)),
                        rd=[t_ones, t_sq[i]], wr=[t_bank[b]] if c == 0 else [], sig=True)
                t_bank[b].w = {"pe": E.cnt["pe"]}
                rstd_from_bank(b, 0, st_mem, t_stmem, 1.0 / D, nout=MEM)
            for c in range(16):
                E.op("dve", lambda e, c=c, g_ap=cs(l, C_GMEM + c): e.scalar_tensor_tensor(
                    out=memn[:, c, :], in0=memf[:, c, :], scalar=g_ap, in1=st_mem[:, 0, 0:MEM],
                    op0=ALU.mult, op1=ALU.mult), rd=[t_s1, t_stmem, t_cst], wr=[t_s2])

        def kv_tiles(t0, t1):
            for t8 in range(t0, t1):
                s = next_tile()
                wv = ring[:, s, :].rearrange("p (k c) -> p k c", c=256)
                if t8 < 4:
                    t = t8
                    for sc in range(2):
                        dch = t * 2 + sc
                        b = group([(wv[:, k, sc * 128:(sc + 1) * 128], memn[:, k, :], [t_slot[s], t_s2])
                                   for k in range(16)], nout=MEM)
                        E.op("act", lambda e, dch=dch, b=b: e.activation(out=kT[:, dch, :], in_=ps[:, b, 0:MEM], func=AF.Copy),
                             rd=[t_bank[b]], wr=[t_kv])
                else:
                    t = t8 - 4
                    for mc in range(2):
                        b = group([(memn[:, k, mc * 128:(mc + 1) * 128], wv[:, k, :], [t_slot[s], t_s2])
                                   for k in range(16)], nout=256)
                        E.op("act", lambda e, t=t, mc=mc, b=b: e.activation(out=vv[:, mc, t * 256:(t + 1) * 256],
                                                                         in_=ps[:, b, 0:256], func=AF.Copy),
                             rd=[t_bank[b]], wr=[t_kv])
                rel(1)

        final_evs = []
        for blk in range(nblk):
            base_i = len(tile_state["order"])
            tile_state["order"].extend(list(range(n_layers * TPL)))
            prefetch(rr["done"] + NSLOT)
            E.dma("sp", "xin", xs[:], xT[:, :, blk * NT:(blk + 1) * NT], wr=t_xs_all)
            for l in range(n_layers):
                cur["lo"] = LO[l + DEPTH - n_layers] if (blk == 0 and TRIM) else 0
                if l == 0 and blk == 0:
                    kv_norm(0)
                kv_tiles(0, KV_SPLIT)
                rms_pre(l, C_GPRE)
                kv_tiles(KV_SPLIT, 8)
                hrhs = lambda k, h: hb[:, k, H(h)]
                hbtk = lambda k, h: [thb[k][h]]
                for t, s in enumerate(tiles(4)):
                    for sc in range(2):
                        c = t * 2 + sc
                        std_unit(s, sc, 16, hrhs, hbtk, lambda h, b, c=c: E.op(
                            "act", lambda e: e.activation(out=s2[:, c, H(h)], in_=ps[:, b, P(h)], func=AF.Copy),
                            rd=[t_bank[b]], wr=[t_s2]))
                E.op("dve", lambda e, l=l: e.tensor_copy(out=s1[:, :, 30:32], in_=uhist[:, l, :, :]),
                     rd=[t_hist], wr=[t_s1], strict=True)
                for t, s in enumerate(tiles(4)):
                    for sc in range(2):
                        c = t * 2 + sc
                        std_unit(s, sc, 16, hrhs, hbtk, lambda h, b, c=c: E.op(
                            "dve", lambda e: e.tensor_tensor(out=s1[:, c, 32 + H(h).start:32 + H(h).stop], in0=ps[:, b, P(h)],
                                                             in1=s2[:, c, H(h)], op=ALU.mult),
                            rd=[t_bank[b], t_s2], wr=[t_s1]))
                E.op("dve", lambda e, l=l: e.tensor_copy(out=uhist[:, l, :, :], in_=s1[:, :, 32 + NT - 2:32 + NT]),
                     rd=[t_s1], wr=[t_hist], strict=True)
                for c in range(8):
                    for h in range(2):
                        E.op("dve", lambda e, c=c, h=h, w_ap=cs(l, C_CBW + c * 3 + 0): e.tensor_scalar(
                            out=u32[:, c, H(h)], in0=s1[:, c, 30 + H(h).start:30 + H(h).stop],
                            scalar1=w_ap, scalar2=None, op0=ALU.mult),
                            rd=[t_s1, t_cst], wr=[t_mrg])
                        for k in (1, 2):
                            E.op("dve", lambda e, c=c, h=h, k=k, w_ap=cs(l, C_CBW + c * 3 + k): e.scalar_tensor_tensor(
                                out=u32[:, c, H(h)], in0=s1[:, c, 30 + k + H(h).start:30 + k + H(h).stop],
                                scalar=w_ap, in1=u32[:, c, H(h)], op0=ALU.mult, op1=ALU.add),
                                rd=[t_s1, t_cst], wr=[t_mrg])
                for t, s in enumerate(tiles(4)):
                    for sc in range(2):
                        c = t * 2 + sc
                        std_unit(s, sc, 16, hrhs, hbtk, lambda h, b, c=c: E.op(
                            "act", lambda e: e.activation(out=s2[:, c, H(h)], in_=ps[:, b, P(h)], func=AF.Sigmoid),
                            rd=[t_bank[b]], wr=[t_s2]))
                E.op("dve", lambda e, l=l: e.tensor_copy(out=s1[:, :, 2:32], in_=ahist[:, l, :, :]),
                     rd=[t_hist], wr=[t_s1], strict=True)
                def conv_chunk_ops(c, l=l):
                    ops = []

                    def first():
                        E.op("dve", lambda e: e.tensor_scalar(
                            out=s2[:, c, cur["lo"]:NT], in0=s1[:, c, 2 + cur["lo"]:2 + NT],
                            scalar1=cs(l, C_CAW + c * CK), scalar2=cs(l, C_CAB + c), op0=ALU.mult, op1=ALU.add),
                            rd=[t_s1, t_cst], wr=[t_s2])
                    ops.append(first)
                    for k in range(1, CK):
                        def tap(k=k):
                            E.op("dve", lambda e: e.scalar_tensor_tensor(
                                out=s2[:, c, cur["lo"]:NT], in0=s1[:, c, 2 + k + cur["lo"]:2 + k + NT],
                                scalar=cs(l, C_CAW + c * CK + k), in1=s2[:, c, cur["lo"]:NT], op0=ALU.mult, op1=ALU.add),
                                rd=[t_s1, t_cst], wr=[t_s2])
                        ops.append(tap)
                    return ops

                early = bg_conv and EARLY_CONV and POOL_CHUNKS == 0
                if early:
                    E.bg_rate["dve"] = BG_RATE
                for t, s in enumerate(tiles(4)):
                    for sc in range(2):
                        c = t * 2 + sc
                        std_unit(s, sc, 16, hrhs, hbtk, lambda h, b, c=c: E.op(
                            "dve", lambda e: e.tensor_tensor(out=s1[:, c, 32 + H(h).start:32 + H(h).stop], in0=ps[:, b, P(h)],
                                                             in1=s2[:, c, H(h)], op=ALU.mult),
                            rd=[t_bank[b], t_s2], wr=[t_s1]))
                        if early:
                            E.drain_bg("dve", A_DRAIN)
                    if early:
                        E.bg["dve"].extend(conv_chunk_ops(2 * t) + conv_chunk_ops(2 * t + 1))
                E.op("dve", lambda e, l=l: e.tensor_copy(out=ahist[:, l, :, :], in_=s1[:, :, 32 + NT - 30:32 + NT]),
                     rd=[t_s1], wr=[t_hist], strict=True)

                if cur["lo"] > 0:
                    cur["lo"] += 30
                NDC = 8 - POOL_CHUNKS
                t_s2p.w = dict(t_s2.w); t_s2p.r = dict(t_s2.r)

                def conv_ops(l=l):
                    ops = []
                    for c in range(NDC):
                        ops += conv_chunk_ops(c)
                    return ops

                def conv_ops_pool(l=l):
                    ops = []
                    for c in range(NDC, 8):
                        for h in range(2):
                            def first(c=c, h=h):
                                E.op("pool", lambda e: e.tensor_scalar(
                                    out=s2[:, c, H(h)], in0=s1[:, c, 2 + H(h).start:2 + H(h).stop],
                                    scalar1=cs(l, C_CAW + c * CK), scalar2=cs(l, C_CAB + c), op0=ALU.mult, op1=ALU.add),
                                    rd=[t_s1, t_cst], wr=[t_s2p])
                            ops.append(first)
                            for k in range(1, CK):
                                def tap(c=c, h=h, k=k):
                                    E.op("pool", lambda e: e.tensor_scalar(
                                        out=ptmp[:], in0=s1[:, c, 2 + k + H(h).start:2 + k + H(h).stop],
                                        scalar1=cs(l, C_CAW + c * CK + k), scalar2=None, op0=ALU.mult),
                                        rd=[t_s1, t_cst], wr=[t_ptmp])
                                    E.op("pool", lambda e: e.tensor_tensor(
                                        out=s2[:, c, H(h)], in0=s2[:, c, H(h)], in1=ptmp[:], op=ALU.add),
                                        rd=[t_ptmp], wr=[t_s2p])
                                ops.append(tap)
                    return ops

                ln_defer = []

                def ln_silu(l=l):
                    bsum = [6, 7]
                    bsq = [nbank(), nbank()]
                    rr["skip"] = set(bsq); rr["skipn"] = LN_SKIP
                    lag = []

                    def pe_flush(keep):
                        while len(lag) > keep:
                            (i, bnk, h, first, last) = lag.pop(0)
                            E.op("pe", lambda e, i=i, bnk=bnk, h=h, first=first, last=last: e.matmul(
                                ps[:, bnk, P(h)], lhsT=ones[:], rhs=sq[:, i, P(h)], start=first, stop=last),
                                rd=[t_ones, t_sq[i]], wr=[t_bank[bnk]] if first else [], sig=True)
                            t_bank[bnk].w = {"pe": E.cnt["pe"]}
                    for h in range(2):
                        for c in range(8):
                            i = nsq()
                            E.op("act", lambda e, c=c, i=i, h=h: e.activation(out=sq[:, i, P(h)], in_=s2[:, c, H(h)], func=AF.Copy),
                                 rd=[t_s2, t_s2p], wr=[t_sq[i]])
                            lag.append((i, bsum[h], h, c == 0, c == 7))
                            j = nsq()
                            E.op("act", lambda e, c=c, j=j, h=h: e.activation(out=sq[:, j, P(h)], in_=s2[:, c, H(h)], func=AF.Square),
                                 rd=[t_s2, t_s2p], wr=[t_sq[j]])
                            lag.append((j, bsq[h], h, c == 0, c == 7))
                            pe_flush(4)
                    pe_flush(0)
                    for h in range(2):
                        E.op("dve", lambda e, h=h: e.tensor_scalar(
                            out=st_b[:, h, P(h)], in0=ps[:, bsum[h], P(h)], scalar1=1.0 / 1024, scalar2=None, op0=ALU.mult),
                            rd=[t_bank[bsum[h]]], wr=[t_stb[h]])
                        E.op("dve", lambda e, h=h: e.tensor_tensor(out=st_a[:, h, P(h)], in0=st_b[:, h, P(h)], in1=st_b[:, h, P(h)], op=ALU.mult),
                             rd=[t_stb[h]], wr=[t_sta[h]])
                        E.op("dve", lambda e, h=h: e.scalar_tensor_tensor(
                            out=st_c[:, h, P(h)], in0=ps[:, bsq[h], P(h)], scalar=1.0 / 1024, in1=st_a[:, h, P(h)],
                            op0=ALU.mult, op1=ALU.subtract), rd=[t_bank[bsq[h]], t_sta[h]], wr=[t_stc[h]])
                        E.op("act", lambda e, h=h: e.activation(out=st_c[:, h, P(h)], in_=st_c[:, h, P(h)], func=AF.Ln,
                                                             bias=cst_eps[:], scale=1.0), rd=[t_stc[h], t_cst], wr=[t_stc[h]])
                        E.op("act", lambda e, h=h: e.activation(out=st_c[:, h, P(h)], in_=st_c[:, h, P(h)], func=AF.Exp, scale=-0.5),
                             rd=[t_stc[h]], wr=[t_stc[h]])
                    for h in range(2):
                        for c in range(8):
                            E.op("dve", lambda e, c=c, h=h: e.tensor_tensor(out=s2[:, c, H(h)], in0=s2[:, c, H(h)],
                                                                          in1=st_b[:, h, P(h)], op=ALU.subtract),
                                 rd=[t_stb[h], t_s2p], wr=[t_s2])
                            E.op("dve", lambda e, c=c, h=h: e.tensor_tensor(out=s2[:, c, H(h)], in0=s2[:, c, H(h)],
                                                                          in1=st_c[:, h, P(h)], op=ALU.mult),
                                 rd=[t_stc[h]], wr=[t_s2])
                            def silu_op(c=c, h=h):
                                E.op("act", lambda e: e.activation(out=aact[:, c, H(h)], in_=s2[:, c, H(h)], func=AF.Silu,
                                                                   bias=cs(l, C_LNB + c), scale=cs(l, C_LNG + c)),
                                     rd=[t_s2, t_cst], wr=[t_aact])
                            if LN_DEFER and bg_conv:
                                ln_defer.append(silu_op)
                            else:
                                silu_op()
                    for k_, v_ in list(t_s2p.r.items()) + list(t_s2p.w.items()):
                        if t_s2.r.get(k_, 0) < v_:
                            t_s2.r[k_] = v_

                cops = [] if early else conv_ops()
                pops = conv_ops_pool()
                if bg_conv:
                    if not early:
                        E.bg["dve"] = cops
                    E.bg_rate["dve"] = BG_RATE
                    E.bg["pool"] = pops
                    E.bg_rate["pool"] = 14
                else:
                    for f in pops:
                        f()
                    for f in cops:
                        f()
                    ln_silu()

                for t, s in enumerate(tiles(4)):
                    for sc in range(2):
                        c = t * 2 + sc
                        std_unit(s, sc, 16, hrhs, hbtk, lambda h, b, c=c: E.op(
                            "dve", lambda e: e.tensor_tensor(out=bact[:, c, H(h)], in0=ps[:, b, P(h)],
                                                             in1=u32[:, c, H(h)], op=ALU.mult),
                            rd=[t_bank[b], t_mrg], wr=[t_bact]))
                        if bg_conv:
                            E.drain_bg("dve", A_DRAIN)
                for t, s in enumerate(tiles(4)):
                    for sc in range(2):
                        c = t * 2 + sc
                        std_unit(s, sc, 16, hrhs, hbtk, lambda h, b, c=c: E.op(
                            "act", lambda e: e.activation(out=qo[:, c, H(h)], in_=ps[:, b, P(h)], func=AF.Copy),
                            rd=[t_bank[b]], wr=[t_qo[c // 2]]))
                        if bg_conv:
                            E.drain_bg("dve", Q_DRAIN)
                units = [(hd, h) for hd in range(4) for h in range(2)]
                upts = {}

                def att_a(u):
                    hd, h = units[u]
                    pts = []
                    for mc in range(2):
                        b = group([(kT[:, hd * 2 + dc, mc * 128:(mc + 1) * 128], qo[:, hd * 2 + dc, H(h)], [t_kv, t_qo[hd]])
                                   for dc in range(2)], h=h)
                        i = nsq()
                        E.op("act", lambda e, i=i, b=b: e.activation(out=sq[:, i, P(h)], in_=ps[:, b, P(h)], func=AF.Exp,
                                                                   scale=1.0 / 16.0),
                             rd=[t_bank[b]], wr=[t_sq[i]])
                        pts.append(i)
                    upts[u] = pts

                def att_b(u):
                    hd, h = units[u]
                    pts = upts[u]
                    bden = colsum([(sq[:, i, P(h)], [t_sq[i]]) for i in pts], 6 + h, h=h)
                    E.op("act", lambda e, h=h, bden=bden: e.activation(out=st_b[:, h, P(h)], in_=ps[:, bden, P(h)], func=AF.Ln),
                         rd=[t_bank[bden]], wr=[t_stb[h]])
                    E.op("act", lambda e, h=h: e.activation(out=st_b[:, h, P(h)], in_=st_b[:, h, P(h)], func=AF.Exp, scale=-1.0),
                         rd=[t_stb[h]], wr=[t_stb[h]])
                    for dc in range(2):
                        b = group([(vv[:, mc, (hd * 2 + dc) * 128:(hd * 2 + dc + 1) * 128], sq[:, pts[mc], P(h)], [t_kv, t_sq[pts[mc]]])
                                   for mc in range(2)], h=h)
                        E.op("dve", lambda e, hd=hd, dc=dc, h=h, b=b: e.tensor_tensor(
                            out=qo[:, hd * 2 + dc, H(h)], in0=ps[:, b, P(h)], in1=st_b[:, h, P(h)], op=ALU.mult),
                            rd=[t_bank[b], t_stb[h]], wr=[t_qo[hd]])

                att_a(0)
                for u in range(8):
                    if u + 1 < 8:
                        att_a(u + 1)
                    att_b(u)
                for jp in range(8):
                    T = {}
                    for br in (1, 2):
                        s = next_tile()
                        wg = ring[:, s, :].rearrange("p (k c) -> p k c", c=256)
                        for sc in range(2):
                            cs_ = slice(sc * 128, (sc + 1) * 128)
                            for h in range(2):
                                bg_ = group([(wg[:, k, cs_], hb[:, k, H(h)], [t_slot[s], thb[k][h]]) for k in range(16)], h=h)
                                i1 = ntmp()
                                E.op("act", lambda e, i1=i1, bg_=bg_: e.activation(out=tmpf[:, i1, P(h)], in_=ps[:, bg_, P(h)], func=AF.Sigmoid),
                                     rd=[t_bank[bg_]], wr=[t_tmp[i1]])
                                T[br, sc, h] = i1
                                if bg_conv:
                                    E.drain_bg("dve", G_DRAIN)
                        rel(1)
                    s = next_tile()
                    wbx = ring[:, s, :].rearrange("p (b k c) -> p b k c", b=2, c=256)
                    for sc in range(2):
                        j = jp * 2 + sc
                        cs_ = slice(sc * 128, (sc + 1) * 128)
                        for h in range(2):
                            i1 = T[1, sc, h]; i2 = T[2, sc, h]
                            byb = group([(wbx[:, 0, k, cs_], bact[:, k, H(h)], [t_slot[s], t_bact]) for k in range(8)], h=h)
                            E.op("dve", lambda e, i1=i1, byb=byb: e.tensor_tensor(out=tmpf[:, i1, P(h)], in0=ps[:, byb, P(h)],
                                                                                in1=tmpf[:, i1, P(h)], op=ALU.mult),
                                 rd=[t_bank[byb], t_tmp[i1]], wr=[t_tmp[i1]])
                            byx = group([(wbx[:, 1, k, cs_], qo[:, k, H(h)], [t_slot[s]] + t_qo) for k in range(8)], h=h)
                            E.op("dve", lambda e, i2=i2, byx=byx: e.tensor_tensor(out=tmpf[:, i2, P(h)], in0=ps[:, byx, P(h)],
                                                                                in1=tmpf[:, i2, P(h)], op=ALU.mult),
                                 rd=[t_bank[byx], t_tmp[i2]], wr=[t_tmp[i2]])
                            E.op("dve", lambda e, i1=i1, i2=i2, j=j, h=h: e.tensor_tensor(
                                out=merged[:, j, H(h)], in0=tmpf[:, i1, P(h)], in1=tmpf[:, i2, P(h)], op=ALU.add),
                                rd=[t_tmp[i1], t_tmp[i2]], wr=[t_mrg])
                    rel(1)
                    if bg_conv and jp == LN_AT:
                        E.drain_bg("pool")
                        E.drain_bg("dve")
                        ln_silu()
                    elif bg_conv and jp == LN_AT + 1:
                        while ln_defer:
                            ln_defer.pop(0)()
                while ln_defer:
                    ln_defer.pop(0)()
                for jq in range(4):
                    T = {}
                    for half4 in range(2):
                        s = next_tile()
                        wg = ring[:, s, :].rearrange("p (k c) -> p k c", c=256)
                        for sc in range(2):
                            q4 = half4 * 2 + sc
                            cg = slice(sc * 128, (sc + 1) * 128)
                            for h in range(2):
                                bg0 = group([(wg[:, k, cg], hb[:, k, H(h)], [t_slot[s], thb[k][h]]) for k in range(16)], h=h)
                                i1 = ntmp()
                                E.op("act", lambda e, i1=i1, bg0=bg0: e.activation(out=tmpf[:, i1, P(h)], in_=ps[:, bg0, P(h)], func=AF.Sigmoid),
                                     rd=[t_bank[bg0]], wr=[t_tmp[i1]])
                                T[q4, h] = i1
                        rel(1)
                    if bg_conv and jq == 0 and LN_AT < 0:
                        E.drain_bg("pool")
                        E.drain_bg("dve")
                        ln_silu()
                    s = next_tile()
                    wa = ring[:, s, :].rearrange("p (k c) -> p k c", c=512)
                    for q4 in range(4):
                        j = jq * 4 + q4
                        ca = slice(q4 * 128, (q4 + 1) * 128)
                        for h in range(2):
                            i1 = T[q4, h]
                            bya = group([(wa[:, k, ca], aact[:, k, H(h)], [t_slot[s], t_aact]) for k in range(8)], h=h)
                            E.op("dve", lambda e, i1=i1, bya=bya: e.tensor_tensor(out=tmpf[:, i1, P(h)], in0=ps[:, bya, P(h)],
                                                                                in1=tmpf[:, i1, P(h)], op=ALU.mult),
                                 rd=[t_bank[bya], t_tmp[i1]], wr=[t_tmp[i1]])
                            E.op("dve", lambda e, i1=i1, j=j, h=h: e.tensor_tensor(
                                out=merged[:, j, H(h)], in0=merged[:, j, H(h)], in1=tmpf[:, i1, P(h)], op=ALU.add),
                                rd=[t_tmp[i1]], wr=[t_mrg])
                    rel(1)
                for k, v in list(t_s2.w.items()) + list(t_s2.r.items()):
                    if t_s1.r.get(k, 0) < v:
                        t_s1.r[k] = v
                pend = []
                for t, s in enumerate(tiles(8)):
                    for sc in range(2):
                        j = t * 2 + sc
                        std_unit(s, sc, 16, lambda k, h: merged[:, k, H(h)], [t_mrg],
                                 lambda h, b, j=j: zevac_with_stats(zm, t_s1, j, h, b, j == 0, j == 15, pend))
                        flush_stats(pend, keep=2)
                flush_stats(pend)
                post_norm_update(l, C_GPOST, zm, t_s1)
                t_s2.w = dict(t_s1.w); t_s2.r = dict(t_s1.r)
                ffn_stats = ffn_pre(l)
                for tk in (t_s1, t_s2, t_aact, t_bact, t_kv) + tuple(t_qo):
                    for k, v in list(tk.w.items()) + list(tk.r.items()):
                        if t_hid.r.get(k, 0) < v:
                            t_hid.r[k] = v
                for t, s in enumerate(tiles(32)):
                    for sc in range(2):
                        c = t * 2 + sc

                        def ev_up(h, b, c=c):
                            i = ntmp()
                            E.op("act", lambda e: e.activation(out=tmpf[:, i, P(h)], in_=ps[:, b, P(h)], func=AF.Relu),
                                 rd=[t_bank[b]], wr=[t_tmp[i]])
                            E.op("dve", lambda e: e.tensor_tensor(out=hid[:, c, H(h)], in0=tmpf[:, i, P(h)], in1=tmpf[:, i, P(h)], op=ALU.mult),
                                 rd=[t_tmp[i]], wr=[t_hid])
                        std_unit(s, sc, 16, hrhs, hbtk, ev_up)
                    if t >= 2:
                        for _ in range(2):
                            if ffn_stats:
                                ffn_stats.pop(0)()
                for tk in t_hb_all + [t_mrg]:
                    for k, v in list(tk.w.items()) + list(tk.r.items()):
                        if t_z.r.get(k, 0) < v:
                            t_z.r[k] = v
                pend = []
                for jc in range(16):
                    sl = [next_tile(), next_tile()]
                    for h in range(2):
                        mms = []
                        for kg in range(2):
                            wv = ring[:, sl[kg], :].rearrange("p (k c) -> p k c", c=128)
                            mms += [(wv[:, k, :], hid[:, kg * 32 + k, H(h)], [t_slot[sl[kg]], t_hid]) for k in range(32)]
                        b = group(mms, h=h)
                        zevac_with_stats(zf, t_z, jc, h, b, jc == 0, jc == 15, pend)
                    flush_stats(pend, keep=2)
                    rel(2)
                flush_stats(pend)
                for tk in (t_s1, t_s2, t_aact, t_bact, t_kv) + tuple(t_qo):
                    tk.w = dict(t_hid.w); tk.r = dict(t_hid.r)
                post_norm_rstd(eps_tile=True)
                post_norm_xupdate(l, C_GMPOST, zf, t_z, halves=(0,))
                if l + 1 < n_layers:
                    kv_norm(l + 1)
                elif blk + 1 < nblk:
                    kv_norm(0)
                post_norm_xupdate(l, C_GMPOST, zf, t_z, halves=(1,))
                for tk in t_hb_all:
                    tk.w = dict(t_z.w); tk.r = dict(t_z.r)
                t_mrg.w = dict(t_z.w); t_mrg.r = dict(t_z.r)
                if blk == 0:
                    for c in range(16):
                        E.op("dve", lambda e, c=c: e.tensor_scalar(out=xs[:, c, 0:HALO], in0=xs[:, c, 0:HALO],
                                                                  scalar1=hmask[:, 0:1], scalar2=None, op0=ALU.mult),
                             rd=[t_cst], wr=txs[c])
            if blk == 0:
                ev = E.dma("sp", "out", outT[:, :, 0:NT - HALO], xs[:, :, HALO:NT], rd=t_xs_all)
            else:
                ev = E.dma("sp", "out", outT[:, :, blk * NT - HALO:(blk + 1) * NT - HALO], xs[:], rd=t_xs_all)
            final_evs.append(ev)
        E.finalize(st, final_evs)
    return nc


def _std(W, c0, ncols=256, k0=0, nk=16):
    blk = W[k0 * 128:(k0 + nk) * 128, c0:c0 + ncols]
    return blk.reshape(nk, 128, ncols).transpose(1, 0, 2).reshape(128, nk * ncols)


def pack_layer_tiles(out, w_in, w_a_out, w_b_out, w_kv, w_x_out, w_o, w_up, w_down):
    i = 0

    def put(a):
        nonlocal i
        out[i] = a
        i += 1
    for t in range(8):
        put(_std(w_kv, t * 256))
    for base in (3072, 4096, 1024, 0, 2048, 5120):
        for t in range(4):
            put(_std(w_in, base + t * 256))
    for jp in range(8):
        put(_std(w_in, 8192 + jp * 256))
        put(_std(w_in, 10240 + jp * 256))
        put(np.concatenate([_std(w_b_out, jp * 256, nk=8), _std(w_x_out, jp * 256, nk=8)], axis=1))
    for jq in range(4):
        put(_std(w_in, 6144 + (2 * jq) * 256))
        put(_std(w_in, 6144 + (2 * jq + 1) * 256))
        put(_std(w_a_out, jq * 512, ncols=512, nk=8))
    for t in range(8):
        put(_std(w_o, t * 256))
    for t in range(32):
        put(_std(w_up, t * 256))
    for jc in range(16):
        for kg in range(2):
            put(_std(w_down, jc * 128, ncols=128, k0=kg * 32, nk=32))
    assert i == TPL


def pack_consts(layers, g_mix_pre, g_mix_post, g_mlp_pre, g_mlp_post, g_mem, conv_a_w, conv_a_b, ln_a_g, ln_a_b, conv_b_w):
    cst = np.zeros((128, len(layers) * C_PER), np.float32)
    for li, l in enumerate(layers):
        o = li * C_PER
        for col, g in ((C_GPRE, g_mix_pre), (C_GPOST, g_mix_post), (C_GMPRE, g_mlp_pre), (C_GMPOST, g_mlp_post), (C_GMEM, g_mem)):
            cst[:, o + col:o + col + 16] = g[l].reshape(16, 128).T
        cst[:, o + C_CAW:o + C_CAW + 8 * CK] = conv_a_w[l].reshape(CK, 8, 128).transpose(2, 1, 0).reshape(128, 8 * CK)
        cst[:, o + C_CAB:o + C_CAB + 8] = conv_a_b[l].reshape(8, 128).T
        cst[:, o + C_LNG:o + C_LNG + 8] = ln_a_g[l].reshape(8, 128).T
        cst[:, o + C_LNB:o + C_LNB + 8] = ln_a_b[l].reshape(8, 128).T
        cst[:, o + C_CBW:o + C_CBW + 24] = conv_b_w[l].reshape(3, 8, 128).transpose(2, 1, 0).reshape(128, 24)
    return cst


def shard_x(x2d, core):
    lo = core * TOK - HALO
    if lo < 0:
        blk = np.concatenate([np.zeros((HALO, D), np.float32), x2d[0:TOK]], axis=0)
    else:
        blk = x2d[lo:lo + TOK + HALO]
    return np.ascontiguousarray(blk.T.reshape(16, 128, TOK + HALO).transpose(1, 0, 2))


_PROG_CACHE = {}


def _get_prog(n_layers):
    if n_layers not in _PROG_CACHE:
        _PROG_CACHE[n_layers] = build_program(n_layers)
    return _PROG_CACHE[n_layers]


FUSED = True


def kernel(x, mem, g_mix_pre, w_in, conv_a_w, conv_a_b, ln_a_g, ln_a_b, w_a_out, conv_b_w, w_b_out,
           g_mem, w_kv, w_x_out, w_o, g_mix_post, g_mlp_pre, w_up, w_down, g_mlp_post):
    f = lambda a: np.asarray(a, dtype=np.float32)
    x2d = f(x)[0]
    memT = np.ascontiguousarray(f(mem)[0].T.reshape(16, 128, MEM).transpose(1, 0, 2))
    groups = [list(range(DEPTH))] if FUSED else [[l] for l in range(DEPTH)]
    for layers in groups:
        nl = len(layers)
        wts = np.empty((nl * TPL, 128, TILE), np.float32)
        for li, l in enumerate(layers):
            pack_layer_tiles(wts[li * TPL:(li + 1) * TPL], f(w_in[l]), f(w_a_out[l]), f(w_b_out[l]), f(w_kv[l]),
                             f(w_x_out[l]), f(w_o[l]), f(w_up[l]), f(w_down[l]))
        cst = pack_consts(layers, f(g_mix_pre), f(g_mix_post), f(g_mlp_pre), f(g_mlp_post), f(g_mem),
                          f(conv_a_w), f(conv_a_b), f(ln_a_g), f(ln_a_b), f(conv_b_w))
        nc = _get_prog(nl)
        in_maps = []
        for c in range(NCORE):
            in_maps.append({"xT": shard_x(x2d, c), "wts": wts, "cst": cst, "memT": memT,
                            "hmask": np.full((128, 1), 0.0 if c == 0 else 1.0, np.float32)})
        res = run_bass_kernel_spmd(nc, in_maps, core_ids=list(range(NCORE)))
        outs = []
        for c in range(NCORE):
            o = res.results[c]["outT"]
            outs.append(o.transpose(2, 1, 0).reshape(TOK, D))
        x2d = np.concatenate(outs, axis=0)
    return np.ascontiguousarray(x2d[None]).astype(np.float32)
```
